# Optimizing a Trainium2 kernel written in Bass

```python
import jax, jax.numpy as jnp
from jax import lax
import numpy as np

D_MODEL = 1024
BATCH = 16
SEQ = 2048
DEPTH = 1

CHUNK = 64
FOX_HEADS = 16
HEAD_DIM = 64
FOX_WIDTH = FOX_HEADS * HEAD_DIM
CONV_GROUPS = 16
CONV_WIDTH = D_MODEL
D_MIX = FOX_WIDTH + CONV_WIDTH
CONV_KERNEL = 31
Q_BLOCK = 128
EPS = 1e-6
NEG_INF = -1e30
IN_COLS = 3 * FOX_WIDTH + FOX_HEADS + FOX_WIDTH + 2 * CONV_WIDTH + CONV_WIDTH

kernel_name = "hybrid_fox_conformer_block"


def rmsnorm(x, g):
    xf = x.astype(jnp.float32)
    y = xf * lax.rsqrt(jnp.mean(xf * xf, axis=-1, keepdims=True) + EPS)
    return (y * g.astype(jnp.float32)).astype(x.dtype)


def layernorm(x, g, b):
    xf = x.astype(jnp.float32)
    mu = jnp.mean(xf, axis=-1, keepdims=True)
    var = jnp.mean(jnp.square(xf - mu), axis=-1, keepdims=True)
    y = (xf - mu) * lax.rsqrt(var + EPS)
    return (y * g.astype(jnp.float32) + b.astype(jnp.float32)).astype(x.dtype)


def fox_attention(q, k, v, log_f):
    S = q.shape[1]
    c = jnp.cumsum(log_f, axis=1)
    c = jnp.transpose(c, (0, 2, 1))
    scale = HEAD_DIM ** -0.5
    outs = []
    for i in range(S // Q_BLOCK):
        q0, q1 = i * Q_BLOCK, (i + 1) * Q_BLOCK
        qb = q[:, q0:q1].astype(jnp.float32)
        kb = k[:, :q1].astype(jnp.float32)
        vb = v[:, :q1]
        logits = jnp.einsum('bqhd,bkhd->bhqk', qb, kb) * scale
        decay = c[:, :, q0:q1, None] - c[:, :, None, :q1]
        qpos = jnp.arange(q0, q1)[:, None]
        kpos = jnp.arange(q1)[None, :]
        logits = jnp.where(kpos <= qpos, logits + decay, NEG_INF)
        probs = jax.nn.softmax(logits, axis=-1).astype(v.dtype)
        outs.append(jnp.einsum('bhqk,bkhd->bqhd', probs, vb))
    return jnp.concatenate(outs, axis=1)


def causal_depthwise_conv(u, w, b):
    C = u.shape[-1]
    y = lax.conv_general_dilated(
        u, w.reshape(CONV_KERNEL, 1, C).astype(u.dtype),
        window_strides=(1,), padding=[(CONV_KERNEL - 1, 0)],
        dimension_numbers=('NWC', 'WIO', 'NWC'), feature_group_count=C)
    return y + b.astype(u.dtype)


def setup_inputs(seed: int = 0) -> dict:
    key = jax.random.key(seed)
    ks = jax.random.split(key, 12)
    f32 = jnp.float32
    x = jax.random.normal(ks[0], (BATCH, SEQ, D_MODEL), f32)
    norm_g = 1.0 + 0.02 * jax.random.normal(ks[1], (DEPTH, D_MODEL), f32)
    w_in = jax.random.normal(ks[2], (DEPTH, D_MODEL, IN_COLS), f32) * D_MODEL ** -0.5
    b_forget = (jnp.linspace(1.0, 5.0, FOX_HEADS, dtype=f32)[None, :]
                + 0.1 * jax.random.normal(ks[3], (DEPTH, FOX_HEADS), f32))
    q_norm_g = 1.0 + 0.02 * jax.random.normal(ks[4], (DEPTH, FOX_HEADS, HEAD_DIM), f32)
    k_norm_g = 1.0 + 0.02 * jax.random.normal(ks[5], (DEPTH, FOX_HEADS, HEAD_DIM), f32)
    conv_w = jax.random.normal(ks[6], (DEPTH, CONV_KERNEL, CONV_WIDTH), f32) * CONV_KERNEL ** -0.5
    conv_b = 0.02 * jax.random.normal(ks[7], (DEPTH, CONV_WIDTH), f32)
    conv_ln_g = 1.0 + 0.02 * jax.random.normal(ks[8], (DEPTH, CONV_WIDTH), f32)
    conv_ln_b = 0.02 * jax.random.normal(ks[9], (DEPTH, CONV_WIDTH), f32)
    w_out = jax.random.normal(ks[10], (DEPTH, D_MIX, D_MODEL), f32) * D_MIX ** -0.5
    return {"x": x, "norm_g": norm_g, "w_in": w_in, "b_forget": b_forget,
            "q_norm_g": q_norm_g, "k_norm_g": k_norm_g, "conv_w": conv_w,
            "conv_b": conv_b, "conv_ln_g": conv_ln_g, "conv_ln_b": conv_ln_b,
            "w_out": w_out}


def reference(x, norm_g, w_in, b_forget, q_norm_g, k_norm_g, conv_w, conv_b,
              conv_ln_g, conv_ln_b, w_out):
    B, S, _ = x.shape
    o_q = 0
    o_k = o_q + FOX_WIDTH
    o_v = o_k + FOX_WIDTH
    o_f = o_v + FOX_WIDTH
    o_gf = o_f + FOX_HEADS
    o_glu = o_gf + FOX_WIDTH
    o_gc = o_glu + 2 * CONV_WIDTH
    for l in range(DEPTH):
        h = rmsnorm(x, norm_g[l])
        z = jnp.einsum('bsd,de->bse', h, w_in[l])

        q = z[..., o_q:o_k].reshape(B, S, FOX_HEADS, HEAD_DIM)
        k = z[..., o_k:o_v].reshape(B, S, FOX_HEADS, HEAD_DIM)
        v = z[..., o_v:o_f].reshape(B, S, FOX_HEADS, HEAD_DIM)
        q = rmsnorm(q, q_norm_g[l])
        k = rmsnorm(k, k_norm_g[l])
        log_f = jax.nn.log_sigmoid(z[..., o_f:o_gf].astype(jnp.float32)
                                   + b_forget[l].astype(jnp.float32))
        a = fox_attention(q, k, v, log_f).reshape(B, S, FOX_WIDTH)
        a = a * jax.nn.silu(z[..., o_gf:o_glu])

        u = z[..., o_glu:o_gc]
        u = u[..., :CONV_WIDTH] * jax.nn.sigmoid(u[..., CONV_WIDTH:])
        u = causal_depthwise_conv(u, conv_w[l], conv_b[l])
        u = jax.nn.silu(layernorm(u, conv_ln_g[l], conv_ln_b[l]))
        u = u * jax.nn.silu(z[..., o_gc:])

        y = jnp.concatenate([a, u], axis=-1)
        x = x + jnp.einsum('bse,ed->bsd', y, w_out[l])
    return x
```

```python
import numpy as np
from contextlib import ExitStack
import concourse.bass as bass
import concourse.mybir as mybir
from concourse.bass_utils import run_bass_kernel_spmd

F32 = mybir.dt.float32
BF16 = mybir.dt.bfloat16
AF = mybir.ActivationFunctionType
ALU = mybir.AluOpType

D = 1024
NH = 16
HD = 64
KC = 31
IN_COLS = 7184
O_Q, O_K, O_V, O_F, O_GF, O_GLU, O_GC = 0, 1024, 2048, 3072, 3088, 4112, 6160
EPS = 1e-6
MASKV = -30000.0
N_CORES = 8
INTERLEAVE = False
PJB = (0, 1, 3, 5, 7)
NSET = 3
NPE = 23


class Sched:
    def __init__(self, nc, es):
        self.nc = nc
        self.eng = {"pe": nc.tensor, "act": nc.scalar, "dve": nc.vector, "pool": nc.gpsimd, "sp": nc.sync}
        self.sem = {k: es.enter_context(nc.semaphore("s_" + k)) for k in self.eng}
        self.cnt = {k: 0 for k in self.eng}
        self.seen = {k: {} for k in self.eng}
        self.dsem = {}
        self.dcnt = {}
        self.es = es
        self.res = {}
        self.pending_noinc = {k: False for k in self.eng}

    def _semh(self, key):
        return self.sem[key] if key in self.sem else self.dsem[key]

    def _wait(self, engine, tok):
        key, val = tok
        if self.seen[engine].get(key, 0) >= val:
            return
        self.seen[engine][key] = val
        self.eng[engine].wait_ge(self._semh(key), val)

    def _deps(self, engine, reads, writes):
        for r in reads:
            st = self.res.get(r)
            if st is not None and st["w"] is not None:
                if not (engine == "pe" and st["w"][0] == "pe"):
                    self._wait(engine, st["w"])
        for w in writes:
            st = self.res.get(w)
            if st is None:
                continue
            if st["w"] is not None and not (engine == "pe" and st["w"][0] == "pe"):
                self._wait(engine, st["w"])
            for k, v in st["r"].items():
                if k == engine and engine == "pe":
                    continue
                self._wait(engine, (k, v))

    def _record(self, tok, reads, writes):
        for r in reads:
            st = self.res.setdefault(r, {"w": None, "r": {}})
            if st["r"].get(tok[0], 0) < tok[1]:
                st["r"][tok[0]] = tok[1]
        for w in writes:
            self.res[w] = {"w": tok, "r": {}}

    def op(self, engine, fn, reads=(), writes=(), inc=True):
        self._deps(engine, reads, writes)
        ins = fn()
        if inc:
            self.cnt[engine] += 1
            ins.then_inc(self.sem[engine], 1)
            tok = (engine, self.cnt[engine])
            self.pending_noinc[engine] = False
        else:
            tok = (engine, self.cnt[engine] + 1)
            self.pending_noinc[engine] = True
        self._record(tok, reads, writes)
        return tok

    def dma(self, queue, semname, out, in_, reads=(), writes=()):
        if semname not in self.dsem:
            self.dsem[semname] = self.es.enter_context(self.nc.semaphore("d_" + semname))
            self.dcnt[semname] = 0
        self._deps(queue, reads, writes)
        ins = self.eng[queue].dma_start(out=out, in_=in_)
        self.dcnt[semname] += 16
        ins.then_inc(self.dsem[semname], 16)
        tok = (semname, self.dcnt[semname])
        self._record(tok, reads, writes)
        return tok

    def retoken(self, names, tok):
        for n in names:
            self.res[n] = {"w": tok, "r": {}}

    def barrier(self):
        assert not self.pending_noinc["pe"]
        for e in self.eng:
            self.wait_all(e)

    def wait_all(self, engine):
        for k in self.eng:
            if k != engine and self.cnt[k] > 0:
                self._wait(engine, (k, self.cnt[k]))
        for k, v in self.dcnt.items():
            if v > 0:
                self._wait(engine, (k, v))


class _Stop(Exception):
    pass


def build_nc(NB=2, S=2048, STOP=9):
    NT = S // 128
    NTT = S // 512
    nc = bass.Bass("TRN2", target_bir_lowering=False)
    x = nc.dram_tensor("x", [NB, S, D], F32, kind="ExternalInput").ap()
    w_in = nc.dram_tensor("w_in", [D, IN_COLS], F32, kind="ExternalInput").ap()
    w_out = nc.dram_tensor("w_out", [2 * D, D], F32, kind="ExternalInput").ap()
    norm_g = nc.dram_tensor("norm_g", [1, D], F32, kind="ExternalInput").ap()
    b_forget = nc.dram_tensor("b_forget", [1, NH], F32, kind="ExternalInput").ap()
    gq_d = nc.dram_tensor("gq", [128, 8], F32, kind="ExternalInput").ap()
    gk_d = nc.dram_tensor("gk", [128, 8], F32, kind="ExternalInput").ap()
    convw_d = nc.dram_tensor("convw", [128, 8, KC], F32, kind="ExternalInput").ap()
    convb_d = nc.dram_tensor("convb", [128, 8], F32, kind="ExternalInput").ap()
    lng_d = nc.dram_tensor("lng", [128, 8], F32, kind="ExternalInput").ap()
    lnb_d = nc.dram_tensor("lnb", [128, 8], F32, kind="ExternalInput").ap()
    out = nc.dram_tensor("out", [NB, S, D], F32, kind="ExternalOutput").ap()

    w_in_v = w_in.rearrange("(c p) n -> p c n", p=128)
    w_out_v = w_out.rearrange("(c p) n -> p c n", p=128)

    with ExitStack() as es:
        sc = Sched(nc, es)

        def sb(name, shape, dt):
            return es.enter_context(nc.sbuf_tensor(name, shape, dt))

        ident = sb("ident", [128, 128], BF16)
        maskT = sb("maskT", [128, 128], BF16)
        blockones = sb("blockones", [128, 128], BF16)
        onesmean = sb("onesmean", [128, 128], BF16)
        Uneg = sb("Uneg", [128, 128], F32)
        OnesNeg = sb("OnesNeg", [128, 128], F32)
        Sel = sb("Sel", [128, 128], F32)
        bf_tile = sb("bf_tile", [128, NH], F32)
        gq = sb("gq_sb", [128, 8], F32)
        gk = sb("gk_sb", [128, 8], F32)
        gq8 = sb("gq8", [128, 8], F32)
        convw = sb("convw_sb", [128, 8, KC], F32)
        convwh = sb("convwh", [128, 8, KC], F32)
        convb = sb("convb_sb", [128, 8], F32)
        lng = sb("lng_sb", [128, 8], F32)
        lnb = sb("lnb_sb", [128, 8], F32)
        lng_h = sb("lng_h", [128, 8], F32)
        lnb_h = sb("lnb_h", [128, 8], F32)
        lng_q = sb("lng_q", [128, 8], F32)
        lnb_q = sb("lnb_q", [128, 8], F32)
        wf = sb("wf", [128, 8, NH], BF16)
        hT = sb("hT", [128, 8, S], BF16)
        yT = sb("yT", [128, 16, S], BF16)
        wbuf = [sb("wbuf%d" % i, [128, 8, 512], BF16) for i in range(2)]

        PS = [es.enter_context(nc.psum_tensor("ps%d" % i, [128, 1024], F32)) for i in range(4)]

        def bank(b):
            return PS[b // 2][:, (b % 2) * 512:(b % 2) * 512 + 512]

        def bank_bf(b):
            t = PS[b // 2].bitcast(BF16)
            return t[:, (b % 2) * 1024:(b % 2) * 1024 + 1024]

        BN = ["B%d" % i for i in range(8)]

        g = nc.gpsimd
        sc.op("pool", lambda: g.memset(ident[:], 1.0), writes=["ident"])
        sc.op("pool", lambda: g.affine_select(out=ident[:], in_=ident[:], compare_op=ALU.is_equal, fill=0.0,
                                              base=0, pattern=[[1, 128]], channel_multiplier=-1),
              reads=["ident"], writes=["ident"])
        sc.op("pool", lambda: g.memset(maskT[:], 0.0), writes=["maskT"])
        sc.op("pool", lambda: g.affine_select(out=maskT[:], in_=maskT[:], compare_op=ALU.is_ge, fill=MASKV,
                                              base=0, pattern=[[1, 128]], channel_multiplier=-1),
              reads=["maskT"], writes=["maskT"])
        sc.op("pool", lambda: g.memset(blockones[:], 0.0), writes=["blockones"])
        sc.op("pool", lambda: g.memset(blockones[0:64, 0:64], 1.0 / 64), reads=["blockones"], writes=["blockones"])
        sc.op("pool", lambda: g.memset(blockones[64:128, 64:128], 1.0 / 64), reads=["blockones"], writes=["blockones"])
        sc.op("pool", lambda: g.memset(onesmean[:], 1.0 / 1024), writes=["onesmean"])
        sc.op("pool", lambda: g.memset(Uneg[:], -1.0), writes=["Uneg"])
        sc.op("pool", lambda: g.affine_select(out=Uneg[:], in_=Uneg[:], compare_op=ALU.is_ge, fill=0.0,
                                              base=0, pattern=[[1, 128]], channel_multiplier=-1),
              reads=["Uneg"], writes=["Uneg"])
        sc.op("pool", lambda: g.memset(OnesNeg[:], -1.0), writes=["OnesNeg"])
        sc.op("pool", lambda: g.memset(Sel[:], 1.0), writes=["Sel"])
        sc.op("pool", lambda: g.affine_select(out=Sel[:], in_=Sel[:], compare_op=ALU.is_equal, fill=0.0,
                                              base=-127, pattern=[[0, 128]], channel_multiplier=1),
              reads=["Sel"], writes=["Sel"])

        sc.dma("sp", "par", bf_tile[:], b_forget.partition_broadcast(128).rearrange("p o n -> p (o n)"), writes=["bf_tile"])
        sc.dma("sp", "par", gq[:], gq_d, writes=["gq"])
        sc.dma("sp", "par", gk[:], gk_d, writes=["gk"])
        sc.dma("sp", "par", convw[:], convw_d, writes=["convw"])
        sc.dma("sp", "par", convb[:], convb_d, writes=["convb"])
        sc.dma("sp", "par", lng[:], lng_d, writes=["lng"])
        sc.dma("sp", "par", lnb[:], lnb_d, writes=["lnb"])
        sc.retoken(["bf_tile", "gq", "gk", "convw", "convb", "lng", "lnb"], ("par", sc.dcnt["par"]))
        sc.dma("pool", "wfl", wf[:], w_in_v[:, :, O_F:O_F + NH], writes=["wf"])
        v = nc.vector
        sc.op("dve", lambda: v.tensor_scalar(out=gq8[:], in0=gq[:], scalar1=0.125, scalar2=None, op0=ALU.mult),
              reads=["gq"], writes=["gq8"])
        sc.op("dve", lambda: v.tensor_scalar(out=convwh[:], in0=convw[:], scalar1=0.5, scalar2=None, op0=ALU.mult),
              reads=["convw"], writes=["convwh"])
        sc.op("dve", lambda: v.tensor_scalar(out=lng_h[:], in0=lng[:], scalar1=0.5, scalar2=None, op0=ALU.mult),
              reads=["lng"], writes=["lng_h"])
        sc.op("dve", lambda: v.tensor_scalar(out=lnb_h[:], in0=lnb[:], scalar1=0.5, scalar2=None, op0=ALU.mult),
              reads=["lnb"], writes=["lnb_h"])
        sc.op("dve", lambda: v.tensor_scalar(out=lng_q[:], in0=lng[:], scalar1=0.25, scalar2=None, op0=ALU.mult),
              reads=["lng"], writes=["lng_q"])
        sc.op("dve", lambda: v.tensor_scalar(out=lnb_q[:], in0=lnb[:], scalar1=0.25, scalar2=None, op0=ALU.mult),
              reads=["lnb"], writes=["lnb_q"])

        wstate = {"n": 0}

        def load_wgroup(cols, width=128):
            slot = wstate["n"] % 2
            wstate["n"] += 1
            names = []
            tok = None
            for j, c0 in enumerate(cols):
                nm = ["wbuf%d_%d" % (slot, jj) for jj in range(j * width // 128, (j + 1) * width // 128)]
                tok = sc.dma("pool", "w%d" % slot, wbuf[slot][:, :, j * width:(j + 1) * width],
                             w_in_v[:, :, c0:c0 + width], writes=nm)
                names += nm
            sc.retoken(names, tok)
            return slot

        def proj_fm(slot, blk, tt, b_out):
            for c in range(8):
                sc.op("pe", lambda c=c: nc.tensor.matmul(bank(b_out), lhsT=wbuf[slot][:, c, blk * 128:(blk + 1) * 128],
                                                         rhs=hT[:, c, tt * 512:(tt + 1) * 512],
                                                         start=(c == 0), stop=(c == 7)),
                      reads=["wbuf%d_%d" % (slot, blk)] + ["hT%d" % i for i in range(4 * tt, 4 * tt + 4)],
                      writes=[BN[b_out]], inc=(c == 7))

        stopped = False
        for b in range(NB):
          try:
                with ExitStack() as ph:
                    if STOP < 1:
                        raise _Stop()
                    sc.barrier()
                    NXS = 4
                    g_tile = ph.enter_context(nc.sbuf_tensor("g_tile%d" % b, [128, D], F32))
                    sc.dma("sp", "gt", g_tile[:], norm_g.partition_broadcast(128).rearrange("p o n -> p (o n)"), writes=["g_tile"])
                    xt = [ph.enter_context(nc.sbuf_tensor("xt%d_%d" % (b, i), [128, D], F32)) for i in range(NXS)]
                    hb = [ph.enter_context(nc.sbuf_tensor("hb%d_%d" % (b, i), [128, D], BF16)) for i in range(3)]
                    junk = ph.enter_context(nc.sbuf_tensor("junk%d" % b, [128, D], BF16))
                    ssq = ph.enter_context(nc.sbuf_tensor("ssq%d" % b, [128, NT], F32))
                    msx = ph.enter_context(nc.sbuf_tensor("msx%d" % b, [128, NT], F32))
                    rsx = ph.enter_context(nc.sbuf_tensor("rsx%d" % b, [128, NT], F32))

                    def p1_stages(i):
                        s_ = i % NXS
                        hs = i % 3
                        pb = i % 2
                        pbf = bank_bf(pb)

                        def s0():
                            sc.dma("sp", "xl%d" % s_, xt[s_][:], x[b, i * 128:(i + 1) * 128, :], writes=["xt%d" % s_])

                        def s1():
                            sc.op("act", lambda: nc.scalar.activation(out=junk[:], in_=xt[s_][:], func=AF.Square,
                                                                      accum_out=ssq[:, i:i + 1]),
                                  reads=["xt%d" % s_], writes=["junk", "ssq%d" % i])
                            sc.op("act", lambda: nc.scalar.activation(out=msx[:, i:i + 1], in_=ssq[:, i:i + 1],
                                                                      func=AF.Ln, bias=EPS, scale=1.0 / D),
                                  reads=["ssq%d" % i], writes=["msx%d" % i])
                            sc.op("act", lambda: nc.scalar.activation(out=rsx[:, i:i + 1], in_=msx[:, i:i + 1],
                                                                      func=AF.Exp, scale=-0.5),
                                  reads=["msx%d" % i], writes=["rsx%d" % i])

                        def s2():
                            sc.op("dve", lambda: nc.vector.scalar_tensor_tensor(out=hb[hs][:], in0=xt[s_][:],
                                                                                scalar=rsx[:, i:i + 1], in1=g_tile[:],
                                                                                op0=ALU.mult, op1=ALU.mult),
                                  reads=["xt%d" % s_, "rsx%d" % i, "g_tile"], writes=["hb%d" % hs])

                        def s3():
                            for c in range(8):
                                sc.op("pe", lambda: nc.tensor.transpose(pbf[:, c * 128:(c + 1) * 128],
                                                                        hb[hs][:, c * 128:(c + 1) * 128], ident[:]),
                                      reads=["hb%d" % hs, "ident"], writes=[BN[pb]], inc=(c == 7))

                        def s4():
                            sc.op("dve", lambda: nc.vector.tensor_copy(out=hT[:, :, i * 128:(i + 1) * 128],
                                                                       in_=pbf.rearrange("p (c t) -> p c t", c=8)),
                                  reads=[BN[pb]], writes=["hT%d" % i])
                        return [s0, s1, s2, s3, s4]

                    p1u = [p1_stages(i) for i in range(NT)]
                    for slot_i in range(NT + 4):
                        for k in range(4, -1, -1):
                            ui_ = slot_i - k
                            if 0 <= ui_ < NT:
                                p1u[ui_][k]()

                pre = {}
                pre["pairs"] = [load_wgroup([O_Q + 128 * p_, O_K + 128 * p_, O_V + 128 * p_, O_GF + 128 * p_]) for p_ in range(2)]
                ph23 = ExitStack()
                bias_all = ph23.enter_context(nc.sbuf_tensor("bias_all%d" % b, [128, NH, NTT, NT], F32))
                dT = ph23.enter_context(nc.sbuf_tensor("dT%d" % b, [NH, S], BF16))
                with ExitStack() as ph:
                    if STOP < 2:
                        raise _Stop()
                    sc.barrier()
                    xf = ph.enter_context(nc.sbuf_tensor("xf%d" % b, [128, NT, NH], F32))
                    ef = ph.enter_context(nc.sbuf_tensor("ef%d" % b, [128, NT, NH], F32))
                    lfn = ph.enter_context(nc.sbuf_tensor("lfn%d" % b, [128, NT, NH], F32))
                    padA = ph.enter_context(nc.sbuf_tensor("padA%d" % b, [128, 8 + NT, NH], F32))
                    padB = ph.enter_context(nc.sbuf_tensor("padB%d" % b, [128, 8 + NT, NH], F32))
                    c_tok = ph.enter_context(nc.sbuf_tensor("c_tok%d" % b, [128, NT, NH], F32))
                    rB = ph.enter_context(nc.sbuf_tensor("rB%d" % b, [128, NT, NH], F32))
                    d_tok = ph.enter_context(nc.sbuf_tensor("d_tok%d" % b, [128, NT, NH], BF16))
                    NF = NT * NH
                    zf = bank(0)[:, 0:NF]
                    for i in range(NT):
                        for c in range(8):
                            sc.op("pe", lambda c=c: nc.tensor.matmul(bank(0)[:, i * NH:(i + 1) * NH],
                                                                     lhsT=hT[:, c, i * 128:(i + 1) * 128], rhs=wf[:, c, :],
                                                                     start=(c == 0), stop=(c == 7)),
                                  reads=["hT%d" % i, "wf"], writes=[BN[0]], inc=(c == 7 and i == NT - 1))
                    sc.op("dve", lambda: nc.vector.tensor_tensor(
                        out=xf[:], in0=zf.rearrange("p (i h) -> p i h", h=NH),
                        in1=bf_tile[:].rearrange("p (o h) -> p o h", o=1).broadcast_to([128, NT, NH]), op=ALU.add),
                        reads=[BN[0], "bf_tile"], writes=["xf"])
                    sc.op("act", lambda: nc.scalar.activation(out=ef[:], in_=xf[:], func=AF.Exp, scale=-1.0),
                          reads=["xf"], writes=["ef"])
                    sc.op("act", lambda: nc.scalar.activation(out=lfn[:], in_=ef[:], func=AF.Ln, bias=1.0, scale=1.0),
                          reads=["ef"], writes=["lfn"])
                    lfn2 = lfn[:].rearrange("p i h -> p (i h)")
                    sc.op("pe", lambda: nc.tensor.matmul(bank(1)[:, 0:NF], lhsT=Uneg[:], rhs=lfn2, start=True, stop=True),
                          reads=["lfn", "Uneg"], writes=[BN[1]], inc=False)
                    sc.op("pe", lambda: nc.tensor.matmul(bank(1)[:, 256:256 + NF], lhsT=OnesNeg[:], rhs=lfn2, start=True,
                                                         stop=True),
                          reads=["lfn", "OnesNeg"], writes=[BN[1]])
                    sc.op("dve", lambda: nc.vector.memset(padA[:], 0.0), writes=["padA"])
                    sc.op("dve", lambda: nc.vector.memset(padB[:], 0.0), writes=["padB"])
                    if NT > 1:
                        sc.op("dve", lambda: nc.vector.tensor_copy(
                            out=padA[:, 9:8 + NT, :],
                            in_=bank(1)[:, 256:256 + NF].rearrange("p (i h) -> p i h", h=NH)[:, 0:NT - 1, :]),
                            reads=[BN[1], "padA"], writes=["padA"])
                    cur, oth, curn, othn = padA, padB, "padA", "padB"
                    dstep = 1
                    while dstep < NT:
                        sc.op("dve", lambda cur=cur, oth=oth, dstep=dstep: nc.vector.tensor_tensor(
                            out=oth[:, 8:8 + NT, :], in0=cur[:, 8:8 + NT, :], in1=cur[:, 8 - dstep:8 + NT - dstep, :],
                            op=ALU.add), reads=[curn], writes=[othn])
                        cur, oth, curn, othn = oth, cur, othn, curn
                        dstep *= 2
                    sc.op("dve", lambda: nc.vector.tensor_tensor(
                        out=c_tok[:], in0=bank(1)[:, 0:NF].rearrange("p (i h) -> p i h", h=NH), in1=cur[:, 8:8 + NT, :],
                        op=ALU.add), reads=[BN[1], curn], writes=["c_tok"])
                    sc.op("pe", lambda: nc.tensor.matmul(bank(0)[:, 0:NF], lhsT=Sel[:],
                                                         rhs=c_tok[:].rearrange("p i h -> p (i h)"), start=True, stop=True),
                          reads=["c_tok", "Sel"], writes=[BN[0]])
                    sc.op("act", lambda: nc.scalar.copy(out=rB[:].rearrange("p i h -> p (i h)"), in_=bank(0)[:, 0:NF]),
                          reads=[BN[0]], writes=["rB"])
                    c4 = c_tok[:].rearrange("p (a j) h -> p a j h", j=4)
                    r4 = rB[:].rearrange("p (a j) h -> p a j h", j=4)[:, :, 3:4, :].broadcast_to([128, NTT, 4, NH])
                    sc.op("dve", lambda: nc.vector.tensor_tensor(
                        out=d_tok[:].rearrange("p (a j) h -> p a j h", j=4), in0=c4, in1=r4, op=ALU.subtract),
                        reads=["c_tok", "rB"], writes=["d_tok"])
                    for T in range(NTT):
                        sc.op("dve", lambda T=T: nc.vector.tensor_tensor(
                            out=bias_all[:, :, T, :],
                            in0=rB[:, 4 * T + 3:4 * T + 4, :].rearrange("p o h -> p h o").broadcast_to([128, NH, NT]),
                            in1=c_tok[:].rearrange("p i h -> p h i"), op=ALU.subtract),
                            reads=["c_tok", "rB"], writes=["bias_all"])
                    dps = PS[1].bitcast(BF16)
                    for i in range(NT):
                        sc.op("pe", lambda i=i: nc.tensor.transpose(dps[0:NH, i * 128:(i + 1) * 128], d_tok[:, i, :],
                                                                    ident[:]),
                              reads=["d_tok", "ident"], writes=[BN[2], BN[3]], inc=(i == NT - 1))
                    sc.op("act", lambda: nc.scalar.copy(out=dT[:], in_=dps[0:NH, 0:S]),
                          reads=[BN[2], BN[3]], writes=["dT"])

                with ExitStack() as ph:
                    if STOP < 3:
                        raise _Stop()
                    sc.barrier()
                    QA = [[ph.enter_context(nc.sbuf_tensor("QA%d_%d_%d" % (b, pp, i), [128, S], BF16)) for i in range(2)] for pp in range(2)]
                    KA = [[ph.enter_context(nc.sbuf_tensor("KA%d_%d_%d" % (b, pp, i), [128, S], BF16)) for i in range(2)] for pp in range(2)]
                    Vaug = [ph.enter_context(nc.sbuf_tensor("Vaug%d_%d" % (b, pp), [128, NT, 2, 128], BF16)) for pp in range(2)]
                    gate = [ph.enter_context(nc.sbuf_tensor("gate%d_%d" % (b, pp), [128, S], BF16)) for pp in range(2)]
                    sq_sb = [ph.enter_context(nc.sbuf_tensor("sq%d_%d" % (b, i), [128, 512], BF16)) for i in range(2)]
                    msb = [ph.enter_context(nc.sbuf_tensor("msb%d_%d" % (b, i), [128, 512], F32)) for i in range(2)]
                    rstd = [ph.enter_context(nc.sbuf_tensor("rstd%d_%d" % (b, i), [128, 512], F32)) for i in range(2)]
                    th_sb = [ph.enter_context(nc.sbuf_tensor("th%d_%d" % (b, i), [128, 512], F32)) for i in range(2)]
                    pbuf = [ph.enter_context(nc.sbuf_tensor("pbuf%d_%d" % (b, i), [128, 512], BF16)) for i in range(4)]
                    rec = [ph.enter_context(nc.sbuf_tensor("rec%d_%d" % (b, i), [128, 512], F32)) for i in range(2)]
                    for pp in range(2):
                        for hh in range(2):
                            t_ = "%d%d" % (pp, hh)
                            sc.op("dve", lambda: nc.vector.memset(KA[pp][hh][64:128, :], 0.0), writes=["KArow" + t_])
                            sc.op("dve", lambda: nc.vector.memset(KA[pp][hh][64:65, :], 1.0), reads=["KArow" + t_],
                                  writes=["KArow" + t_])
                            sc.op("dve", lambda: nc.vector.memset(QA[pp][hh][64:128, :], 0.0), writes=["QArow" + t_])
                        sc.op("dve", lambda: nc.vector.memset(Vaug[pp][:, :, 0, 64:128], 1.0), writes=["Vones%d0" % pp])
                        sc.op("dve", lambda: nc.vector.memset(Vaug[pp][:, :, 1, 0:64], 1.0), writes=["Vones%d1" % pp])

                    nq = {"n": 0}
                    wslots = {}

                    def load_pair(p):
                        wslots[p] = load_wgroup([O_Q + 128 * p, O_K + 128 * p, O_V + 128 * p, O_GF + 128 * p])

                    def proj_units(p):
                        pp = p % 2
                        units = []

                        def v_unit(i0):
                            st = {}

                            def A():
                                cur_slot = wslots[p]
                                vb = PJB[nq["n"] % 5]
                                nq["n"] += 1
                                st["vb"] = vb
                                for ii in range(4):
                                    i = i0 + ii
                                    for c in range(8):
                                        sc.op("pe", lambda: nc.tensor.matmul(
                                            bank(vb)[:, ii * 128:(ii + 1) * 128], lhsT=hT[:, c, i * 128:(i + 1) * 128],
                                            rhs=wbuf[cur_slot][:, c, 256:384], start=(c == 0), stop=(c == 7)),
                                            reads=["hT%d" % i, "wbuf%d_2" % cur_slot], writes=[BN[vb]],
                                            inc=(c == 7 and ii == 3))

                            def B():
                                vb = st["vb"]
                                bv = bank(vb).rearrange("p (i c) -> p i c", i=4)
                                sc.op("act", lambda: nc.scalar.copy(out=Vaug[pp][:, i0:i0 + 4, 0, 0:64], in_=bv[:, :, 0:64]),
                                      reads=[BN[vb], "Vones%d0" % pp], writes=["V%d0_%d" % (pp, i0 // 4)])
                                sc.op("act", lambda: nc.scalar.copy(out=Vaug[pp][:, i0:i0 + 4, 1, 64:128], in_=bv[:, :, 64:128]),
                                      reads=[BN[vb], "Vones%d1" % pp], writes=["V%d1_%d" % (pp, i0 // 4)])
                            return [A, B]

                        def qk_unit(tt, qk):
                            st = {}

                            def A():
                                cur_slot = wslots[p]
                                n = nq["n"]
                                nq["n"] += 1
                                st["u"] = u = n % 2
                                st["bk"] = bk = PJB[n % 5]
                                proj_fm(cur_slot, qk, tt, bk)

                            def A2():
                                u = st["u"]
                                bk = st["bk"]
                                sc.op("act", lambda: nc.scalar.activation(out=sq_sb[u][:], in_=bank(bk), func=AF.Square),
                                      reads=[BN[bk]], writes=["sq%d" % u])

                            def B1():
                                u = st["u"]
                                ab = 6
                                sc.op("pe", lambda: nc.tensor.matmul(bank(ab), lhsT=blockones[:], rhs=sq_sb[u][:], start=True, stop=True),
                                      reads=["sq%d" % u, "blockones"], writes=[BN[ab]])
                                sc.op("act", lambda: nc.scalar.activation(out=msb[u][:], in_=bank(ab), func=AF.Ln, bias=EPS, scale=1.0),
                                      reads=[BN[ab]], writes=["msb%d" % u])

                            def B2():
                                u = st["u"]
                                sc.op("act", lambda: nc.scalar.activation(out=rstd[u][:], in_=msb[u][:], func=AF.Exp, scale=-0.5),
                                      reads=["msb%d" % u], writes=["rstd%d" % u])

                            def C():
                                u = st["u"]
                                bk = st["bk"]
                                dst = QA[pp] if qk == 0 else KA[pp]
                                gsc = gq8 if qk == 0 else gk
                                nm = "QA" if qk == 0 else "KA"
                                for hh in range(2):
                                    r0 = hh * 64
                                    sc.op("dve", lambda: nc.vector.scalar_tensor_tensor(
                                        out=dst[hh][0:64, tt * 512:(tt + 1) * 512], in0=bank(bk)[r0:r0 + 64, :],
                                        scalar=gsc[r0:r0 + 64, p:p + 1], in1=rstd[u][r0:r0 + 64, :], op0=ALU.mult, op1=ALU.mult),
                                        reads=[BN[bk], "rstd%d" % u, "gq8", "gk"], writes=["%s%d%d_%d" % (nm, pp, hh, tt)])
                            return [A, A2, B1, B2, C]

                        def gate_unit(tt):
                            st = {}

                            def A():
                                cur_slot = wslots[p]
                                n = nq["n"]
                                nq["n"] += 1
                                st["u"] = u = n % 2
                                st["bk"] = bk = PJB[n % 5]
                                proj_fm(cur_slot, 3, tt, bk)

                            def A2():
                                u = st["u"]
                                bk = st["bk"]
                                sc.op("act", lambda: nc.scalar.activation(out=th_sb[u][:], in_=bank(bk), func=AF.Tanh, scale=0.5),
                                      reads=[BN[bk]], writes=["th%d" % u])

                            def B():
                                u = st["u"]
                                bk = st["bk"]
                                sc.op("dve", lambda: nc.vector.scalar_tensor_tensor(
                                    out=gate[pp][:, tt * 512:(tt + 1) * 512], in0=th_sb[u][:], scalar=1.0, in1=bank(bk),
                                    op0=ALU.add, op1=ALU.mult),
                                    reads=[BN[bk], "th%d" % u], writes=["gate%d_%d" % (pp, tt)])
                            return [A, A2, B]

                        def drow_unit():
                            def A():
                                for hh in range(2):
                                    h = 2 * p + hh
                                    sc.dma("sp", "drow%d%d" % (pp, hh), QA[pp][hh][64:65, :], dT[h:h + 1, :], reads=["dT"],
                                           writes=["QArow%d%d" % (pp, hh)])
                            return [A]

                        ulist = [drow_unit()]
                        for i0 in range(0, NT, 4):
                            ulist.append(v_unit(i0))
                        for tt in range(NTT):
                            for qk in range(2):
                                ulist.append(qk_unit(tt, qk))
                        for tt in range(NTT):
                            ulist.append(gate_unit(tt))
                        flat = []
                        nu = len(ulist)
                        mx = max(len(u_) for u_ in ulist)
                        for slot_i in range(nu + mx - 1):
                            for k in range(mx - 1, -1, -1):
                                ui_ = slot_i - k
                                if 0 <= ui_ < nu and k < len(ulist[ui_]):
                                    flat.append(ulist[ui_][k])
                        return flat

                    def attention(p, next_units):
                        pp = p % 2
                        steps = []
                        for hh in range(2):
                            for T in range(NTT):
                                for J in range(4 * T + 4):
                                    steps.append((hh, T, J))

                        def emit_S(si):
                            hh, T, J = steps[si]
                            j = J - 4 * T
                            col0 = 128 * j if j >= 0 else 0
                            sbk = (2, 6, 3)[si % 3]
                            sc.op("pe", lambda: nc.tensor.matmul(
                                bank(sbk)[:, col0:512], lhsT=KA[pp][hh][:, J * 128:(J + 1) * 128],
                                rhs=QA[pp][hh][:, T * 512 + col0:(T + 1) * 512], start=True, stop=(j < 0)),
                                reads=["KA%d%d_%d" % (pp, hh, J // 4), "KArow%d%d" % (pp, hh), "QA%d%d_%d" % (pp, hh, T),
                                       "QArow%d%d" % (pp, hh)],
                                writes=[BN[sbk]], inc=(j < 0))
                            if j >= 0:
                                sc.op("pe", lambda: nc.tensor.matmul(
                                    bank(sbk)[:, col0:col0 + 128], lhsT=ident[:], rhs=maskT[:], start=False, stop=True),
                                    reads=["ident", "maskT"], writes=[BN[sbk]])

                        def emit_exp(si):
                            hh, T, J = steps[si]
                            h = 2 * p + hh
                            j = J - 4 * T
                            col0 = 128 * j if j >= 0 else 0
                            sbk = (2, 6, 3)[si % 3]
                            pi = si % 4
                            sc.op("act", lambda: nc.scalar.activation(
                                out=pbuf[pi][:, col0:512], in_=bank(sbk)[:, col0:512], func=AF.Exp,
                                bias=bias_all[:, h, T, J:J + 1], scale=1.0),
                                reads=[BN[sbk], "bias_all"], writes=["pbuf%d" % pi])

                        def emit_PV(si):
                            hh, T, J = steps[si]
                            j = J - 4 * T
                            col0 = 128 * j if j >= 0 else 0
                            pi = si % 4
                            acc_n = (hh * NTT + T) % 2
                            ab = 4 + acc_n
                            last = (J == 4 * T + 3)
                            sc.op("pe", lambda: nc.tensor.matmul(
                                bank(ab)[:, col0:512], lhsT=Vaug[pp][:, J, hh, :], rhs=pbuf[pi][:, col0:512],
                                start=(J == 0), stop=last),
                                reads=["pbuf%d" % pi, "V%d%d_%d" % (pp, hh, J // 4), "Vones%d%d" % (pp, hh)], writes=[BN[ab]],
                                inc=last)
                            if last:
                                num0, den0 = (0, 64) if hh == 0 else (64, 0)
                                ri = acc_n
                                sc.op("dve", lambda: nc.vector.reciprocal(out=rec[ri][num0:num0 + 64, :],
                                                                          in_=bank(ab)[den0:den0 + 64, :]),
                                      reads=[BN[ab]], writes=["rec%d" % ri])
                                sc.op("pool", lambda: nc.gpsimd.tensor_tensor(
                                    out=rec[ri][num0:num0 + 64, :], in0=rec[ri][num0:num0 + 64, :],
                                    in1=gate[pp][num0:num0 + 64, T * 512:(T + 1) * 512], op=ALU.mult),
                                    reads=["rec%d" % ri, "gate%d_%d" % (pp, T)], writes=["rec%d" % ri])
                                sc.op("dve", lambda: nc.vector.scalar_tensor_tensor(
                                    out=yT[num0:num0 + 64, p, T * 512:(T + 1) * 512], in0=bank(ab)[num0:num0 + 64, :],
                                    scalar=0.5, in1=rec[ri][num0:num0 + 64, :], op0=ALU.mult, op1=ALU.mult),
                                    reads=[BN[ab], "rec%d" % ri], writes=["yT%d_%d_%d" % (p, hh, T)])

                        nst = len(steps)
                        nun = len(next_units)
                        stride = max(1, nst // (nun + 1)) if nun else nst
                        ui = 0
                        emit_S(0)
                        if nst > 1:
                            emit_S(1)
                        for si in range(nst):
                            emit_exp(si)
                            if si + 2 < nst:
                                emit_S(si + 2)
                            emit_PV(si)
                            if INTERLEAVE and ui < nun and (si + 1) % stride == 0:
                                next_units[ui]()
                                ui += 1
                        while ui < nun:
                            next_units[ui]()
                            ui += 1

                    wslots[0], wslots[1] = pre["pairs"]
                    for un in proj_units(0):
                        un()
                    for p in range(8):
                        if p + 2 < 8:
                            load_pair(p + 2)
                        if p == 7:
                            pre["glu"] = [load_wgroup([O_GLU + 128 * c_, O_GLU + 1024 + 128 * c_]) for c_ in range(2)]
                        attention(p, proj_units(p + 1) if p + 1 < 8 else [])

                ph23.close()
                with ExitStack() as ph:
                    if STOP < 4:
                        raise _Stop()
                    sc.barrier()
                    wout = ph.enter_context(nc.sbuf_tensor("wout%d" % b, [128, 16, D], BF16))
                    for half in range(2):
                        sc.dma("pool", "wo%d" % half, wout[:, half * 8:(half + 1) * 8, :], w_out_v[:, half * 8:(half + 1) * 8, :],
                               writes=["wout%d" % half])
                    with ExitStack() as ph4:
                        with ExitStack() as ph4a:
                            uT = [ph4a.enter_context(nc.sbuf_tensor("uT%d_%d" % (b, i), [128, S + 32], BF16)) for i in range(2)]
                            Wd = [ph4a.enter_context(nc.sbuf_tensor("Wd%d_%d" % (b, i), [128, KC, 128], BF16)) for i in range(2)]
                            tb = [ph4a.enter_context(nc.sbuf_tensor("tb%d_%d" % (b, i), [128, 512], F32)) for i in range(2)]
                            cacc = [ph4a.enter_context(nc.sbuf_tensor("cacc%d_%d" % (b, i), [128, 512], F32)) for i in range(2)]
                            for i in range(2):
                                sc.op("pool", lambda i=i: nc.gpsimd.memset(uT[i][:, 0:32], 0.0), writes=["uTpad%d" % i])
                            def build_wd(ch):
                                us = ch % 2
                                sc.op("pool", lambda: nc.gpsimd.tensor_tensor(
                                    out=Wd[us][:],
                                    in0=ident[:].rearrange("p (o n) -> p o n", o=1).broadcast_to([128, KC, 128]),
                                    in1=convwh[:, ch, :].rearrange("p (k o) -> p k o", o=1).broadcast_to([128, KC, 128]),
                                    op=ALU.mult), reads=["ident", "convwh"], writes=["Wd%d" % us])

                            def glu_pe(ch, tts, slot_):
                                for tt in tts:
                                    ba = tt % 2
                                    bb = 6 + (tt % 2)
                                    u = tt % 2
                                    proj_fm(slot_, 0, tt, ba)
                                    proj_fm(slot_, 1, tt, bb)
                                    sc.op("act", lambda: nc.scalar.activation(out=tb[u][:], in_=bank(bb), func=AF.Tanh, scale=0.5),
                                          reads=[BN[bb]], writes=["tb%d" % u])

                            def glu_dve(ch, tts):
                                us = ch % 2
                                for tt in tts:
                                    ba = tt % 2
                                    u = tt % 2
                                    sc.op("dve", lambda: nc.vector.scalar_tensor_tensor(
                                        out=uT[us][:, 30 + tt * 512:30 + (tt + 1) * 512], in0=tb[u][:], scalar=1.0,
                                        in1=bank(ba), op0=ALU.add, op1=ALU.mult),
                                        reads=[BN[ba], "tb%d" % u, "uTpad%d" % us], writes=["uT%d_%d" % (us, tt)])

                            def glu_proj(ch, tts, slot_):
                                for tt in tts:
                                    glu_pe(ch, [tt], slot_)
                                    glu_dve(ch, [tt])

                            def taps(ch, tts, rds, k0, k1):
                                us = ch % 2
                                for k in range(k0, k1):
                                    for tt in tts:
                                        ca = cacc[tt % 2]
                                        can = "cacc%d" % (tt % 2)
                                        if k == NPE:
                                            sc.op("dve", lambda: nc.vector.tensor_scalar(
                                                out=ca[:], in0=uT[us][:, tt * 512 + k:tt * 512 + k + 512],
                                                scalar1=convwh[:, ch, k:k + 1], scalar2=None, op0=ALU.mult),
                                                reads=rds[tt] + ["convwh"], writes=[can])
                                        else:
                                            sc.op("dve", lambda: nc.vector.scalar_tensor_tensor(
                                                out=ca[:], in0=uT[us][:, tt * 512 + k:tt * 512 + k + 512],
                                                scalar=convwh[:, ch, k:k + 1], in1=ca[:], op0=ALU.mult, op1=ALU.add),
                                                reads=rds[tt] + ["convwh", can], writes=[can])

                            gslot = {0: pre["glu"][0], 1: pre["glu"][1]}
                            build_wd(0)
                            glu_proj(0, list(range(NTT)), gslot[0])
                            KMID = (NPE + KC) // 2
                            for ch in range(8):
                                us = ch % 2
                                if ch == 7:
                                    pre["gc"] = [load_wgroup([O_GC + 512 * gi], width=512) for gi in range(2)]
                                if ch + 2 < 8:
                                    gslot[ch + 2] = load_wgroup([O_GLU + 128 * (ch + 2), O_GLU + 1024 + 128 * (ch + 2)])
                                if ch + 1 < 8:
                                    build_wd(ch + 1)
                                for tp in range(0, NTT, 2):
                                    tts = [t for t in (tp, tp + 1) if t < NTT]
                                    rds = {}
                                    if ch + 1 < 8:
                                        glu_pe(ch + 1, tts, gslot[ch + 1])
                                    for tt in tts:
                                        cb = 2 + (tt % 2)
                                        rd = ["Wd%d" % us, "uTpad%d" % us, "uT%d_%d" % (us, tt)]
                                        if tt > 0:
                                            rd.append("uT%d_%d" % (us, tt - 1))
                                        rds[tt] = rd
                                        for k in range(NPE):
                                            sc.op("pe", lambda k=k: nc.tensor.matmul(
                                                bank(cb), lhsT=Wd[us][:, k, :], rhs=uT[us][:, tt * 512 + k:tt * 512 + k + 512],
                                                start=(k == 0), stop=(k == NPE - 1)),
                                                reads=rd, writes=[BN[cb]], inc=(k == NPE - 1))
                                    taps(ch, tts, rds, NPE, KMID)
                                    if ch + 1 < 8:
                                        glu_dve(ch + 1, tts)
                                    taps(ch, tts, rds, KMID, KC)
                                    for tt in tts:
                                        cb = 2 + (tt % 2)
                                        ca = cacc[tt % 2]
                                        can = "cacc%d" % (tt % 2)
                                        sc.op("dve", lambda: nc.vector.scalar_tensor_tensor(
                                            out=yT[:, 8 + ch, tt * 512:(tt + 1) * 512], in0=bank(cb),
                                            scalar=convb[:, ch:ch + 1], in1=ca[:], op0=ALU.add, op1=ALU.add),
                                            reads=[BN[cb], "convb", can], writes=["yc%d_%d" % (ch, tt)])
                        sc.barrier()
                        sq2 = [ph4.enter_context(nc.sbuf_tensor("sqc%d_%d" % (b, i), [128, 512], BF16)) for i in range(2)]
                        mean_sb = ph4.enter_context(nc.sbuf_tensor("mean%d" % b, [128, S], F32))
                        rstd_sb = ph4.enter_context(nc.sbuf_tensor("rstdc%d" % b, [128, S], F32))
                        t1 = [ph4.enter_context(nc.sbuf_tensor("t1_%d_%d" % (b, i), [128, 512], F32)) for i in range(NSET)]
                        th2 = [ph4.enter_context(nc.sbuf_tensor("th2_%d_%d" % (b, i), [128, 512], F32)) for i in range(2)]
                        vq = [ph4.enter_context(nc.sbuf_tensor("vq_%d_%d" % (b, i), [128, 512], F32)) for i in range(NSET)]
                        thc = [ph4.enter_context(nc.sbuf_tensor("thc_%d_%d" % (b, i), [128, 512], F32)) for i in range(NSET)]
                        gcv = [ph4.enter_context(nc.sbuf_tensor("gcv_%d_%d" % (b, i), [128, 512], F32)) for i in range(NSET)]
                        m2 = t1[0]
                        vare = t1[1]
                        for tt in range(NTT):
                            for ch in range(8):
                                sc.op("pe", lambda ch=ch: nc.tensor.matmul(
                                    bank(4), lhsT=onesmean[:], rhs=yT[:, 8 + ch, tt * 512:(tt + 1) * 512],
                                    start=(ch == 0), stop=(ch == 7)),
                                    reads=["onesmean", "yc%d_%d" % (ch, tt)], writes=[BN[4]], inc=(ch == 7))
                            sc.op("act", lambda: nc.scalar.copy(out=mean_sb[:, tt * 512:(tt + 1) * 512], in_=bank(4)),
                                  reads=[BN[4]], writes=["mean%d" % tt])
                            for ch in range(8):
                                qi = ch % 2
                                ysq = yT[:, 8 + ch, tt * 512:(tt + 1) * 512]
                                if qi == 0:
                                    sc.op("act", lambda: nc.scalar.activation(out=sq2[qi][:], in_=ysq, func=AF.Square),
                                          reads=["yc%d_%d" % (ch, tt)], writes=["sqc%d" % qi])
                                else:
                                    sc.op("dve", lambda: nc.vector.tensor_tensor(out=sq2[qi][:], in0=ysq, in1=ysq, op=ALU.mult),
                                          reads=["yc%d_%d" % (ch, tt)], writes=["sqc%d" % qi])
                                sc.op("pe", lambda: nc.tensor.matmul(
                                    bank(5), lhsT=onesmean[:], rhs=sq2[qi][:], start=(ch == 0), stop=(ch == 7)),
                                    reads=["onesmean", "sqc%d" % qi], writes=[BN[5]], inc=True)
                            sc.op("dve", lambda: nc.vector.tensor_tensor(
                                out=m2[:], in0=mean_sb[:, tt * 512:(tt + 1) * 512],
                                in1=mean_sb[:, tt * 512:(tt + 1) * 512], op=ALU.mult),
                                reads=["mean%d" % tt], writes=["t1_0"])
                            sc.op("dve", lambda: nc.vector.scalar_tensor_tensor(
                                out=vare[:], in0=bank(5), scalar=EPS, in1=m2[:], op0=ALU.add, op1=ALU.subtract),
                                reads=[BN[5], "t1_0"], writes=["t1_1"])
                            sc.op("act", lambda: nc.scalar.activation(out=vare[:], in_=vare[:], func=AF.Ln),
                                  reads=["t1_1"], writes=["t1_1"])
                            sc.op("act", lambda: nc.scalar.activation(
                                out=rstd_sb[:, tt * 512:(tt + 1) * 512], in_=vare[:], func=AF.Exp, scale=-0.5),
                                reads=["t1_1"], writes=["rstdc%d" % tt])
                        gslots = pre["gc"]
                        blocks = [(gi, c4, tt) for tt in range(NTT) for gi in range(2) for c4 in range(4)]
                        xo = [ph4.enter_context(nc.sbuf_tensor("xo%d_%d" % (b, i), [128, D], F32)) for i in range(2)]
                        ot = [ph4.enter_context(nc.sbuf_tensor("ot%d_%d" % (b, i), [128, D], F32)) for i in range(1)]

                        def og(i, half):
                            s_ = i % 2
                            T = i // 4
                            ob = 4 + half
                            if half == 0:
                                sc.dma("sp", "xm%d" % s_, xo[s_][:], x[b, i * 128:(i + 1) * 128, :], writes=["xo%d" % s_])
                            for e in range(16):
                                if e < 8:
                                    rd = ["yT%d_0_%d" % (e, T), "yT%d_1_%d" % (e, T)]
                                else:
                                    rd = ["yc%d_%d" % (e - 8, T)]
                                sc.op("pe", lambda: nc.tensor.matmul(
                                    bank(ob), lhsT=yT[:, e, i * 128:(i + 1) * 128],
                                    rhs=wout[:, e, half * 512:(half + 1) * 512], start=(e == 0), stop=(e == 15)),
                                    reads=rd + ["wout%d" % (e // 8)], writes=[BN[ob]], inc=(e == 15))
                            sc.op("dve", lambda: nc.vector.tensor_tensor(
                                out=ot[0][:, half * 512:(half + 1) * 512], in0=bank(ob),
                                in1=xo[s_][:, half * 512:(half + 1) * 512], op=ALU.add),
                                reads=[BN[ob], "xo%d" % s_], writes=["ot_%d" % half])
                            if half == 1:
                                sc.dma("sp", "ost", out[b, i * 128:(i + 1) * 128, :], ot[0][:], reads=["ot_0", "ot_1"])

                        ogs = [(i, half) for i in range(NT) for half in range(2)]

                        LB = (0, 1, 2, 3, 6, 7)

                        def ln_stages(n):
                            gi, c4, tt = blocks[n]
                            ch = gi * 4 + c4
                            u = n % NSET
                            pbk = LB[n % 6]
                            ysl = yT[:, 8 + ch, tt * 512:(tt + 1) * 512]

                            def s0():
                                proj_fm(gslots[gi], c4, tt, pbk)

                            def s1():
                                sc.op("act", lambda: nc.scalar.activation(out=thc[u][:], in_=bank(pbk), func=AF.Tanh, scale=0.5),
                                      reads=[BN[pbk]], writes=["thc%d" % u])
                                sc.op("dve", lambda: nc.vector.tensor_tensor(
                                    out=t1[u][:], in0=ysl, in1=mean_sb[:, tt * 512:(tt + 1) * 512], op=ALU.subtract),
                                    reads=["yc%d_%d" % (ch, tt), "mean%d" % tt], writes=["t1_%d" % u])

                            def s2():
                                sc.op("pool", lambda: nc.gpsimd.tensor_tensor(
                                    out=t1[u][:], in0=t1[u][:], in1=rstd_sb[:, tt * 512:(tt + 1) * 512], op=ALU.mult),
                                    reads=["t1_%d" % u, "rstdc%d" % tt], writes=["t1_%d" % u])

                            def s3():
                                sc.op("act", lambda: nc.scalar.activation(
                                    out=th2[n % 2][:], in_=t1[u][:], func=AF.Tanh, bias=lnb_h[:, ch:ch + 1],
                                    scale=lng_h[:, ch:ch + 1]),
                                    reads=["t1_%d" % u, "lng_h", "lnb_h"], writes=["th2_%d" % (n % 2)])
                                sc.op("dve", lambda: nc.vector.scalar_tensor_tensor(
                                    out=gcv[u][:], in0=thc[u][:], scalar=1.0, in1=bank(pbk), op0=ALU.add,
                                    op1=ALU.mult), reads=[BN[pbk], "thc%d" % u], writes=["gcv%d" % u])
                                sc.op("act", lambda: nc.scalar.activation(
                                    out=vq[u][:], in_=t1[u][:], func=AF.Identity, bias=lnb_q[:, ch:ch + 1],
                                    scale=lng_q[:, ch:ch + 1]),
                                    reads=["t1_%d" % u, "lng_q", "lnb_q"], writes=["vq%d" % u])

                            def s4():
                                sc.op("dve", lambda: nc.vector.scalar_tensor_tensor(
                                    out=vq[u][:], in0=th2[n % 2][:], scalar=1.0, in1=vq[u][:], op0=ALU.add, op1=ALU.mult),
                                    reads=["th2_%d" % (n % 2), "vq%d" % u], writes=["vq%d" % u])

                            def s5():
                                if n % 2 == 0:
                                    sc.op("pool", lambda: nc.gpsimd.tensor_tensor(
                                        out=ysl, in0=vq[u][:], in1=gcv[u][:], op=ALU.mult),
                                        reads=["vq%d" % u, "gcv%d" % u], writes=["yc%d_%d" % (ch, tt)])
                                else:
                                    sc.op("dve", lambda: nc.vector.tensor_tensor(
                                        out=ysl, in0=vq[u][:], in1=gcv[u][:], op=ALU.mult),
                                        reads=["vq%d" % u, "gcv%d" % u], writes=["yc%d_%d" % (ch, tt)])
                            return [s0, s1, s2, s3, s4, s5]

                        lnu = [ln_stages(n) for n in range(len(blocks))]
                        oq = 0
                        for slot_i in range(len(lnu) + 5):
                            for k in range(5, -1, -1):
                                ui_ = slot_i - k
                                if 0 <= ui_ < len(lnu):
                                    lnu[ui_][k]()
                            if slot_i >= 12 and oq < len(ogs) and oq <= slot_i - 12:
                                og(*ogs[oq])
                                oq += 1
                        while oq < len(ogs):
                            og(*ogs[oq])
                            oq += 1
          except _Stop:
            stopped = True
            break
        sc.wait_all("sp")
        if stopped:
            es.pop_all()
    return nc


def _prep_shared(inp):
    f = np.float32
    qg = np.asarray(inp["q_norm_g"], f)[0]
    kg = np.asarray(inp["k_norm_g"], f)[0]
    gq = np.ascontiguousarray(qg.reshape(8, 128).T)
    gk = np.ascontiguousarray(kg.reshape(8, 128).T)
    cw = np.asarray(inp["conv_w"], f)[0]
    convw = np.ascontiguousarray(cw.T.reshape(8, 128, KC).transpose(1, 0, 2))

    def pc(a):
        return np.ascontiguousarray(np.asarray(a, f)[0].reshape(8, 128).T)

    return {
        "w_in": np.ascontiguousarray(np.asarray(inp["w_in"], f)[0]),
        "w_out": np.ascontiguousarray(np.asarray(inp["w_out"], f)[0]),
        "norm_g": np.ascontiguousarray(np.asarray(inp["norm_g"], f)),
        "b_forget": np.ascontiguousarray(np.asarray(inp["b_forget"], f)),
        "gq": gq, "gk": gk, "convw": convw,
        "convb": pc(inp["conv_b"]), "lng": pc(inp["conv_ln_g"]), "lnb": pc(inp["conv_ln_b"]),
    }


def kernel(**inputs):
    x = np.asarray(inputs["x"], np.float32)
    B, S, _ = x.shape
    n = N_CORES
    nb = B // n
    shared = _prep_shared(inputs)
    nc = build_nc(NB=nb, S=S)
    in_maps = []
    for c in range(n):
        m = dict(shared)
        m["x"] = np.ascontiguousarray(x[c * nb:(c + 1) * nb])
        in_maps.append(m)
    res = run_bass_kernel_spmd(nc, in_maps, core_ids=list(range(n)))
    return np.concatenate([np.asarray(r["out"]) for r in res.results], axis=0).astype(np.float32)
```

```python
import numpy as np
from contextlib import ExitStack
import concourse.bass as bass
import concourse.mybir as mybir
from concourse.bass_utils import run_bass_kernel_spmd

F32 = mybir.dt.float32
BF16 = mybir.dt.bfloat16
AF = mybir.ActivationFunctionType
ALU = mybir.AluOpType

D = 1024
NH = 16
HD = 64
KC = 31
IN_COLS = 7184
O_Q, O_K, O_V, O_F, O_GF, O_GLU, O_GC = 0, 1024, 2048, 3072, 3088, 4112, 6160
EPS = 1e-6
MASKV = -30000.0
N_CORES = 8
INTERLEAVE = False
PJB = (0, 1, 3, 5, 7)
NSET = 3
NPE = 23


class Sched:
    def __init__(self, nc, es):
        self.nc = nc
        self.eng = {"pe": nc.tensor, "act": nc.scalar, "dve": nc.vector, "pool": nc.gpsimd, "sp": nc.sync}
        self.sem = {k: es.enter_context(nc.semaphore("s_" + k)) for k in self.eng}
        self.cnt = {k: 0 for k in self.eng}
        self.seen = {k: {} for k in self.eng}
        self.dsem = {}
        self.dcnt = {}
        self.es = es
        self.res = {}
        self.pending_noinc = {k: False for k in self.eng}

    def _semh(self, key):
        return self.sem[key] if key in self.sem else self.dsem[key]

    def _wait(self, engine, tok):
        key, val = tok
        if self.seen[engine].get(key, 0) >= val:
            return
        self.seen[engine][key] = val
        self.eng[engine].wait_ge(self._semh(key), val)

    def _deps(self, engine, reads, writes):
        for r in reads:
            st = self.res.get(r)
            if st is not None and st["w"] is not None:
                if not (engine == "pe" and st["w"][0] == "pe"):
                    self._wait(engine, st["w"])
        for w in writes:
            st = self.res.get(w)
            if st is None:
                continue
            if st["w"] is not None and not (engine == "pe" and st["w"][0] == "pe"):
                self._wait(engine, st["w"])
            for k, v in st["r"].items():
                if k == engine and engine == "pe":
                    continue
                self._wait(engine, (k, v))

    def _record(self, tok, reads, writes):
        for r in reads:
            st = self.res.setdefault(r, {"w": None, "r": {}})
            if st["r"].get(tok[0], 0) < tok[1]:
                st["r"][tok[0]] = tok[1]
        for w in writes:
            self.res[w] = {"w": tok, "r": {}}

    def op(self, engine, fn, reads=(), writes=(), inc=True):
        self._deps(engine, reads, writes)
        ins = fn()
        if inc:
            self.cnt[engine] += 1
            ins.then_inc(self.sem[engine], 1)
            tok = (engine, self.cnt[engine])
            self.pending_noinc[engine] = False
        else:
            tok = (engine, self.cnt[engine] + 1)
            self.pending_noinc[engine] = True
        self._record(tok, reads, writes)
        return tok

    def dma(self, queue, semname, out, in_, reads=(), writes=()):
        if semname not in self.dsem:
            self.dsem[semname] = self.es.enter_context(self.nc.semaphore("d_" + semname))
            self.dcnt[semname] = 0
        self._deps(queue, reads, writes)
        ins = self.eng[queue].dma_start(out=out, in_=in_)
        self.dcnt[semname] += 16
        ins.then_inc(self.dsem[semname], 16)
        tok = (semname, self.dcnt[semname])
        self._record(tok, reads, writes)
        return tok

    def retoken(self, names, tok):
        for n in names:
            self.res[n] = {"w": tok, "r": {}}

    def barrier(self):
        assert not self.pending_noinc["pe"]
        for e in self.eng:
            self.wait_all(e)

    def wait_all(self, engine):
        for k in self.eng:
            if k != engine and self.cnt[k] > 0:
                self._wait(engine, (k, self.cnt[k]))
        for k, v in self.dcnt.items():
            if v > 0:
                self._wait(engine, (k, v))


class _Stop(Exception):
    pass


def build_nc(NB=2, S=2048, STOP=9):
    NT = S // 128
    NTT = S // 512
    nc = bass.Bass("TRN2", target_bir_lowering=False)
    x = nc.dram_tensor("x", [NB, S, D], F32, kind="ExternalInput").ap()
    w_in = nc.dram_tensor("w_in", [D, IN_COLS], F32, kind="ExternalInput").ap()
    w_out = nc.dram_tensor("w_out", [2 * D, D], F32, kind="ExternalInput").ap()
    norm_g = nc.dram_tensor("norm_g", [1, D], F32, kind="ExternalInput").ap()
    b_forget = nc.dram_tensor("b_forget", [1, NH], F32, kind="ExternalInput").ap()
    gq_d = nc.dram_tensor("gq", [128, 8], F32, kind="ExternalInput").ap()
    gk_d = nc.dram_tensor("gk", [128, 8], F32, kind="ExternalInput").ap()
    convw_d = nc.dram_tensor("convw", [128, 8, KC], F32, kind="ExternalInput").ap()
    convb_d = nc.dram_tensor("convb", [128, 8], F32, kind="ExternalInput").ap()
    lng_d = nc.dram_tensor("lng", [128, 8], F32, kind="ExternalInput").ap()
    lnb_d = nc.dram_tensor("lnb", [128, 8], F32, kind="ExternalInput").ap()
    out = nc.dram_tensor("out", [NB, S, D], F32, kind="ExternalOutput").ap()

    w_in_v = w_in.rearrange("(c p) n -> p c n", p=128)
    w_out_v = w_out.rearrange("(c p) n -> p c n", p=128)

    with ExitStack() as es:
        sc = Sched(nc, es)

        def sb(name, shape, dt):
            return es.enter_context(nc.sbuf_tensor(name, shape, dt))

        ident = sb("ident", [128, 128], BF16)
        maskT = sb("maskT", [128, 128], BF16)
        blockones = sb("blockones", [128, 128], BF16)
        onesmean = sb("onesmean", [128, 128], BF16)
        Uneg = sb("Uneg", [128, 128], F32)
        OnesNeg = sb("OnesNeg", [128, 128], F32)
        Sel = sb("Sel", [128, 128], F32)
        bf_tile = sb("bf_tile", [128, NH], F32)
        gq = sb("gq_sb", [128, 8], F32)
        gk = sb("gk_sb", [128, 8], F32)
        gq8 = sb("gq8", [128, 8], F32)
        convw = sb("convw_sb", [128, 8, KC], F32)
        convwh = sb("convwh", [128, 8, KC], F32)
        convb = sb("convb_sb", [128, 8], F32)
        lng = sb("lng_sb", [128, 8], F32)
        lnb = sb("lnb_sb", [128, 8], F32)
        lng_h = sb("lng_h", [128, 8], F32)
        lnb_h = sb("lnb_h", [128, 8], F32)
        lng_q = sb("lng_q", [128, 8], F32)
        lnb_q = sb("lnb_q", [128, 8], F32)
        wf = sb("wf", [128, 8, NH], BF16)
        hT = sb("hT", [128, 8, S], BF16)
        yT = sb("yT", [128, 16, S], BF16)
        wbuf = [sb("wbuf%d" % i, [128, 8, 512], BF16) for i in range(2)]

        PS = [es.enter_context(nc.psum_tensor("ps%d" % i, [128, 1024], F32)) for i in range(4)]

        def bank(b):
            return PS[b // 2][:, (b % 2) * 512:(b % 2) * 512 + 512]

        def bank_bf(b):
            t = PS[b // 2].bitcast(BF16)
            return t[:, (b % 2) * 1024:(b % 2) * 1024 + 1024]

        BN = ["B%d" % i for i in range(8)]

        g = nc.gpsimd
        sc.op("pool", lambda: g.memset(ident[:], 1.0), writes=["ident"])
        sc.op("pool", lambda: g.affine_select(out=ident[:], in_=ident[:], compare_op=ALU.is_equal, fill=0.0,
                                              base=0, pattern=[[1, 128]], channel_multiplier=-1),
              reads=["ident"], writes=["ident"])
        sc.op("pool", lambda: g.memset(maskT[:], 0.0), writes=["maskT"])
        sc.op("pool", lambda: g.affine_select(out=maskT[:], in_=maskT[:], compare_op=ALU.is_ge, fill=MASKV,
                                              base=0, pattern=[[1, 128]], channel_multiplier=-1),
              reads=["maskT"], writes=["maskT"])
        sc.op("pool", lambda: g.memset(blockones[:], 0.0), writes=["blockones"])
        sc.op("pool", lambda: g.memset(blockones[0:64, 0:64], 1.0 / 64), reads=["blockones"], writes=["blockones"])
        sc.op("pool", lambda: g.memset(blockones[64:128, 64:128], 1.0 / 64), reads=["blockones"], writes=["blockones"])
        sc.op("pool", lambda: g.memset(onesmean[:], 1.0 / 1024), writes=["onesmean"])
        sc.op("pool", lambda: g.memset(Uneg[:], -1.0), writes=["Uneg"])
        sc.op("pool", lambda: g.affine_select(out=Uneg[:], in_=Uneg[:], compare_op=ALU.is_ge, fill=0.0,
                                              base=0, pattern=[[1, 128]], channel_multiplier=-1),
              reads=["Uneg"], writes=["Uneg"])
        sc.op("pool", lambda: g.memset(OnesNeg[:], -1.0), writes=["OnesNeg"])
        sc.op("pool", lambda: g.memset(Sel[:], 1.0), writes=["Sel"])
        sc.op("pool", lambda: g.affine_select(out=Sel[:], in_=Sel[:], compare_op=ALU.is_equal, fill=0.0,
                                              base=-127, pattern=[[0, 128]], channel_multiplier=1),
              reads=["Sel"], writes=["Sel"])

        sc.dma("sp", "par", bf_tile[:], b_forget.partition_broadcast(128).rearrange("p o n -> p (o n)"), writes=["bf_tile"])
        sc.dma("sp", "par", gq[:], gq_d, writes=["gq"])
        sc.dma("sp", "par", gk[:], gk_d, writes=["gk"])
        sc.dma("sp", "par", convw[:], convw_d, writes=["convw"])
        sc.dma("sp", "par", convb[:], convb_d, writes=["convb"])
        sc.dma("sp", "par", lng[:], lng_d, writes=["lng"])
        sc.dma("sp", "par", lnb[:], lnb_d, writes=["lnb"])
        sc.retoken(["bf_tile", "gq", "gk", "convw", "convb", "lng", "lnb"], ("par", sc.dcnt["par"]))
        sc.dma("pool", "wfl", wf[:], w_in_v[:, :, O_F:O_F + NH], writes=["wf"])
        v = nc.vector
        sc.op("dve", lambda: v.tensor_scalar(out=gq8[:], in0=gq[:], scalar1=0.125, scalar2=None, op0=ALU.mult),
              reads=["gq"], writes=["gq8"])
        sc.op("dve", lambda: v.tensor_scalar(out=convwh[:], in0=convw[:], scalar1=0.5, scalar2=None, op0=ALU.mult),
              reads=["convw"], writes=["convwh"])
        sc.op("dve", lambda: v.tensor_scalar(out=lng_h[:], in0=lng[:], scalar1=0.5, scalar2=None, op0=ALU.mult),
              reads=["lng"], writes=["lng_h"])
        sc.op("dve", lambda: v.tensor_scalar(out=lnb_h[:], in0=lnb[:], scalar1=0.5, scalar2=None, op0=ALU.mult),
              reads=["lnb"], writes=["lnb_h"])
        sc.op("dve", lambda: v.tensor_scalar(out=lng_q[:], in0=lng[:], scalar1=0.25, scalar2=None, op0=ALU.mult),
              reads=["lng"], writes=["lng_q"])
        sc.op("dve", lambda: v.tensor_scalar(out=lnb_q[:], in0=lnb[:], scalar1=0.25, scalar2=None, op0=ALU.mult),
              reads=["lnb"], writes=["lnb_q"])

        wstate = {"n": 0}

        def load_wgroup(cols, width=128):
            slot = wstate["n"] % 2
            wstate["n"] += 1
            names = []
            tok = None
            for j, c0 in enumerate(cols):
                nm = ["wbuf%d_%d" % (slot, jj) for jj in range(j * width // 128, (j + 1) * width // 128)]
                tok = sc.dma("pool", "w%d" % slot, wbuf[slot][:, :, j * width:(j + 1) * width],
                             w_in_v[:, :, c0:c0 + width], writes=nm)
                names += nm
            sc.retoken(names, tok)
            return slot

        def proj_fm(slot, blk, tt, b_out):
            for c in range(8):
                sc.op("pe", lambda c=c: nc.tensor.matmul(bank(b_out), lhsT=wbuf[slot][:, c, blk * 128:(blk + 1) * 128],
                                                         rhs=hT[:, c, tt * 512:(tt + 1) * 512],
                                                         start=(c == 0), stop=(c == 7)),
                      reads=["wbuf%d_%d" % (slot, blk)] + ["hT%d" % i for i in range(4 * tt, 4 * tt + 4)],
                      writes=[BN[b_out]], inc=(c == 7))

        stopped = False
        for b in range(NB):
          try:
                with ExitStack() as ph:
                    if STOP < 1:
                        raise _Stop()
                    sc.barrier()
                    NXS = 4
                    g_tile = ph.enter_context(nc.sbuf_tensor("g_tile%d" % b, [128, D], F32))
                    sc.dma("sp", "gt", g_tile[:], norm_g.partition_broadcast(128).rearrange("p o n -> p (o n)"), writes=["g_tile"])
                    xt = [ph.enter_context(nc.sbuf_tensor("xt%d_%d" % (b, i), [128, D], F32)) for i in range(NXS)]
                    hb = [ph.enter_context(nc.sbuf_tensor("hb%d_%d" % (b, i), [128, D], BF16)) for i in range(3)]
                    junk = ph.enter_context(nc.sbuf_tensor("junk%d" % b, [128, D], BF16))
                    ssq = ph.enter_context(nc.sbuf_tensor("ssq%d" % b, [128, NT], F32))
                    msx = ph.enter_context(nc.sbuf_tensor("msx%d" % b, [128, NT], F32))
                    rsx = ph.enter_context(nc.sbuf_tensor("rsx%d" % b, [128, NT], F32))

                    def p1_stages(i):
                        s_ = i % NXS
                        hs = i % 3
                        pb = i % 2
                        pbf = bank_bf(pb)

                        def s0():
                            sc.dma("sp", "xl%d" % s_, xt[s_][:], x[b, i * 128:(i + 1) * 128, :], writes=["xt%d" % s_])

                        def s1():
                            sc.op("act", lambda: nc.scalar.activation(out=junk[:], in_=xt[s_][:], func=AF.Square,
                                                                      accum_out=ssq[:, i:i + 1]),
                                  reads=["xt%d" % s_], writes=["junk", "ssq%d" % i])
                            sc.op("act", lambda: nc.scalar.activation(out=msx[:, i:i + 1], in_=ssq[:, i:i + 1],
                                                                      func=AF.Ln, bias=EPS, scale=1.0 / D),
                                  reads=["ssq%d" % i], writes=["msx%d" % i])
                            sc.op("act", lambda: nc.scalar.activation(out=rsx[:, i:i + 1], in_=msx[:, i:i + 1],
                                                                      func=AF.Exp, scale=-0.5),
                                  reads=["msx%d" % i], writes=["rsx%d" % i])

                        def s2():
                            sc.op("dve", lambda: nc.vector.scalar_tensor_tensor(out=hb[hs][:], in0=xt[s_][:],
                                                                                scalar=rsx[:, i:i + 1], in1=g_tile[:],
                                                                                op0=ALU.mult, op1=ALU.mult),
                                  reads=["xt%d" % s_, "rsx%d" % i, "g_tile"], writes=["hb%d" % hs])

                        def s3():
                            for c in range(8):
                                sc.op("pe", lambda: nc.tensor.transpose(pbf[:, c * 128:(c + 1) * 128],
                                                                        hb[hs][:, c * 128:(c + 1) * 128], ident[:]),
                                      reads=["hb%d" % hs, "ident"], writes=[BN[pb]], inc=(c == 7))

                        def s4():
                            sc.op("dve", lambda: nc.vector.tensor_copy(out=hT[:, :, i * 128:(i + 1) * 128],
                                                                       in_=pbf.rearrange("p (c t) -> p c t", c=8)),
                                  reads=[BN[pb]], writes=["hT%d" % i])
                        return [s0, s1, s2, s3, s4]

                    p1u = [p1_stages(i) for i in range(NT)]
                    for slot_i in range(NT + 4):
                        for k in range(4, -1, -1):
                            ui_ = slot_i - k
                            if 0 <= ui_ < NT:
                                p1u[ui_][k]()

                pre = {}
                pre["pairs"] = [load_wgroup([O_Q + 128 * p_, O_K + 128 * p_, O_V + 128 * p_, O_GF + 128 * p_]) for p_ in range(2)]
                ph23 = ExitStack()
                bias_all = ph23.enter_context(nc.sbuf_tensor("bias_all%d" % b, [128, NH, NTT, NT], F32))
                dT = ph23.enter_context(nc.sbuf_tensor("dT%d" % b, [NH, S], BF16))
                with ExitStack() as ph:
                    if STOP < 2:
                        raise _Stop()
                    sc.barrier()
                    xf = ph.enter_context(nc.sbuf_tensor("xf%d" % b, [128, NT, NH], F32))
                    ef = ph.enter_context(nc.sbuf_tensor("ef%d" % b, [128, NT, NH], F32))
                    lfn = ph.enter_context(nc.sbuf_tensor("lfn%d" % b, [128, NT, NH], F32))
                    padA = ph.enter_context(nc.sbuf_tensor("padA%d" % b, [128, 8 + NT, NH], F32))
                    padB = ph.enter_context(nc.sbuf_tensor("padB%d" % b, [128, 8 + NT, NH], F32))
                    c_tok = ph.enter_context(nc.sbuf_tensor("c_tok%d" % b, [128, NT, NH], F32))
                    rB = ph.enter_context(nc.sbuf_tensor("rB%d" % b, [128, NT, NH], F32))
                    d_tok = ph.enter_context(nc.sbuf_tensor("d_tok%d" % b, [128, NT, NH], BF16))
                    NF = NT * NH
                    zf = bank(0)[:, 0:NF]
                    for i in range(NT):
                        for c in range(8):
                            sc.op("pe", lambda c=c: nc.tensor.matmul(bank(0)[:, i * NH:(i + 1) * NH],
                                                                     lhsT=hT[:, c, i * 128:(i + 1) * 128], rhs=wf[:, c, :],
                                                                     start=(c == 0), stop=(c == 7)),
                                  reads=["hT%d" % i, "wf"], writes=[BN[0]], inc=(c == 7 and i == NT - 1))
                    sc.op("dve", lambda: nc.vector.tensor_tensor(
                        out=xf[:], in0=zf.rearrange("p (i h) -> p i h", h=NH),
                        in1=bf_tile[:].rearrange("p (o h) -> p o h", o=1).broadcast_to([128, NT, NH]), op=ALU.add),
                        reads=[BN[0], "bf_tile"], writes=["xf"])
                    sc.op("act", lambda: nc.scalar.activation(out=ef[:], in_=xf[:], func=AF.Exp, scale=-1.0),
                          reads=["xf"], writes=["ef"])
                    sc.op("act", lambda: nc.scalar.activation(out=lfn[:], in_=ef[:], func=AF.Ln, bias=1.0, scale=1.0),
                          reads=["ef"], writes=["lfn"])
                    lfn2 = lfn[:].rearrange("p i h -> p (i h)")
                    sc.op("pe", lambda: nc.tensor.matmul(bank(1)[:, 0:NF], lhsT=Uneg[:], rhs=lfn2, start=True, stop=True),
                          reads=["lfn", "Uneg"], writes=[BN[1]], inc=False)
                    sc.op("pe", lambda: nc.tensor.matmul(bank(1)[:, 256:256 + NF], lhsT=OnesNeg[:], rhs=lfn2, start=True,
                                                         stop=True),
                          reads=["lfn", "OnesNeg"], writes=[BN[1]])
                    sc.op("dve", lambda: nc.vector.memset(padA[:], 0.0), writes=["padA"])
                    sc.op("dve", lambda: nc.vector.memset(padB[:], 0.0), writes=["padB"])
                    if NT > 1:
                        sc.op("dve", lambda: nc.vector.tensor_copy(
                            out=padA[:, 9:8 + NT, :],
                            in_=bank(1)[:, 256:256 + NF].rearrange("p (i h) -> p i h", h=NH)[:, 0:NT - 1, :]),
                            reads=[BN[1], "padA"], writes=["padA"])
                    cur, oth, curn, othn = padA, padB, "padA", "padB"
                    dstep = 1
                    while dstep < NT:
                        sc.op("dve", lambda cur=cur, oth=oth, dstep=dstep: nc.vector.tensor_tensor(
                            out=oth[:, 8:8 + NT, :], in0=cur[:, 8:8 + NT, :], in1=cur[:, 8 - dstep:8 + NT - dstep, :],
                            op=ALU.add), reads=[curn], writes=[othn])
                        cur, oth, curn, othn = oth, cur, othn, curn
                        dstep *= 2
                    sc.op("dve", lambda: nc.vector.tensor_tensor(
                        out=c_tok[:], in0=bank(1)[:, 0:NF].rearrange("p (i h) -> p i h", h=NH), in1=cur[:, 8:8 + NT, :],
                        op=ALU.add), reads=[BN[1], curn], writes=["c_tok"])
                    sc.op("pe", lambda: nc.tensor.matmul(bank(0)[:, 0:NF], lhsT=Sel[:],
                                                         rhs=c_tok[:].rearrange("p i h -> p (i h)"), start=True, stop=True),
                          reads=["c_tok", "Sel"], writes=[BN[0]])
                    sc.op("act", lambda: nc.scalar.copy(out=rB[:].rearrange("p i h -> p (i h)"), in_=bank(0)[:, 0:NF]),
                          reads=[BN[0]], writes=["rB"])
                    c4 = c_tok[:].rearrange("p (a j) h -> p a j h", j=4)
                    r4 = rB[:].rearrange("p (a j) h -> p a j h", j=4)[:, :, 3:4, :].broadcast_to([128, NTT, 4, NH])
                    sc.op("dve", lambda: nc.vector.tensor_tensor(
                        out=d_tok[:].rearrange("p (a j) h -> p a j h", j=4), in0=c4, in1=r4, op=ALU.subtract),
                        reads=["c_tok", "rB"], writes=["d_tok"])
                    for T in range(NTT):
                        sc.op("dve", lambda T=T: nc.vector.tensor_tensor(
                            out=bias_all[:, :, T, :],
                            in0=rB[:, 4 * T + 3:4 * T + 4, :].rearrange("p o h -> p h o").broadcast_to([128, NH, NT]),
                            in1=c_tok[:].rearrange("p i h -> p h i"), op=ALU.subtract),
                            reads=["c_tok", "rB"], writes=["bias_all"])
                    dps = PS[1].bitcast(BF16)
                    for i in range(NT):
                        sc.op("pe", lambda i=i: nc.tensor.transpose(dps[0:NH, i * 128:(i + 1) * 128], d_tok[:, i, :],
                                                                    ident[:]),
                              reads=["d_tok", "ident"], writes=[BN[2], BN[3]], inc=(i == NT - 1))
                    sc.op("act", lambda: nc.scalar.copy(out=dT[:], in_=dps[0:NH, 0:S]),
                          reads=[BN[2], BN[3]], writes=["dT"])

                with ExitStack() as ph:
                    if STOP < 3:
                        raise _Stop()
                    sc.barrier()
                    QA = [[ph.enter_context(nc.sbuf_tensor("QA%d_%d_%d" % (b, pp, i), [128, S], BF16)) for i in range(2)] for pp in range(2)]
                    KA = [[ph.enter_context(nc.sbuf_tensor("KA%d_%d_%d" % (b, pp, i), [128, S], BF16)) for i in range(2)] for pp in range(2)]
                    Vaug = [ph.enter_context(nc.sbuf_tensor("Vaug%d_%d" % (b, pp), [128, NT, 2, 128], BF16)) for pp in range(2)]
                    gate = [ph.enter_context(nc.sbuf_tensor("gate%d_%d" % (b, pp), [128, S], BF16)) for pp in range(2)]
                    sq_sb = [ph.enter_context(nc.sbuf_tensor("sq%d_%d" % (b, i), [128, 512], BF16)) for i in range(2)]
                    msb = [ph.enter_context(nc.sbuf_tensor("msb%d_%d" % (b, i), [128, 512], F32)) for i in range(2)]
                    rstd = [ph.enter_context(nc.sbuf_tensor("rstd%d_%d" % (b, i), [128, 512], F32)) for i in range(2)]
                    th_sb = [ph.enter_context(nc.sbuf_tensor("th%d_%d" % (b, i), [128, 512], F32)) for i in range(2)]
                    pbuf = [ph.enter_context(nc.sbuf_tensor("pbuf%d_%d" % (b, i), [128, 512], BF16)) for i in range(4)]
                    rec = [ph.enter_context(nc.sbuf_tensor("rec%d_%d" % (b, i), [128, 512], F32)) for i in range(3)]
                    for pp in range(2):
                        for hh in range(2):
                            t_ = "%d%d" % (pp, hh)
                            sc.op("dve", lambda: nc.vector.memset(KA[pp][hh][64:128, :], 0.0), writes=["KArow" + t_])
                            sc.op("dve", lambda: nc.vector.memset(KA[pp][hh][64:65, :], 1.0), reads=["KArow" + t_],
                                  writes=["KArow" + t_])
                            sc.op("dve", lambda: nc.vector.memset(QA[pp][hh][64:128, :], 0.0), writes=["QArow" + t_])
                        sc.op("dve", lambda: nc.vector.memset(Vaug[pp][:, :, 0, 64:128], 1.0), writes=["Vones%d0" % pp])
                        sc.op("dve", lambda: nc.vector.memset(Vaug[pp][:, :, 1, 0:64], 1.0), writes=["Vones%d1" % pp])

                    nq = {"n": 0}
                    wslots = {}

                    def load_pair(p):
                        wslots[p] = load_wgroup([O_Q + 128 * p, O_K + 128 * p, O_V + 128 * p, O_GF + 128 * p])

                    def proj_units(p):
                        pp = p % 2
                        units = []

                        def v_unit(i0):
                            st = {}

                            def A():
                                cur_slot = wslots[p]
                                vb = PJB[nq["n"] % 5]
                                nq["n"] += 1
                                st["vb"] = vb
                                for ii in range(4):
                                    i = i0 + ii
                                    for c in range(8):
                                        sc.op("pe", lambda: nc.tensor.matmul(
                                            bank(vb)[:, ii * 128:(ii + 1) * 128], lhsT=hT[:, c, i * 128:(i + 1) * 128],
                                            rhs=wbuf[cur_slot][:, c, 256:384], start=(c == 0), stop=(c == 7)),
                                            reads=["hT%d" % i, "wbuf%d_2" % cur_slot], writes=[BN[vb]],
                                            inc=(c == 7 and ii == 3))

                            def B():
                                vb = st["vb"]
                                bv = bank(vb).rearrange("p (i c) -> p i c", i=4)
                                sc.op("act", lambda: nc.scalar.copy(out=Vaug[pp][:, i0:i0 + 4, 0, 0:64], in_=bv[:, :, 0:64]),
                                      reads=[BN[vb], "Vones%d0" % pp], writes=["V%d0_%d" % (pp, i0 // 4)])
                                sc.op("act", lambda: nc.scalar.copy(out=Vaug[pp][:, i0:i0 + 4, 1, 64:128], in_=bv[:, :, 64:128]),
                                      reads=[BN[vb], "Vones%d1" % pp], writes=["V%d1_%d" % (pp, i0 // 4)])
                            return [A, B]

                        def qk_unit(tt, qk):
                            st = {}

                            def A():
                                cur_slot = wslots[p]
                                n = nq["n"]
                                nq["n"] += 1
                                st["u"] = u = n % 2
                                st["bk"] = bk = PJB[n % 5]
                                proj_fm(cur_slot, qk, tt, bk)

                            def A2():
                                u = st["u"]
                                bk = st["bk"]
                                sc.op("act", lambda: nc.scalar.activation(out=sq_sb[u][:], in_=bank(bk), func=AF.Square),
                                      reads=[BN[bk]], writes=["sq%d" % u])

                            def B1():
                                u = st["u"]
                                ab = 6
                                sc.op("pe", lambda: nc.tensor.matmul(bank(ab), lhsT=blockones[:], rhs=sq_sb[u][:], start=True, stop=True),
                                      reads=["sq%d" % u, "blockones"], writes=[BN[ab]])
                                sc.op("act", lambda: nc.scalar.activation(out=msb[u][:], in_=bank(ab), func=AF.Ln, bias=EPS, scale=1.0),
                                      reads=[BN[ab]], writes=["msb%d" % u])

                            def B2():
                                u = st["u"]
                                sc.op("act", lambda: nc.scalar.activation(out=rstd[u][:], in_=msb[u][:], func=AF.Exp, scale=-0.5),
                                      reads=["msb%d" % u], writes=["rstd%d" % u])

                            def C():
                                u = st["u"]
                                bk = st["bk"]
                                dst = QA[pp] if qk == 0 else KA[pp]
                                gsc = gq8 if qk == 0 else gk
                                nm = "QA" if qk == 0 else "KA"
                                for hh in range(2):
                                    r0 = hh * 64
                                    sc.op("dve", lambda: nc.vector.scalar_tensor_tensor(
                                        out=dst[hh][0:64, tt * 512:(tt + 1) * 512], in0=bank(bk)[r0:r0 + 64, :],
                                        scalar=gsc[r0:r0 + 64, p:p + 1], in1=rstd[u][r0:r0 + 64, :], op0=ALU.mult, op1=ALU.mult),
                                        reads=[BN[bk], "rstd%d" % u, "gq8", "gk"], writes=["%s%d%d_%d" % (nm, pp, hh, tt)])
                            return [A, A2, B1, B2, C]

                        def gate_unit(tt):
                            st = {}

                            def A():
                                cur_slot = wslots[p]
                                n = nq["n"]
                                nq["n"] += 1
                                st["u"] = u = n % 2
                                st["bk"] = bk = PJB[n % 5]
                                proj_fm(cur_slot, 3, tt, bk)

                            def A2():
                                u = st["u"]
                                bk = st["bk"]
                                sc.op("act", lambda: nc.scalar.activation(out=th_sb[u][:], in_=bank(bk), func=AF.Tanh, scale=0.5),
                                      reads=[BN[bk]], writes=["th%d" % u])

                            def B():
                                u = st["u"]
                                bk = st["bk"]
                                sc.op("dve", lambda: nc.vector.scalar_tensor_tensor(
                                    out=gate[pp][:, tt * 512:(tt + 1) * 512], in0=th_sb[u][:], scalar=1.0, in1=bank(bk),
                                    op0=ALU.add, op1=ALU.mult),
                                    reads=[BN[bk], "th%d" % u], writes=["gate%d_%d" % (pp, tt)])
                            return [A, A2, B]

                        def drow_unit():
                            def A():
                                for hh in range(2):
                                    h = 2 * p + hh
                                    sc.dma("sp", "drow%d%d" % (pp, hh), QA[pp][hh][64:65, :], dT[h:h + 1, :], reads=["dT"],
                                           writes=["QArow%d%d" % (pp, hh)])
                            return [A]

                        ulist = [drow_unit()]
                        for i0 in range(0, NT, 4):
                            ulist.append(v_unit(i0))
                        for tt in range(NTT):
                            for qk in range(2):
                                ulist.append(qk_unit(tt, qk))
                        for tt in range(NTT):
                            ulist.append(gate_unit(tt))
                        flat = []
                        nu = len(ulist)
                        mx = max(len(u_) for u_ in ulist)
                        for slot_i in range(nu + mx - 1):
                            for k in range(mx - 1, -1, -1):
                                ui_ = slot_i - k
                                if 0 <= ui_ < nu and k < len(ulist[ui_]):
                                    flat.append(ulist[ui_][k])
                        return flat

                    def attention(p, next_units):
                        pp = p % 2
                        steps = []
                        for hh in range(2):
                            for T in range(NTT):
                                for J in range(4 * T + 4):
                                    steps.append((hh, T, J))

                        def emit_S(si):
                            hh, T, J = steps[si]
                            j = J - 4 * T
                            col0 = 128 * j if j >= 0 else 0
                            sbk = (2, 6, 3)[si % 3]
                            sc.op("pe", lambda: nc.tensor.matmul(
                                bank(sbk)[:, col0:512], lhsT=KA[pp][hh][:, J * 128:(J + 1) * 128],
                                rhs=QA[pp][hh][:, T * 512 + col0:(T + 1) * 512], start=True, stop=(j < 0)),
                                reads=["KA%d%d_%d" % (pp, hh, J // 4), "KArow%d%d" % (pp, hh), "QA%d%d_%d" % (pp, hh, T),
                                       "QArow%d%d" % (pp, hh)],
                                writes=[BN[sbk]], inc=(j < 0))
                            if j >= 0:
                                sc.op("pe", lambda: nc.tensor.matmul(
                                    bank(sbk)[:, col0:col0 + 128], lhsT=ident[:], rhs=maskT[:], start=False, stop=True),
                                    reads=["ident", "maskT"], writes=[BN[sbk]])

                        def emit_exp(si):
                            hh, T, J = steps[si]
                            h = 2 * p + hh
                            j = J - 4 * T
                            col0 = 128 * j if j >= 0 else 0
                            sbk = (2, 6, 3)[si % 3]
                            pi = si % 4
                            sc.op("act", lambda: nc.scalar.activation(
                                out=pbuf[pi][:, col0:512], in_=bank(sbk)[:, col0:512], func=AF.Exp,
                                bias=bias_all[:, h, T, J:J + 1], scale=1.0),
                                reads=[BN[sbk], "bias_all"], writes=["pbuf%d" % pi])

                        def emit_PV(si):
                            hh, T, J = steps[si]
                            j = J - 4 * T
                            col0 = 128 * j if j >= 0 else 0
                            pi = si % 4
                            acc_n = (hh * NTT + T) % 3
                            ab = (4, 5, 0)[acc_n]
                            last = (J == 4 * T + 3)
                            sc.op("pe", lambda: nc.tensor.matmul(
                                bank(ab)[:, col0:512], lhsT=Vaug[pp][:, J, hh, :], rhs=pbuf[pi][:, col0:512],
                                start=(J == 0), stop=last),
                                reads=["pbuf%d" % pi, "V%d%d_%d" % (pp, hh, J // 4), "Vones%d%d" % (pp, hh)], writes=[BN[ab]],
                                inc=last)
                            if last:
                                num0, den0 = (0, 64) if hh == 0 else (64, 0)
                                ri = acc_n
                                sc.op("dve", lambda: nc.vector.reciprocal(out=rec[ri][num0:num0 + 64, :],
                                                                          in_=bank(ab)[den0:den0 + 64, :]),
                                      reads=[BN[ab]], writes=["rec%d" % ri])
                                sc.op("pool", lambda: nc.gpsimd.tensor_tensor(
                                    out=rec[ri][num0:num0 + 64, :], in0=rec[ri][num0:num0 + 64, :],
                                    in1=gate[pp][num0:num0 + 64, T * 512:(T + 1) * 512], op=ALU.mult),
                                    reads=["rec%d" % ri, "gate%d_%d" % (pp, T)], writes=["rec%d" % ri])
                                sc.op("dve", lambda: nc.vector.scalar_tensor_tensor(
                                    out=yT[num0:num0 + 64, p, T * 512:(T + 1) * 512], in0=bank(ab)[num0:num0 + 64, :],
                                    scalar=0.5, in1=rec[ri][num0:num0 + 64, :], op0=ALU.mult, op1=ALU.mult),
                                    reads=[BN[ab], "rec%d" % ri], writes=["yT%d_%d_%d" % (p, hh, T)])

                        nst = len(steps)
                        nun = len(next_units)
                        stride = max(1, nst // (nun + 1)) if nun else nst
                        ui = 0
                        emit_S(0)
                        if nst > 1:
                            emit_S(1)
                        for si in range(nst):
                            emit_exp(si)
                            if si + 2 < nst:
                                emit_S(si + 2)
                            emit_PV(si)
                            if INTERLEAVE and ui < nun and (si + 1) % stride == 0:
                                next_units[ui]()
                                ui += 1
                        while ui < nun:
                            next_units[ui]()
                            ui += 1

                    wslots[0], wslots[1] = pre["pairs"]
                    for un in proj_units(0):
                        un()
                    for p in range(8):
                        if p + 2 < 8:
                            load_pair(p + 2)
                        if p == 7:
                            pre["glu"] = [load_wgroup([O_GLU + 128 * c_, O_GLU + 1024 + 128 * c_]) for c_ in range(2)]
                        attention(p, proj_units(p + 1) if p + 1 < 8 else [])

                ph23.close()
                with ExitStack() as ph:
                    if STOP < 4:
                        raise _Stop()
                    sc.barrier()
                    wout = ph.enter_context(nc.sbuf_tensor("wout%d" % b, [128, 16, D], BF16))
                    for half in range(2):
                        sc.dma("pool", "wo%d" % half, wout[:, half * 8:(half + 1) * 8, :], w_out_v[:, half * 8:(half + 1) * 8, :],
                               writes=["wout%d" % half])
                    with ExitStack() as ph4:
                        with ExitStack() as ph4a:
                            uT = [ph4a.enter_context(nc.sbuf_tensor("uT%d_%d" % (b, i), [128, S + 32], BF16)) for i in range(2)]
                            Wd = [ph4a.enter_context(nc.sbuf_tensor("Wd%d_%d" % (b, i), [128, KC, 128], BF16)) for i in range(2)]
                            tb = [ph4a.enter_context(nc.sbuf_tensor("tb%d_%d" % (b, i), [128, 512], F32)) for i in range(2)]
                            cacc = [ph4a.enter_context(nc.sbuf_tensor("cacc%d_%d" % (b, i), [128, 512], F32)) for i in range(2)]
                            for i in range(2):
                                sc.op("pool", lambda i=i: nc.gpsimd.memset(uT[i][:, 0:32], 0.0), writes=["uTpad%d" % i])
                            def build_wd(ch):
                                us = ch % 2
                                sc.op("pool", lambda: nc.gpsimd.tensor_tensor(
                                    out=Wd[us][:],
                                    in0=ident[:].rearrange("p (o n) -> p o n", o=1).broadcast_to([128, KC, 128]),
                                    in1=convwh[:, ch, :].rearrange("p (k o) -> p k o", o=1).broadcast_to([128, KC, 128]),
                                    op=ALU.mult), reads=["ident", "convwh"], writes=["Wd%d" % us])

                            def glu_pe(ch, tts, slot_):
                                for tt in tts:
                                    ba = tt % 2
                                    bb = 6 + (tt % 2)
                                    u = tt % 2
                                    proj_fm(slot_, 0, tt, ba)
                                    proj_fm(slot_, 1, tt, bb)
                                    sc.op("act", lambda: nc.scalar.activation(out=tb[u][:], in_=bank(bb), func=AF.Tanh, scale=0.5),
                                          reads=[BN[bb]], writes=["tb%d" % u])

                            def glu_dve(ch, tts):
                                us = ch % 2
                                for tt in tts:
                                    ba = tt % 2
                                    u = tt % 2
                                    sc.op("dve", lambda: nc.vector.scalar_tensor_tensor(
                                        out=uT[us][:, 30 + tt * 512:30 + (tt + 1) * 512], in0=tb[u][:], scalar=1.0,
                                        in1=bank(ba), op0=ALU.add, op1=ALU.mult),
                                        reads=[BN[ba], "tb%d" % u, "uTpad%d" % us], writes=["uT%d_%d" % (us, tt)])

                            def glu_proj(ch, tts, slot_):
                                for tt in tts:
                                    glu_pe(ch, [tt], slot_)
                                    glu_dve(ch, [tt])

                            def taps(ch, tts, rds, k0, k1):
                                us = ch % 2
                                for k in range(k0, k1):
                                    for tt in tts:
                                        ca = cacc[tt % 2]
                                        can = "cacc%d" % (tt % 2)
                                        if k == NPE:
                                            sc.op("dve", lambda: nc.vector.tensor_scalar(
                                                out=ca[:], in0=uT[us][:, tt * 512 + k:tt * 512 + k + 512],
                                                scalar1=convwh[:, ch, k:k + 1], scalar2=None, op0=ALU.mult),
                                                reads=rds[tt] + ["convwh"], writes=[can])
                                        else:
                                            sc.op("dve", lambda: nc.vector.scalar_tensor_tensor(
                                                out=ca[:], in0=uT[us][:, tt * 512 + k:tt * 512 + k + 512],
                                                scalar=convwh[:, ch, k:k + 1], in1=ca[:], op0=ALU.mult, op1=ALU.add),
                                                reads=rds[tt] + ["convwh", can], writes=[can])

                            gslot = {0: pre["glu"][0], 1: pre["glu"][1]}
                            build_wd(0)
                            glu_proj(0, list(range(NTT)), gslot[0])
                            KMID = (NPE + KC) // 2
                            for ch in range(8):
                                us = ch % 2
                                if ch == 7:
                                    pre["gc"] = [load_wgroup([O_GC + 512 * gi], width=512) for gi in range(2)]
                                if ch + 2 < 8:
                                    gslot[ch + 2] = load_wgroup([O_GLU + 128 * (ch + 2), O_GLU + 1024 + 128 * (ch + 2)])
                                if ch + 1 < 8:
                                    build_wd(ch + 1)
                                for tp in range(0, NTT, 2):
                                    tts = [t for t in (tp, tp + 1) if t < NTT]
                                    rds = {}
                                    if ch + 1 < 8:
                                        glu_pe(ch + 1, tts, gslot[ch + 1])
                                    for tt in tts:
                                        cb = 2 + (tt % 2)
                                        rd = ["Wd%d" % us, "uTpad%d" % us, "uT%d_%d" % (us, tt)]
                                        if tt > 0:
                                            rd.append("uT%d_%d" % (us, tt - 1))
                                        rds[tt] = rd
                                        for k in range(NPE):
                                            sc.op("pe", lambda k=k: nc.tensor.matmul(
                                                bank(cb), lhsT=Wd[us][:, k, :], rhs=uT[us][:, tt * 512 + k:tt * 512 + k + 512],
                                                start=(k == 0), stop=(k == NPE - 1)),
                                                reads=rd, writes=[BN[cb]], inc=(k == NPE - 1))
                                    taps(ch, tts, rds, NPE, KMID)
                                    if ch + 1 < 8:
                                        glu_dve(ch + 1, tts)
                                    taps(ch, tts, rds, KMID, KC)
                                    for tt in tts:
                                        cb = 2 + (tt % 2)
                                        ca = cacc[tt % 2]
                                        can = "cacc%d" % (tt % 2)
                                        sc.op("dve", lambda: nc.vector.scalar_tensor_tensor(
                                            out=yT[:, 8 + ch, tt * 512:(tt + 1) * 512], in0=bank(cb),
                                            scalar=convb[:, ch:ch + 1], in1=ca[:], op0=ALU.add, op1=ALU.add),
                                            reads=[BN[cb], "convb", can], writes=["yc%d_%d" % (ch, tt)])
                        sc.barrier()
                        sq2 = [ph4.enter_context(nc.sbuf_tensor("sqc%d_%d" % (b, i), [128, 512], BF16)) for i in range(2)]
                        mean_sb = ph4.enter_context(nc.sbuf_tensor("mean%d" % b, [128, S], F32))
                        rstd_sb = ph4.enter_context(nc.sbuf_tensor("rstdc%d" % b, [128, S], F32))
                        t1 = [ph4.enter_context(nc.sbuf_tensor("t1_%d_%d" % (b, i), [128, 512], F32)) for i in range(NSET)]
                        th2 = [ph4.enter_context(nc.sbuf_tensor("th2_%d_%d" % (b, i), [128, 512], F32)) for i in range(2)]
                        vq = [ph4.enter_context(nc.sbuf_tensor("vq_%d_%d" % (b, i), [128, 512], F32)) for i in range(NSET)]
                        thc = [ph4.enter_context(nc.sbuf_tensor("thc_%d_%d" % (b, i), [128, 512], F32)) for i in range(NSET)]
                        gcv = [ph4.enter_context(nc.sbuf_tensor("gcv_%d_%d" % (b, i), [128, 512], F32)) for i in range(NSET)]
                        m2 = t1[0]
                        vare = t1[1]
                        for tt in range(NTT):
                            for ch in range(8):
                                sc.op("pe", lambda ch=ch: nc.tensor.matmul(
                                    bank(4), lhsT=onesmean[:], rhs=yT[:, 8 + ch, tt * 512:(tt + 1) * 512],
                                    start=(ch == 0), stop=(ch == 7)),
                                    reads=["onesmean", "yc%d_%d" % (ch, tt)], writes=[BN[4]], inc=(ch == 7))
                            for ch in range(8):
                                qi = ch % 2
                                sc.op("act", lambda ch=ch, qi=qi: nc.scalar.activation(
                                    out=sq2[qi][:], in_=yT[:, 8 + ch, tt * 512:(tt + 1) * 512], func=AF.Square),
                                    reads=["yc%d_%d" % (ch, tt)], writes=["sqc%d" % qi])
                                sc.op("pe", lambda ch=ch, qi=qi: nc.tensor.matmul(
                                    bank(5), lhsT=onesmean[:], rhs=sq2[qi][:], start=(ch == 0), stop=(ch == 7)),
                                    reads=["onesmean", "sqc%d" % qi], writes=[BN[5]], inc=True)
                            sc.op("act", lambda: nc.scalar.copy(out=mean_sb[:, tt * 512:(tt + 1) * 512], in_=bank(4)),
                                  reads=[BN[4]], writes=["mean%d" % tt])
                            sc.op("dve", lambda: nc.vector.tensor_tensor(
                                out=m2[:], in0=mean_sb[:, tt * 512:(tt + 1) * 512],
                                in1=mean_sb[:, tt * 512:(tt + 1) * 512], op=ALU.mult),
                                reads=["mean%d" % tt], writes=["t1_0"])
                            sc.op("dve", lambda: nc.vector.scalar_tensor_tensor(
                                out=vare[:], in0=bank(5), scalar=EPS, in1=m2[:], op0=ALU.add, op1=ALU.subtract),
                                reads=[BN[5], "t1_0"], writes=["t1_1"])
                            sc.op("act", lambda: nc.scalar.activation(out=vare[:], in_=vare[:], func=AF.Ln),
                                  reads=["t1_1"], writes=["t1_1"])
                            sc.op("act", lambda: nc.scalar.activation(
                                out=rstd_sb[:, tt * 512:(tt + 1) * 512], in_=vare[:], func=AF.Exp, scale=-0.5),
                                reads=["t1_1"], writes=["rstdc%d" % tt])
                        gslots = pre["gc"]
                        blocks = [(gi, c4, tt) for tt in range(NTT) for gi in range(2) for c4 in range(4)]
                        xo = [ph4.enter_context(nc.sbuf_tensor("xo%d_%d" % (b, i), [128, D], F32)) for i in range(2)]
                        ot = [ph4.enter_context(nc.sbuf_tensor("ot%d_%d" % (b, i), [128, D], F32)) for i in range(1)]

                        def og(i, half):
                            s_ = i % 2
                            T = i // 4
                            ob = 4 + half
                            if half == 0:
                                sc.dma("sp", "xm%d" % s_, xo[s_][:], x[b, i * 128:(i + 1) * 128, :], writes=["xo%d" % s_])
                            for e in range(16):
                                if e < 8:
                                    rd = ["yT%d_0_%d" % (e, T), "yT%d_1_%d" % (e, T)]
                                else:
                                    rd = ["yc%d_%d" % (e - 8, T)]
                                sc.op("pe", lambda: nc.tensor.matmul(
                                    bank(ob), lhsT=yT[:, e, i * 128:(i + 1) * 128],
                                    rhs=wout[:, e, half * 512:(half + 1) * 512], start=(e == 0), stop=(e == 15)),
                                    reads=rd + ["wout%d" % (e // 8)], writes=[BN[ob]], inc=(e == 15))
                            sc.op("dve", lambda: nc.vector.tensor_tensor(
                                out=ot[0][:, half * 512:(half + 1) * 512], in0=bank(ob),
                                in1=xo[s_][:, half * 512:(half + 1) * 512], op=ALU.add),
                                reads=[BN[ob], "xo%d" % s_], writes=["ot_%d" % half])
                            if half == 1:
                                sc.dma("sp", "ost", out[b, i * 128:(i + 1) * 128, :], ot[0][:], reads=["ot_0", "ot_1"])

                        ogs = [(i, half) for i in range(NT) for half in range(2)]

                        LB = (0, 1, 2, 3, 6, 7)

                        def ln_stages(n):
                            gi, c4, tt = blocks[n]
                            ch = gi * 4 + c4
                            u = n % NSET
                            pbk = LB[n % 6]
                            ysl = yT[:, 8 + ch, tt * 512:(tt + 1) * 512]

                            def s0():
                                proj_fm(gslots[gi], c4, tt, pbk)

                            def s1():
                                sc.op("act", lambda: nc.scalar.activation(out=thc[u][:], in_=bank(pbk), func=AF.Tanh, scale=0.5),
                                      reads=[BN[pbk]], writes=["thc%d" % u])
                                sc.op("dve", lambda: nc.vector.tensor_tensor(
                                    out=t1[u][:], in0=ysl, in1=mean_sb[:, tt * 512:(tt + 1) * 512], op=ALU.subtract),
                                    reads=["yc%d_%d" % (ch, tt), "mean%d" % tt], writes=["t1_%d" % u])

                            def s2():
                                sc.op("pool", lambda: nc.gpsimd.tensor_tensor(
                                    out=t1[u][:], in0=t1[u][:], in1=rstd_sb[:, tt * 512:(tt + 1) * 512], op=ALU.mult),
                                    reads=["t1_%d" % u, "rstdc%d" % tt], writes=["t1_%d" % u])

                            def s3():
                                sc.op("act", lambda: nc.scalar.activation(
                                    out=th2[n % 2][:], in_=t1[u][:], func=AF.Tanh, bias=lnb_h[:, ch:ch + 1],
                                    scale=lng_h[:, ch:ch + 1]),
                                    reads=["t1_%d" % u, "lng_h", "lnb_h"], writes=["th2_%d" % (n % 2)])
                                sc.op("dve", lambda: nc.vector.scalar_tensor_tensor(
                                    out=gcv[u][:], in0=thc[u][:], scalar=1.0, in1=bank(pbk), op0=ALU.add,
                                    op1=ALU.mult), reads=[BN[pbk], "thc%d" % u], writes=["gcv%d" % u])
                                sc.op("act", lambda: nc.scalar.activation(
                                    out=vq[u][:], in_=t1[u][:], func=AF.Identity, bias=lnb_q[:, ch:ch + 1],
                                    scale=lng_q[:, ch:ch + 1]),
                                    reads=["t1_%d" % u, "lng_q", "lnb_q"], writes=["vq%d" % u])

                            def s4():
                                sc.op("dve", lambda: nc.vector.scalar_tensor_tensor(
                                    out=vq[u][:], in0=th2[n % 2][:], scalar=1.0, in1=vq[u][:], op0=ALU.add, op1=ALU.mult),
                                    reads=["th2_%d" % (n % 2), "vq%d" % u], writes=["vq%d" % u])

                            def s5():
                                if n % 2 == 0:
                                    sc.op("pool", lambda: nc.gpsimd.tensor_tensor(
                                        out=ysl, in0=vq[u][:], in1=gcv[u][:], op=ALU.mult),
                                        reads=["vq%d" % u, "gcv%d" % u], writes=["yc%d_%d" % (ch, tt)])
                                else:
                                    sc.op("dve", lambda: nc.vector.tensor_tensor(
                                        out=ysl, in0=vq[u][:], in1=gcv[u][:], op=ALU.mult),
                                        reads=["vq%d" % u, "gcv%d" % u], writes=["yc%d_%d" % (ch, tt)])
                            return [s0, s1, s2, s3, s4, s5]

                        lnu = [ln_stages(n) for n in range(len(blocks))]
                        oq = 0
                        for slot_i in range(len(lnu) + 5):
                            for k in range(5, -1, -1):
                                ui_ = slot_i - k
                                if 0 <= ui_ < len(lnu):
                                    lnu[ui_][k]()
                            if slot_i >= 12 and oq < len(ogs) and oq <= slot_i - 12:
                                og(*ogs[oq])
                                oq += 1
                        while oq < len(ogs):
                            og(*ogs[oq])
                            oq += 1
          except _Stop:
            stopped = True
            break
        sc.wait_all("sp")
        if stopped:
            es.pop_all()
    return nc


def _prep_shared(inp):
    f = np.float32
    qg = np.asarray(inp["q_norm_g"], f)[0]
    kg = np.asarray(inp["k_norm_g"], f)[0]
    gq = np.ascontiguousarray(qg.reshape(8, 128).T)
    gk = np.ascontiguousarray(kg.reshape(8, 128).T)
    cw = np.asarray(inp["conv_w"], f)[0]
    convw = np.ascontiguousarray(cw.T.reshape(8, 128, KC).transpose(1, 0, 2))

    def pc(a):
        return np.ascontiguousarray(np.asarray(a, f)[0].reshape(8, 128).T)

    return {
        "w_in": np.ascontiguousarray(np.asarray(inp["w_in"], f)[0]),
        "w_out": np.ascontiguousarray(np.asarray(inp["w_out"], f)[0]),
        "norm_g": np.ascontiguousarray(np.asarray(inp["norm_g"], f)),
        "b_forget": np.ascontiguousarray(np.asarray(inp["b_forget"], f)),
        "gq": gq, "gk": gk, "convw": convw,
        "convb": pc(inp["conv_b"]), "lng": pc(inp["conv_ln_g"]), "lnb": pc(inp["conv_ln_b"]),
    }


def kernel(**inputs):
    x = np.asarray(inputs["x"], np.float32)
    B, S, _ = x.shape
    n = N_CORES
    nb = B // n
    shared = _prep_shared(inputs)
    nc = build_nc(NB=nb, S=S)
    in_maps = []
    for c in range(n):
        m = dict(shared)
        m["x"] = np.ascontiguousarray(x[c * nb:(c + 1) * nb])
        in_maps.append(m)
    res = run_bass_kernel_spmd(nc, in_maps, core_ids=list(range(n)))
    return np.concatenate([np.asarray(r["out"]) for r in res.results], axis=0).astype(np.float32)
```

```python
import numpy as np
from contextlib import ExitStack
import concourse.bass as bass
import concourse.mybir as mybir
from concourse.bass_utils import run_bass_kernel_spmd

F32 = mybir.dt.float32
BF16 = mybir.dt.bfloat16
AF = mybir.ActivationFunctionType
ALU = mybir.AluOpType

D = 1024
NH = 16
HD = 64
KC = 31
IN_COLS = 7184
O_Q, O_K, O_V, O_F, O_GF, O_GLU, O_GC = 0, 1024, 2048, 3072, 3088, 4112, 6160
EPS = 1e-6
MASKV = -30000.0
N_CORES = 8
INTERLEAVE = False
PJB = (1, 7, 3, 0, 5)
NSET = 3
NPE = 23


class Sched:
    def __init__(self, nc, es):
        self.nc = nc
        self.eng = {"pe": nc.tensor, "act": nc.scalar, "dve": nc.vector, "pool": nc.gpsimd, "sp": nc.sync}
        self.sem = {k: es.enter_context(nc.semaphore("s_" + k)) for k in self.eng}
        self.cnt = {k: 0 for k in self.eng}
        self.seen = {k: {} for k in self.eng}
        self.dsem = {}
        self.dcnt = {}
        self.es = es
        self.res = {}
        self.pending_noinc = {k: False for k in self.eng}

    def _semh(self, key):
        return self.sem[key] if key in self.sem else self.dsem[key]

    def _wait(self, engine, tok):
        key, val = tok
        if self.seen[engine].get(key, 0) >= val:
            return
        self.seen[engine][key] = val
        self.eng[engine].wait_ge(self._semh(key), val)

    def _deps(self, engine, reads, writes):
        for r in reads:
            st = self.res.get(r)
            if st is not None and st["w"] is not None:
                if not (engine == "pe" and st["w"][0] == "pe"):
                    self._wait(engine, st["w"])
        for w in writes:
            st = self.res.get(w)
            if st is None:
                continue
            if st["w"] is not None and not (engine == "pe" and st["w"][0] == "pe"):
                self._wait(engine, st["w"])
            for k, v in st["r"].items():
                if k == engine and engine == "pe":
                    continue
                self._wait(engine, (k, v))

    def _record(self, tok, reads, writes):
        for r in reads:
            st = self.res.setdefault(r, {"w": None, "r": {}})
            if st["r"].get(tok[0], 0) < tok[1]:
                st["r"][tok[0]] = tok[1]
        for w in writes:
            self.res[w] = {"w": tok, "r": {}}

    def op(self, engine, fn, reads=(), writes=(), inc=True):
        self._deps(engine, reads, writes)
        ins = fn()
        if inc:
            self.cnt[engine] += 1
            ins.then_inc(self.sem[engine], 1)
            tok = (engine, self.cnt[engine])
            self.pending_noinc[engine] = False
        else:
            tok = (engine, self.cnt[engine] + 1)
            self.pending_noinc[engine] = True
        self._record(tok, reads, writes)
        return tok

    def dma(self, queue, semname, out, in_, reads=(), writes=()):
        if semname not in self.dsem:
            self.dsem[semname] = self.es.enter_context(self.nc.semaphore("d_" + semname))
            self.dcnt[semname] = 0
        self._deps(queue, reads, writes)
        ins = self.eng[queue].dma_start(out=out, in_=in_)
        self.dcnt[semname] += 16
        ins.then_inc(self.dsem[semname], 16)
        tok = (semname, self.dcnt[semname])
        self._record(tok, reads, writes)
        return tok

    def retoken(self, names, tok):
        for n in names:
            self.res[n] = {"w": tok, "r": {}}

    def barrier(self):
        assert not self.pending_noinc["pe"]
        for e in self.eng:
            self.wait_all(e)

    def wait_all(self, engine):
        for k in self.eng:
            if k != engine and self.cnt[k] > 0:
                self._wait(engine, (k, self.cnt[k]))
        for k, v in self.dcnt.items():
            if v > 0:
                self._wait(engine, (k, v))


class _Stop(Exception):
    pass


def build_nc(NB=2, S=2048, STOP=9):
    NT = S // 128
    NTT = S // 512
    nc = bass.Bass("TRN2", target_bir_lowering=False)
    x = nc.dram_tensor("x", [NB, S, D], F32, kind="ExternalInput").ap()
    w_in = nc.dram_tensor("w_in", [D, IN_COLS], F32, kind="ExternalInput").ap()
    w_out = nc.dram_tensor("w_out", [2 * D, D], F32, kind="ExternalInput").ap()
    norm_g = nc.dram_tensor("norm_g", [1, D], F32, kind="ExternalInput").ap()
    b_forget = nc.dram_tensor("b_forget", [1, NH], F32, kind="ExternalInput").ap()
    gq_d = nc.dram_tensor("gq", [128, 8], F32, kind="ExternalInput").ap()
    gk_d = nc.dram_tensor("gk", [128, 8], F32, kind="ExternalInput").ap()
    convw_d = nc.dram_tensor("convw", [128, 8, KC], F32, kind="ExternalInput").ap()
    convb_d = nc.dram_tensor("convb", [128, 8], F32, kind="ExternalInput").ap()
    lng_d = nc.dram_tensor("lng", [128, 8], F32, kind="ExternalInput").ap()
    lnb_d = nc.dram_tensor("lnb", [128, 8], F32, kind="ExternalInput").ap()
    out = nc.dram_tensor("out", [NB, S, D], F32, kind="ExternalOutput").ap()

    w_in_v = w_in.rearrange("(c p) n -> p c n", p=128)
    w_out_v = w_out.rearrange("(c p) n -> p c n", p=128)

    with ExitStack() as es:
        sc = Sched(nc, es)

        def sb(name, shape, dt):
            return es.enter_context(nc.sbuf_tensor(name, shape, dt))

        ident = sb("ident", [128, 128], BF16)
        maskT = sb("maskT", [128, 128], BF16)
        blockones = sb("blockones", [128, 128], BF16)
        onesmean = sb("onesmean", [128, 128], BF16)
        Uneg = sb("Uneg", [128, 128], F32)
        OnesNeg = sb("OnesNeg", [128, 128], F32)
        Sel = sb("Sel", [128, 128], F32)
        bf_tile = sb("bf_tile", [128, NH], F32)
        gq = sb("gq_sb", [128, 8], F32)
        gk = sb("gk_sb", [128, 8], F32)
        gq8 = sb("gq8", [128, 8], F32)
        convw = sb("convw_sb", [128, 8, KC], F32)
        convwh = sb("convwh", [128, 8, KC], F32)
        convb = sb("convb_sb", [128, 8], F32)
        lng = sb("lng_sb", [128, 8], F32)
        lnb = sb("lnb_sb", [128, 8], F32)
        lng_h = sb("lng_h", [128, 8], F32)
        lnb_h = sb("lnb_h", [128, 8], F32)
        lng_q = sb("lng_q", [128, 8], F32)
        lnb_q = sb("lnb_q", [128, 8], F32)
        wf = sb("wf", [128, 8, NH], BF16)
        hT = sb("hT", [128, 8, S], BF16)
        yT = sb("yT", [128, 16, S], BF16)
        wbuf = [sb("wbuf%d" % i, [128, 8, 512], BF16) for i in range(2)]

        PS = [es.enter_context(nc.psum_tensor("ps%d" % i, [128, 1024], F32)) for i in range(4)]

        def bank(b):
            return PS[b // 2][:, (b % 2) * 512:(b % 2) * 512 + 512]

        def bank_bf(b):
            t = PS[b // 2].bitcast(BF16)
            return t[:, (b % 2) * 1024:(b % 2) * 1024 + 1024]

        BN = ["B%d" % i for i in range(8)]

        g = nc.gpsimd
        sc.op("pool", lambda: g.memset(ident[:], 1.0), writes=["ident"])
        sc.op("pool", lambda: g.affine_select(out=ident[:], in_=ident[:], compare_op=ALU.is_equal, fill=0.0,
                                              base=0, pattern=[[1, 128]], channel_multiplier=-1),
              reads=["ident"], writes=["ident"])
        sc.op("pool", lambda: g.memset(maskT[:], 0.0), writes=["maskT"])
        sc.op("pool", lambda: g.affine_select(out=maskT[:], in_=maskT[:], compare_op=ALU.is_ge, fill=MASKV,
                                              base=0, pattern=[[1, 128]], channel_multiplier=-1),
              reads=["maskT"], writes=["maskT"])
        sc.op("pool", lambda: g.memset(blockones[:], 0.0), writes=["blockones"])
        sc.op("pool", lambda: g.memset(blockones[0:64, 0:64], 1.0 / 64), reads=["blockones"], writes=["blockones"])
        sc.op("pool", lambda: g.memset(blockones[64:128, 64:128], 1.0 / 64), reads=["blockones"], writes=["blockones"])
        sc.op("pool", lambda: g.memset(onesmean[:], 1.0 / 1024), writes=["onesmean"])
        sc.op("pool", lambda: g.memset(Uneg[:], -1.0), writes=["Uneg"])
        sc.op("pool", lambda: g.affine_select(out=Uneg[:], in_=Uneg[:], compare_op=ALU.is_ge, fill=0.0,
                                              base=0, pattern=[[1, 128]], channel_multiplier=-1),
              reads=["Uneg"], writes=["Uneg"])
        sc.op("pool", lambda: g.memset(OnesNeg[:], -1.0), writes=["OnesNeg"])
        sc.op("pool", lambda: g.memset(Sel[:], 1.0), writes=["Sel"])
        sc.op("pool", lambda: g.affine_select(out=Sel[:], in_=Sel[:], compare_op=ALU.is_equal, fill=0.0,
                                              base=-127, pattern=[[0, 128]], channel_multiplier=1),
              reads=["Sel"], writes=["Sel"])

        sc.dma("sp", "par", bf_tile[:], b_forget.partition_broadcast(128).rearrange("p o n -> p (o n)"), writes=["bf_tile"])
        sc.dma("sp", "par", gq[:], gq_d, writes=["gq"])
        sc.dma("sp", "par", gk[:], gk_d, writes=["gk"])
        sc.dma("sp", "par", convw[:], convw_d, writes=["convw"])
        sc.dma("sp", "par", convb[:], convb_d, writes=["convb"])
        sc.dma("sp", "par", lng[:], lng_d, writes=["lng"])
        sc.dma("sp", "par", lnb[:], lnb_d, writes=["lnb"])
        sc.retoken(["bf_tile", "gq", "gk", "convw", "convb", "lng", "lnb"], ("par", sc.dcnt["par"]))
        sc.dma("pool", "wfl", wf[:], w_in_v[:, :, O_F:O_F + NH], writes=["wf"])
        v = nc.vector
        sc.op("dve", lambda: v.tensor_scalar(out=gq8[:], in0=gq[:], scalar1=0.125, scalar2=None, op0=ALU.mult),
              reads=["gq"], writes=["gq8"])
        sc.op("dve", lambda: v.tensor_scalar(out=convwh[:], in0=convw[:], scalar1=0.5, scalar2=None, op0=ALU.mult),
              reads=["convw"], writes=["convwh"])
        sc.op("dve", lambda: v.tensor_scalar(out=lng_h[:], in0=lng[:], scalar1=0.5, scalar2=None, op0=ALU.mult),
              reads=["lng"], writes=["lng_h"])
        sc.op("dve", lambda: v.tensor_scalar(out=lnb_h[:], in0=lnb[:], scalar1=0.5, scalar2=None, op0=ALU.mult),
              reads=["lnb"], writes=["lnb_h"])
        sc.op("dve", lambda: v.tensor_scalar(out=lng_q[:], in0=lng[:], scalar1=0.25, scalar2=None, op0=ALU.mult),
              reads=["lng"], writes=["lng_q"])
        sc.op("dve", lambda: v.tensor_scalar(out=lnb_q[:], in0=lnb[:], scalar1=0.25, scalar2=None, op0=ALU.mult),
              reads=["lnb"], writes=["lnb_q"])

        wstate = {"n": 0}

        def load_wgroup(cols, width=128):
            slot = wstate["n"] % 2
            wstate["n"] += 1
            names = []
            tok = None
            for j, c0 in enumerate(cols):
                nm = ["wbuf%d_%d" % (slot, jj) for jj in range(j * width // 128, (j + 1) * width // 128)]
                tok = sc.dma("pool", "w%d" % slot, wbuf[slot][:, :, j * width:(j + 1) * width],
                             w_in_v[:, :, c0:c0 + width], writes=nm)
                names += nm
            sc.retoken(names, tok)
            return slot

        def proj_fm(slot, blk, tt, b_out):
            for c in range(8):
                sc.op("pe", lambda c=c: nc.tensor.matmul(bank(b_out), lhsT=wbuf[slot][:, c, blk * 128:(blk + 1) * 128],
                                                         rhs=hT[:, c, tt * 512:(tt + 1) * 512],
                                                         start=(c == 0), stop=(c == 7)),
                      reads=["wbuf%d_%d" % (slot, blk)] + ["hT%d" % i for i in range(4 * tt, 4 * tt + 4)],
                      writes=[BN[b_out]], inc=(c == 7))

        stopped = False
        for b in range(NB):
          try:
                with ExitStack() as ph:
                    if STOP < 1:
                        raise _Stop()
                    sc.barrier()
                    NXS = 4
                    g_tile = ph.enter_context(nc.sbuf_tensor("g_tile%d" % b, [128, D], F32))
                    sc.dma("sp", "gt", g_tile[:], norm_g.partition_broadcast(128).rearrange("p o n -> p (o n)"), writes=["g_tile"])
                    xt = [ph.enter_context(nc.sbuf_tensor("xt%d_%d" % (b, i), [128, D], F32)) for i in range(NXS)]
                    hb = [ph.enter_context(nc.sbuf_tensor("hb%d_%d" % (b, i), [128, D], BF16)) for i in range(3)]
                    junk = ph.enter_context(nc.sbuf_tensor("junk%d" % b, [128, D], BF16))
                    ssq = ph.enter_context(nc.sbuf_tensor("ssq%d" % b, [128, NT], F32))
                    msx = ph.enter_context(nc.sbuf_tensor("msx%d" % b, [128, NT], F32))
                    rsx = ph.enter_context(nc.sbuf_tensor("rsx%d" % b, [128, NT], F32))

                    def p1_stages(i):
                        s_ = i % NXS
                        hs = i % 3
                        pb = i % 2
                        pbf = bank_bf(pb)

                        def s0():
                            sc.dma("sp", "xl%d" % s_, xt[s_][:], x[b, i * 128:(i + 1) * 128, :], writes=["xt%d" % s_])

                        def s1():
                            sc.op("act", lambda: nc.scalar.activation(out=junk[:], in_=xt[s_][:], func=AF.Square,
                                                                      accum_out=ssq[:, i:i + 1]),
                                  reads=["xt%d" % s_], writes=["junk", "ssq%d" % i])
                            sc.op("act", lambda: nc.scalar.activation(out=msx[:, i:i + 1], in_=ssq[:, i:i + 1],
                                                                      func=AF.Ln, bias=EPS, scale=1.0 / D),
                                  reads=["ssq%d" % i], writes=["msx%d" % i])
                            sc.op("act", lambda: nc.scalar.activation(out=rsx[:, i:i + 1], in_=msx[:, i:i + 1],
                                                                      func=AF.Exp, scale=-0.5),
                                  reads=["msx%d" % i], writes=["rsx%d" % i])

                        def s2():
                            sc.op("dve", lambda: nc.vector.scalar_tensor_tensor(out=hb[hs][:], in0=xt[s_][:],
                                                                                scalar=rsx[:, i:i + 1], in1=g_tile[:],
                                                                                op0=ALU.mult, op1=ALU.mult),
                                  reads=["xt%d" % s_, "rsx%d" % i, "g_tile"], writes=["hb%d" % hs])

                        def s3():
                            for c in range(8):
                                sc.op("pe", lambda: nc.tensor.transpose(pbf[:, c * 128:(c + 1) * 128],
                                                                        hb[hs][:, c * 128:(c + 1) * 128], ident[:]),
                                      reads=["hb%d" % hs, "ident"], writes=[BN[pb]], inc=(c == 7))

                        def s4():
                            sc.op("dve", lambda: nc.vector.tensor_copy(out=hT[:, :, i * 128:(i + 1) * 128],
                                                                       in_=pbf.rearrange("p (c t) -> p c t", c=8)),
                                  reads=[BN[pb]], writes=["hT%d" % i])
                        return [s0, s1, s2, s3, s4]

                    p1u = [p1_stages(i) for i in range(NT)]
                    for slot_i in range(NT + 4):
                        for k in range(4, -1, -1):
                            ui_ = slot_i - k
                            if 0 <= ui_ < NT:
                                p1u[ui_][k]()

                pre = {}
                pre["pairs"] = [load_wgroup([O_Q + 128 * p_, O_K + 128 * p_, O_V + 128 * p_, O_GF + 128 * p_]) for p_ in range(2)]
                ph23 = ExitStack()
                bias_all = ph23.enter_context(nc.sbuf_tensor("bias_all%d" % b, [128, NH, NTT, NT], F32))
                dT = ph23.enter_context(nc.sbuf_tensor("dT%d" % b, [NH, S], BF16))
                with ExitStack() as ph:
                    if STOP < 2:
                        raise _Stop()
                    sc.barrier()
                    xf = ph.enter_context(nc.sbuf_tensor("xf%d" % b, [128, NT, NH], F32))
                    ef = ph.enter_context(nc.sbuf_tensor("ef%d" % b, [128, NT, NH], F32))
                    lfn = ph.enter_context(nc.sbuf_tensor("lfn%d" % b, [128, NT, NH], F32))
                    padA = ph.enter_context(nc.sbuf_tensor("padA%d" % b, [128, 8 + NT, NH], F32))
                    padB = ph.enter_context(nc.sbuf_tensor("padB%d" % b, [128, 8 + NT, NH], F32))
                    c_tok = ph.enter_context(nc.sbuf_tensor("c_tok%d" % b, [128, NT, NH], F32))
                    rB = ph.enter_context(nc.sbuf_tensor("rB%d" % b, [128, NT, NH], F32))
                    d_tok = ph.enter_context(nc.sbuf_tensor("d_tok%d" % b, [128, NT, NH], BF16))
                    NF = NT * NH
                    zf = bank(0)[:, 0:NF]
                    for i in range(NT):
                        for c in range(8):
                            sc.op("pe", lambda c=c: nc.tensor.matmul(bank(0)[:, i * NH:(i + 1) * NH],
                                                                     lhsT=hT[:, c, i * 128:(i + 1) * 128], rhs=wf[:, c, :],
                                                                     start=(c == 0), stop=(c == 7)),
                                  reads=["hT%d" % i, "wf"], writes=[BN[0]], inc=(c == 7 and i == NT - 1))
                    sc.op("dve", lambda: nc.vector.tensor_tensor(
                        out=xf[:], in0=zf.rearrange("p (i h) -> p i h", h=NH),
                        in1=bf_tile[:].rearrange("p (o h) -> p o h", o=1).broadcast_to([128, NT, NH]), op=ALU.add),
                        reads=[BN[0], "bf_tile"], writes=["xf"])
                    sc.op("act", lambda: nc.scalar.activation(out=ef[:], in_=xf[:], func=AF.Exp, scale=-1.0),
                          reads=["xf"], writes=["ef"])
                    sc.op("act", lambda: nc.scalar.activation(out=lfn[:], in_=ef[:], func=AF.Ln, bias=1.0, scale=1.0),
                          reads=["ef"], writes=["lfn"])
                    lfn2 = lfn[:].rearrange("p i h -> p (i h)")
                    sc.op("pe", lambda: nc.tensor.matmul(bank(1)[:, 0:NF], lhsT=Uneg[:], rhs=lfn2, start=True, stop=True),
                          reads=["lfn", "Uneg"], writes=[BN[1]], inc=False)
                    sc.op("pe", lambda: nc.tensor.matmul(bank(1)[:, 256:256 + NF], lhsT=OnesNeg[:], rhs=lfn2, start=True,
                                                         stop=True),
                          reads=["lfn", "OnesNeg"], writes=[BN[1]])
                    sc.op("dve", lambda: nc.vector.memset(padA[:], 0.0), writes=["padA"])
                    sc.op("dve", lambda: nc.vector.memset(padB[:], 0.0), writes=["padB"])
                    if NT > 1:
                        sc.op("dve", lambda: nc.vector.tensor_copy(
                            out=padA[:, 9:8 + NT, :],
                            in_=bank(1)[:, 256:256 + NF].rearrange("p (i h) -> p i h", h=NH)[:, 0:NT - 1, :]),
                            reads=[BN[1], "padA"], writes=["padA"])
                    cur, oth, curn, othn = padA, padB, "padA", "padB"
                    dstep = 1
                    while dstep < NT:
                        sc.op("dve", lambda cur=cur, oth=oth, dstep=dstep: nc.vector.tensor_tensor(
                            out=oth[:, 8:8 + NT, :], in0=cur[:, 8:8 + NT, :], in1=cur[:, 8 - dstep:8 + NT - dstep, :],
                            op=ALU.add), reads=[curn], writes=[othn])
                        cur, oth, curn, othn = oth, cur, othn, curn
                        dstep *= 2
                    sc.op("dve", lambda: nc.vector.tensor_tensor(
                        out=c_tok[:], in0=bank(1)[:, 0:NF].rearrange("p (i h) -> p i h", h=NH), in1=cur[:, 8:8 + NT, :],
                        op=ALU.add), reads=[BN[1], curn], writes=["c_tok"])
                    sc.op("pe", lambda: nc.tensor.matmul(bank(0)[:, 0:NF], lhsT=Sel[:],
                                                         rhs=c_tok[:].rearrange("p i h -> p (i h)"), start=True, stop=True),
                          reads=["c_tok", "Sel"], writes=[BN[0]])
                    sc.op("act", lambda: nc.scalar.copy(out=rB[:].rearrange("p i h -> p (i h)"), in_=bank(0)[:, 0:NF]),
                          reads=[BN[0]], writes=["rB"])
                    c4 = c_tok[:].rearrange("p (a j) h -> p a j h", j=4)
                    r4 = rB[:].rearrange("p (a j) h -> p a j h", j=4)[:, :, 3:4, :].broadcast_to([128, NTT, 4, NH])
                    sc.op("dve", lambda: nc.vector.tensor_tensor(
                        out=d_tok[:].rearrange("p (a j) h -> p a j h", j=4), in0=c4, in1=r4, op=ALU.subtract),
                        reads=["c_tok", "rB"], writes=["d_tok"])
                    for T in range(NTT):
                        sc.op("dve", lambda T=T: nc.vector.tensor_tensor(
                            out=bias_all[:, :, T, :],
                            in0=rB[:, 4 * T + 3:4 * T + 4, :].rearrange("p o h -> p h o").broadcast_to([128, NH, NT]),
                            in1=c_tok[:].rearrange("p i h -> p h i"), op=ALU.subtract),
                            reads=["c_tok", "rB"], writes=["bias_all"])
                    dps = PS[1].bitcast(BF16)
                    for i in range(NT):
                        sc.op("pe", lambda i=i: nc.tensor.transpose(dps[0:NH, i * 128:(i + 1) * 128], d_tok[:, i, :],
                                                                    ident[:]),
                              reads=["d_tok", "ident"], writes=[BN[2], BN[3]], inc=(i == NT - 1))
                    sc.op("act", lambda: nc.scalar.copy(out=dT[:], in_=dps[0:NH, 0:S]),
                          reads=[BN[2], BN[3]], writes=["dT"])

                with ExitStack() as ph:
                    if STOP < 3:
                        raise _Stop()
                    sc.barrier()
                    QA = [[ph.enter_context(nc.sbuf_tensor("QA%d_%d_%d" % (b, pp, i), [128, S], BF16)) for i in range(2)] for pp in range(2)]
                    KA = [[ph.enter_context(nc.sbuf_tensor("KA%d_%d_%d" % (b, pp, i), [128, S], BF16)) for i in range(2)] for pp in range(2)]
                    Vaug = [ph.enter_context(nc.sbuf_tensor("Vaug%d_%d" % (b, pp), [128, NT, 2, 128], BF16)) for pp in range(2)]
                    gate = [ph.enter_context(nc.sbuf_tensor("gate%d_%d" % (b, pp), [128, S], BF16)) for pp in range(2)]
                    sq_sb = [ph.enter_context(nc.sbuf_tensor("sq%d_%d" % (b, i), [128, 512], BF16)) for i in range(2)]
                    msb = [ph.enter_context(nc.sbuf_tensor("msb%d_%d" % (b, i), [128, 512], F32)) for i in range(2)]
                    rstd = [ph.enter_context(nc.sbuf_tensor("rstd%d_%d" % (b, i), [128, 512], F32)) for i in range(2)]
                    th_sb = [ph.enter_context(nc.sbuf_tensor("th%d_%d" % (b, i), [128, 512], F32)) for i in range(2)]
                    pbuf = [ph.enter_context(nc.sbuf_tensor("pbuf%d_%d" % (b, i), [128, 512], BF16)) for i in range(4)]
                    rec = [ph.enter_context(nc.sbuf_tensor("rec%d_%d" % (b, i), [128, 512], F32)) for i in range(3)]
                    for pp in range(2):
                        for hh in range(2):
                            t_ = "%d%d" % (pp, hh)
                            sc.op("dve", lambda: nc.vector.memset(KA[pp][hh][64:128, :], 0.0), writes=["KArow" + t_])
                            sc.op("dve", lambda: nc.vector.memset(KA[pp][hh][64:65, :], 1.0), reads=["KArow" + t_],
                                  writes=["KArow" + t_])
                            sc.op("dve", lambda: nc.vector.memset(QA[pp][hh][64:128, :], 0.0), writes=["QArow" + t_])
                        sc.op("dve", lambda: nc.vector.memset(Vaug[pp][:, :, 0, 64:128], 1.0), writes=["Vones%d0" % pp])
                        sc.op("dve", lambda: nc.vector.memset(Vaug[pp][:, :, 1, 0:64], 1.0), writes=["Vones%d1" % pp])

                    nq = {"n": 0}
                    wslots = {}

                    def load_pair(p):
                        wslots[p] = load_wgroup([O_Q + 128 * p, O_K + 128 * p, O_V + 128 * p, O_GF + 128 * p])

                    def proj_units(p):
                        pp = p % 2
                        units = []

                        def v_unit(i0):
                            st = {}

                            def A():
                                cur_slot = wslots[p]
                                vb = PJB[nq["n"] % 5]
                                nq["n"] += 1
                                st["vb"] = vb
                                for ii in range(4):
                                    i = i0 + ii
                                    for c in range(8):
                                        sc.op("pe", lambda: nc.tensor.matmul(
                                            bank(vb)[:, ii * 128:(ii + 1) * 128], lhsT=hT[:, c, i * 128:(i + 1) * 128],
                                            rhs=wbuf[cur_slot][:, c, 256:384], start=(c == 0), stop=(c == 7)),
                                            reads=["hT%d" % i, "wbuf%d_2" % cur_slot], writes=[BN[vb]],
                                            inc=(c == 7 and ii == 3))

                            def B():
                                vb = st["vb"]
                                bv = bank(vb).rearrange("p (i c) -> p i c", i=4)
                                sc.op("act", lambda: nc.scalar.copy(out=Vaug[pp][:, i0:i0 + 4, 0, 0:64], in_=bv[:, :, 0:64]),
                                      reads=[BN[vb], "Vones%d0" % pp], writes=["V%d0_%d" % (pp, i0 // 4)])
                                sc.op("act", lambda: nc.scalar.copy(out=Vaug[pp][:, i0:i0 + 4, 1, 64:128], in_=bv[:, :, 64:128]),
                                      reads=[BN[vb], "Vones%d1" % pp], writes=["V%d1_%d" % (pp, i0 // 4)])
                            return [A, B]

                        def qk_unit(tt, qk):
                            st = {}

                            def A():
                                cur_slot = wslots[p]
                                n = nq["n"]
                                nq["n"] += 1
                                st["u"] = u = n % 2
                                st["bk"] = bk = PJB[n % 5]
                                proj_fm(cur_slot, qk, tt, bk)

                            def A2():
                                u = st["u"]
                                bk = st["bk"]
                                sc.op("act", lambda: nc.scalar.activation(out=sq_sb[u][:], in_=bank(bk), func=AF.Square),
                                      reads=[BN[bk]], writes=["sq%d" % u])

                            def B1():
                                u = st["u"]
                                ab = 6
                                sc.op("pe", lambda: nc.tensor.matmul(bank(ab), lhsT=blockones[:], rhs=sq_sb[u][:], start=True, stop=True),
                                      reads=["sq%d" % u, "blockones"], writes=[BN[ab]])
                                sc.op("act", lambda: nc.scalar.activation(out=msb[u][:], in_=bank(ab), func=AF.Ln, bias=EPS, scale=1.0),
                                      reads=[BN[ab]], writes=["msb%d" % u])

                            def B2():
                                u = st["u"]
                                sc.op("act", lambda: nc.scalar.activation(out=rstd[u][:], in_=msb[u][:], func=AF.Exp, scale=-0.5),
                                      reads=["msb%d" % u], writes=["rstd%d" % u])

                            def C():
                                u = st["u"]
                                bk = st["bk"]
                                dst = QA[pp] if qk == 0 else KA[pp]
                                gsc = gq8 if qk == 0 else gk
                                nm = "QA" if qk == 0 else "KA"
                                for hh in range(2):
                                    r0 = hh * 64
                                    sc.op("dve", lambda: nc.vector.scalar_tensor_tensor(
                                        out=dst[hh][0:64, tt * 512:(tt + 1) * 512], in0=bank(bk)[r0:r0 + 64, :],
                                        scalar=gsc[r0:r0 + 64, p:p + 1], in1=rstd[u][r0:r0 + 64, :], op0=ALU.mult, op1=ALU.mult),
                                        reads=[BN[bk], "rstd%d" % u, "gq8", "gk"], writes=["%s%d%d_%d" % (nm, pp, hh, tt)])
                            return [A, A2, B1, B2, C]

                        def gate_unit(tt):
                            st = {}

                            def A():
                                cur_slot = wslots[p]
                                n = nq["n"]
                                nq["n"] += 1
                                st["u"] = u = n % 2
                                st["bk"] = bk = PJB[n % 5]
                                proj_fm(cur_slot, 3, tt, bk)

                            def A2():
                                u = st["u"]
                                bk = st["bk"]
                                sc.op("act", lambda: nc.scalar.activation(out=th_sb[u][:], in_=bank(bk), func=AF.Tanh, scale=0.5),
                                      reads=[BN[bk]], writes=["th%d" % u])

                            def B():
                                u = st["u"]
                                bk = st["bk"]
                                sc.op("dve", lambda: nc.vector.scalar_tensor_tensor(
                                    out=gate[pp][:, tt * 512:(tt + 1) * 512], in0=th_sb[u][:], scalar=1.0, in1=bank(bk),
                                    op0=ALU.add, op1=ALU.mult),
                                    reads=[BN[bk], "th%d" % u], writes=["gate%d_%d" % (pp, tt)])
                            return [A, A2, B]

                        def drow_unit():
                            def A():
                                for hh in range(2):
                                    h = 2 * p + hh
                                    sc.dma("sp", "drow%d%d" % (pp, hh), QA[pp][hh][64:65, :], dT[h:h + 1, :], reads=["dT"],
                                           writes=["QArow%d%d" % (pp, hh)])
                            return [A]

                        ulist = [drow_unit()]
                        for i0 in range(0, NT, 4):
                            ulist.append(v_unit(i0))
                        for tt in range(NTT):
                            for qk in range(2):
                                ulist.append(qk_unit(tt, qk))
                        for tt in range(NTT):
                            ulist.append(gate_unit(tt))
                        flat = [lambda: nq.__setitem__("n", 0)]
                        nu = len(ulist)
                        mx = max(len(u_) for u_ in ulist)
                        for slot_i in range(nu + mx - 1):
                            for k in range(mx - 1, -1, -1):
                                ui_ = slot_i - k
                                if 0 <= ui_ < nu and k < len(ulist[ui_]):
                                    flat.append(ulist[ui_][k])
                        return flat

                    def attention(p, next_units):
                        pp = p % 2
                        steps = []
                        for hh in range(2):
                            for T in range(NTT):
                                for J in range(4 * T + 4):
                                    steps.append((hh, T, J))

                        def emit_S(si):
                            hh, T, J = steps[si]
                            j = J - 4 * T
                            col0 = 128 * j if j >= 0 else 0
                            sbk = (2, 6, 3)[si % 3]
                            sc.op("pe", lambda: nc.tensor.matmul(
                                bank(sbk)[:, col0:512], lhsT=KA[pp][hh][:, J * 128:(J + 1) * 128],
                                rhs=QA[pp][hh][:, T * 512 + col0:(T + 1) * 512], start=True, stop=(j < 0)),
                                reads=["KA%d%d_%d" % (pp, hh, J // 4), "KArow%d%d" % (pp, hh), "QA%d%d_%d" % (pp, hh, T),
                                       "QArow%d%d" % (pp, hh)],
                                writes=[BN[sbk]], inc=(j < 0))
                            if j >= 0:
                                sc.op("pe", lambda: nc.tensor.matmul(
                                    bank(sbk)[:, col0:col0 + 128], lhsT=ident[:], rhs=maskT[:], start=False, stop=True),
                                    reads=["ident", "maskT"], writes=[BN[sbk]])

                        def emit_exp(si):
                            hh, T, J = steps[si]
                            h = 2 * p + hh
                            j = J - 4 * T
                            col0 = 128 * j if j >= 0 else 0
                            sbk = (2, 6, 3)[si % 3]
                            pi = si % 4
                            sc.op("act", lambda: nc.scalar.activation(
                                out=pbuf[pi][:, col0:512], in_=bank(sbk)[:, col0:512], func=AF.Exp,
                                bias=bias_all[:, h, T, J:J + 1], scale=1.0),
                                reads=[BN[sbk], "bias_all"], writes=["pbuf%d" % pi])

                        def emit_PV(si):
                            hh, T, J = steps[si]
                            j = J - 4 * T
                            col0 = 128 * j if j >= 0 else 0
                            pi = si % 4
                            acc_n = (hh * NTT + T) % 3
                            ab = (4, 5, 0)[acc_n]
                            last = (J == 4 * T + 3)
                            sc.op("pe", lambda: nc.tensor.matmul(
                                bank(ab)[:, col0:512], lhsT=Vaug[pp][:, J, hh, :], rhs=pbuf[pi][:, col0:512],
                                start=(J == 0), stop=last),
                                reads=["pbuf%d" % pi, "V%d%d_%d" % (pp, hh, J // 4), "Vones%d%d" % (pp, hh)], writes=[BN[ab]],
                                inc=last)
                            if last:
                                num0, den0 = (0, 64) if hh == 0 else (64, 0)
                                ri = acc_n
                                sc.op("dve", lambda: nc.vector.reciprocal(out=rec[ri][num0:num0 + 64, :],
                                                                          in_=bank(ab)[den0:den0 + 64, :]),
                                      reads=[BN[ab]], writes=["rec%d" % ri])
                                sc.op("pool", lambda: nc.gpsimd.tensor_tensor(
                                    out=rec[ri][num0:num0 + 64, :], in0=rec[ri][num0:num0 + 64, :],
                                    in1=gate[pp][num0:num0 + 64, T * 512:(T + 1) * 512], op=ALU.mult),
                                    reads=["rec%d" % ri, "gate%d_%d" % (pp, T)], writes=["rec%d" % ri])
                                sc.op("dve", lambda: nc.vector.scalar_tensor_tensor(
                                    out=yT[num0:num0 + 64, p, T * 512:(T + 1) * 512], in0=bank(ab)[num0:num0 + 64, :],
                                    scalar=0.5, in1=rec[ri][num0:num0 + 64, :], op0=ALU.mult, op1=ALU.mult),
                                    reads=[BN[ab], "rec%d" % ri], writes=["yT%d_%d_%d" % (p, hh, T)])

                        nst = len(steps)
                        nun = len(next_units)
                        stride = max(1, nst // (nun + 1)) if nun else nst
                        ui = 0
                        emit_S(0)
                        if nst > 1:
                            emit_S(1)
                        for si in range(nst):
                            emit_exp(si)
                            if si + 2 < nst:
                                emit_S(si + 2)
                            emit_PV(si)
                            if INTERLEAVE and ui < nun and (si + 1) % stride == 0:
                                next_units[ui]()
                                ui += 1
                        while ui < nun:
                            next_units[ui]()
                            ui += 1

                    wslots[0], wslots[1] = pre["pairs"]
                    for un in proj_units(0):
                        un()
                    for p in range(8):
                        if p + 2 < 8:
                            load_pair(p + 2)
                        if p == 7:
                            pre["glu"] = [load_wgroup([O_GLU + 128 * c_, O_GLU + 1024 + 128 * c_]) for c_ in range(2)]
                        attention(p, proj_units(p + 1) if p + 1 < 8 else [])

                ph23.close()
                with ExitStack() as ph:
                    if STOP < 4:
                        raise _Stop()
                    sc.barrier()
                    wout = ph.enter_context(nc.sbuf_tensor("wout%d" % b, [128, 16, D], BF16))
                    for half in range(2):
                        sc.dma("pool", "wo%d" % half, wout[:, half * 8:(half + 1) * 8, :], w_out_v[:, half * 8:(half + 1) * 8, :],
                               writes=["wout%d" % half])
                    with ExitStack() as ph4:
                        with ExitStack() as ph4a:
                            uT = [ph4a.enter_context(nc.sbuf_tensor("uT%d_%d" % (b, i), [128, S + 32], BF16)) for i in range(2)]
                            Wd = [ph4a.enter_context(nc.sbuf_tensor("Wd%d_%d" % (b, i), [128, KC, 128], BF16)) for i in range(2)]
                            tb = [ph4a.enter_context(nc.sbuf_tensor("tb%d_%d" % (b, i), [128, 512], F32)) for i in range(2)]
                            cacc = [ph4a.enter_context(nc.sbuf_tensor("cacc%d_%d" % (b, i), [128, 512], F32)) for i in range(2)]
                            for i in range(2):
                                sc.op("pool", lambda i=i: nc.gpsimd.memset(uT[i][:, 0:32], 0.0), writes=["uTpad%d" % i])
                            def build_wd(ch):
                                us = ch % 2
                                sc.op("pool", lambda: nc.gpsimd.tensor_tensor(
                                    out=Wd[us][:],
                                    in0=ident[:].rearrange("p (o n) -> p o n", o=1).broadcast_to([128, KC, 128]),
                                    in1=convwh[:, ch, :].rearrange("p (k o) -> p k o", o=1).broadcast_to([128, KC, 128]),
                                    op=ALU.mult), reads=["ident", "convwh"], writes=["Wd%d" % us])

                            def glu_pe(ch, tts, slot_):
                                for tt in tts:
                                    ba = tt % 2
                                    bb = 6 + (tt % 2)
                                    u = tt % 2
                                    proj_fm(slot_, 0, tt, ba)
                                    proj_fm(slot_, 1, tt, bb)
                                    sc.op("act", lambda: nc.scalar.activation(out=tb[u][:], in_=bank(bb), func=AF.Tanh, scale=0.5),
                                          reads=[BN[bb]], writes=["tb%d" % u])

                            def glu_dve(ch, tts):
                                us = ch % 2
                                for tt in tts:
                                    ba = tt % 2
                                    u = tt % 2
                                    sc.op("dve", lambda: nc.vector.scalar_tensor_tensor(
                                        out=uT[us][:, 30 + tt * 512:30 + (tt + 1) * 512], in0=tb[u][:], scalar=1.0,
                                        in1=bank(ba), op0=ALU.add, op1=ALU.mult),
                                        reads=[BN[ba], "tb%d" % u, "uTpad%d" % us], writes=["uT%d_%d" % (us, tt)])

                            def glu_proj(ch, tts, slot_):
                                for tt in tts:
                                    glu_pe(ch, [tt], slot_)
                                    glu_dve(ch, [tt])

                            def taps(ch, tts, rds, k0, k1):
                                us = ch % 2
                                for k in range(k0, k1):
                                    for tt in tts:
                                        ca = cacc[tt % 2]
                                        can = "cacc%d" % (tt % 2)
                                        if k == NPE:
                                            sc.op("dve", lambda: nc.vector.tensor_scalar(
                                                out=ca[:], in0=uT[us][:, tt * 512 + k:tt * 512 + k + 512],
                                                scalar1=convwh[:, ch, k:k + 1], scalar2=None, op0=ALU.mult),
                                                reads=rds[tt] + ["convwh"], writes=[can])
                                        else:
                                            sc.op("dve", lambda: nc.vector.scalar_tensor_tensor(
                                                out=ca[:], in0=uT[us][:, tt * 512 + k:tt * 512 + k + 512],
                                                scalar=convwh[:, ch, k:k + 1], in1=ca[:], op0=ALU.mult, op1=ALU.add),
                                                reads=rds[tt] + ["convwh", can], writes=[can])

                            gslot = {0: pre["glu"][0], 1: pre["glu"][1]}
                            build_wd(0)
                            glu_proj(0, list(range(NTT)), gslot[0])
                            KMID = (NPE + KC) // 2
                            for ch in range(8):
                                us = ch % 2
                                if ch == 7:
                                    pre["gc"] = [load_wgroup([O_GC + 512 * gi], width=512) for gi in range(2)]
                                if ch + 2 < 8:
                                    gslot[ch + 2] = load_wgroup([O_GLU + 128 * (ch + 2), O_GLU + 1024 + 128 * (ch + 2)])
                                if ch + 1 < 8:
                                    build_wd(ch + 1)
                                for tp in range(0, NTT, 2):
                                    tts = [t for t in (tp, tp + 1) if t < NTT]
                                    rds = {}
                                    if ch + 1 < 8:
                                        glu_pe(ch + 1, tts, gslot[ch + 1])
                                    for tt in tts:
                                        cb = 2 + (tt % 2)
                                        rd = ["Wd%d" % us, "uTpad%d" % us, "uT%d_%d" % (us, tt)]
                                        if tt > 0:
                                            rd.append("uT%d_%d" % (us, tt - 1))
                                        rds[tt] = rd
                                        for k in range(NPE):
                                            sc.op("pe", lambda k=k: nc.tensor.matmul(
                                                bank(cb), lhsT=Wd[us][:, k, :], rhs=uT[us][:, tt * 512 + k:tt * 512 + k + 512],
                                                start=(k == 0), stop=(k == NPE - 1)),
                                                reads=rd, writes=[BN[cb]], inc=(k == NPE - 1))
                                    taps(ch, tts, rds, NPE, KMID)
                                    if ch + 1 < 8:
                                        glu_dve(ch + 1, tts)
                                    taps(ch, tts, rds, KMID, KC)
                                    for tt in tts:
                                        cb = 2 + (tt % 2)
                                        ca = cacc[tt % 2]
                                        can = "cacc%d" % (tt % 2)
                                        sc.op("dve", lambda: nc.vector.scalar_tensor_tensor(
                                            out=yT[:, 8 + ch, tt * 512:(tt + 1) * 512], in0=bank(cb),
                                            scalar=convb[:, ch:ch + 1], in1=ca[:], op0=ALU.add, op1=ALU.add),
                                            reads=[BN[cb], "convb", can], writes=["yc%d_%d" % (ch, tt)])
                        sc.barrier()
                        sq2 = [ph4.enter_context(nc.sbuf_tensor("sqc%d_%d" % (b, i), [128, 512], BF16)) for i in range(2)]
                        mean_sb = ph4.enter_context(nc.sbuf_tensor("mean%d" % b, [128, S], F32))
                        rstd_sb = ph4.enter_context(nc.sbuf_tensor("rstdc%d" % b, [128, S], F32))
                        t1 = [ph4.enter_context(nc.sbuf_tensor("t1_%d_%d" % (b, i), [128, 512], F32)) for i in range(NSET)]
                        th2 = [ph4.enter_context(nc.sbuf_tensor("th2_%d_%d" % (b, i), [128, 512], F32)) for i in range(2)]
                        vq = [ph4.enter_context(nc.sbuf_tensor("vq_%d_%d" % (b, i), [128, 512], F32)) for i in range(NSET)]
                        thc = [ph4.enter_context(nc.sbuf_tensor("thc_%d_%d" % (b, i), [128, 512], F32)) for i in range(NSET)]
                        gcv = [ph4.enter_context(nc.sbuf_tensor("gcv_%d_%d" % (b, i), [128, 512], F32)) for i in range(NSET)]
                        m2 = t1[0]
                        vare = t1[1]
                        for tt in range(NTT):
                            for ch in range(8):
                                sc.op("pe", lambda ch=ch: nc.tensor.matmul(
                                    bank(4), lhsT=onesmean[:], rhs=yT[:, 8 + ch, tt * 512:(tt + 1) * 512],
                                    start=(ch == 0), stop=(ch == 7)),
                                    reads=["onesmean", "yc%d_%d" % (ch, tt)], writes=[BN[4]], inc=(ch == 7))
                            for ch in range(8):
                                qi = ch % 2
                                sc.op("act", lambda ch=ch, qi=qi: nc.scalar.activation(
                                    out=sq2[qi][:], in_=yT[:, 8 + ch, tt * 512:(tt + 1) * 512], func=AF.Square),
                                    reads=["yc%d_%d" % (ch, tt)], writes=["sqc%d" % qi])
                                sc.op("pe", lambda ch=ch, qi=qi: nc.tensor.matmul(
                                    bank(5), lhsT=onesmean[:], rhs=sq2[qi][:], start=(ch == 0), stop=(ch == 7)),
                                    reads=["onesmean", "sqc%d" % qi], writes=[BN[5]], inc=True)
                            sc.op("act", lambda: nc.scalar.copy(out=mean_sb[:, tt * 512:(tt + 1) * 512], in_=bank(4)),
                                  reads=[BN[4]], writes=["mean%d" % tt])
                            sc.op("dve", lambda: nc.vector.tensor_tensor(
                                out=m2[:], in0=mean_sb[:, tt * 512:(tt + 1) * 512],
                                in1=mean_sb[:, tt * 512:(tt + 1) * 512], op=ALU.mult),
                                reads=["mean%d" % tt], writes=["t1_0"])
                            sc.op("dve", lambda: nc.vector.scalar_tensor_tensor(
                                out=vare[:], in0=bank(5), scalar=EPS, in1=m2[:], op0=ALU.add, op1=ALU.subtract),
                                reads=[BN[5], "t1_0"], writes=["t1_1"])
                            sc.op("act", lambda: nc.scalar.activation(out=vare[:], in_=vare[:], func=AF.Ln),
                                  reads=["t1_1"], writes=["t1_1"])
                            sc.op("act", lambda: nc.scalar.activation(
                                out=rstd_sb[:, tt * 512:(tt + 1) * 512], in_=vare[:], func=AF.Exp, scale=-0.5),
                                reads=["t1_1"], writes=["rstdc%d" % tt])
                        gslots = pre["gc"]
                        blocks = [(gi, c4, tt) for tt in range(NTT) for gi in range(2) for c4 in range(4)]
                        xo = [ph4.enter_context(nc.sbuf_tensor("xo%d_%d" % (b, i), [128, D], F32)) for i in range(2)]
                        ot = [ph4.enter_context(nc.sbuf_tensor("ot%d_%d" % (b, i), [128, D], F32)) for i in range(1)]

                        def og(i, half):
                            s_ = i % 2
                            T = i // 4
                            ob = 4 + half
                            if half == 0:
                                sc.dma("sp", "xm%d" % s_, xo[s_][:], x[b, i * 128:(i + 1) * 128, :], writes=["xo%d" % s_])
                            for e in range(16):
                                if e < 8:
                                    rd = ["yT%d_0_%d" % (e, T), "yT%d_1_%d" % (e, T)]
                                else:
                                    rd = ["yc%d_%d" % (e - 8, T)]
                                sc.op("pe", lambda: nc.tensor.matmul(
                                    bank(ob), lhsT=yT[:, e, i * 128:(i + 1) * 128],
                                    rhs=wout[:, e, half * 512:(half + 1) * 512], start=(e == 0), stop=(e == 15)),
                                    reads=rd + ["wout%d" % (e // 8)], writes=[BN[ob]], inc=(e == 15))
                            sc.op("dve", lambda: nc.vector.tensor_tensor(
                                out=ot[0][:, half * 512:(half + 1) * 512], in0=bank(ob),
                                in1=xo[s_][:, half * 512:(half + 1) * 512], op=ALU.add),
                                reads=[BN[ob], "xo%d" % s_], writes=["ot_%d" % half])
                            if half == 1:
                                sc.dma("sp", "ost", out[b, i * 128:(i + 1) * 128, :], ot[0][:], reads=["ot_0", "ot_1"])

                        ogs = [(i, half) for i in range(NT) for half in range(2)]

                        LB = (0, 1, 2, 3, 6, 7)

                        def ln_stages(n):
                            gi, c4, tt = blocks[n]
                            ch = gi * 4 + c4
                            u = n % NSET
                            pbk = LB[n % 6]
                            ysl = yT[:, 8 + ch, tt * 512:(tt + 1) * 512]

                            def s0():
                                proj_fm(gslots[gi], c4, tt, pbk)

                            def s1():
                                sc.op("act", lambda: nc.scalar.activation(out=thc[u][:], in_=bank(pbk), func=AF.Tanh, scale=0.5),
                                      reads=[BN[pbk]], writes=["thc%d" % u])
                                sc.op("dve", lambda: nc.vector.tensor_tensor(
                                    out=t1[u][:], in0=ysl, in1=mean_sb[:, tt * 512:(tt + 1) * 512], op=ALU.subtract),
                                    reads=["yc%d_%d" % (ch, tt), "mean%d" % tt], writes=["t1_%d" % u])

                            def s2():
                                sc.op("pool", lambda: nc.gpsimd.tensor_tensor(
                                    out=t1[u][:], in0=t1[u][:], in1=rstd_sb[:, tt * 512:(tt + 1) * 512], op=ALU.mult),
                                    reads=["t1_%d" % u, "rstdc%d" % tt], writes=["t1_%d" % u])

                            def s3():
                                sc.op("act", lambda: nc.scalar.activation(
                                    out=th2[n % 2][:], in_=t1[u][:], func=AF.Tanh, bias=lnb_h[:, ch:ch + 1],
                                    scale=lng_h[:, ch:ch + 1]),
                                    reads=["t1_%d" % u, "lng_h", "lnb_h"], writes=["th2_%d" % (n % 2)])
                                sc.op("dve", lambda: nc.vector.scalar_tensor_tensor(
                                    out=gcv[u][:], in0=thc[u][:], scalar=1.0, in1=bank(pbk), op0=ALU.add,
                                    op1=ALU.mult), reads=[BN[pbk], "thc%d" % u], writes=["gcv%d" % u])
                                sc.op("act", lambda: nc.scalar.activation(
                                    out=vq[u][:], in_=t1[u][:], func=AF.Identity, bias=lnb_q[:, ch:ch + 1],
                                    scale=lng_q[:, ch:ch + 1]),
                                    reads=["t1_%d" % u, "lng_q", "lnb_q"], writes=["vq%d" % u])

                            def s4():
                                sc.op("dve", lambda: nc.vector.scalar_tensor_tensor(
                                    out=vq[u][:], in0=th2[n % 2][:], scalar=1.0, in1=vq[u][:], op0=ALU.add, op1=ALU.mult),
                                    reads=["th2_%d" % (n % 2), "vq%d" % u], writes=["vq%d" % u])

                            def s5():
                                if n % 2 == 0:
                                    sc.op("pool", lambda: nc.gpsimd.tensor_tensor(
                                        out=ysl, in0=vq[u][:], in1=gcv[u][:], op=ALU.mult),
                                        reads=["vq%d" % u, "gcv%d" % u], writes=["yc%d_%d" % (ch, tt)])
                                else:
                                    sc.op("dve", lambda: nc.vector.tensor_tensor(
                                        out=ysl, in0=vq[u][:], in1=gcv[u][:], op=ALU.mult),
                                        reads=["vq%d" % u, "gcv%d" % u], writes=["yc%d_%d" % (ch, tt)])
                            return [s0, s1, s2, s3, s4, s5]

                        lnu = [ln_stages(n) for n in range(len(blocks))]
                        oq = 0
                        for slot_i in range(len(lnu) + 5):
                            for k in range(5, -1, -1):
                                ui_ = slot_i - k
                                if 0 <= ui_ < len(lnu):
                                    lnu[ui_][k]()
                            if slot_i >= 12 and oq < len(ogs) and oq <= slot_i - 12:
                                og(*ogs[oq])
                                oq += 1
                        while oq < len(ogs):
                            og(*ogs[oq])
                            oq += 1
          except _Stop:
            stopped = True
            break
        sc.wait_all("sp")
        if stopped:
            es.pop_all()
    return nc


def _prep_shared(inp):
    f = np.float32
    qg = np.asarray(inp["q_norm_g"], f)[0]
    kg = np.asarray(inp["k_norm_g"], f)[0]
    gq = np.ascontiguousarray(qg.reshape(8, 128).T)
    gk = np.ascontiguousarray(kg.reshape(8, 128).T)
    cw = np.asarray(inp["conv_w"], f)[0]
    convw = np.ascontiguousarray(cw.T.reshape(8, 128, KC).transpose(1, 0, 2))

    def pc(a):
        return np.ascontiguousarray(np.asarray(a, f)[0].reshape(8, 128).T)

    return {
        "w_in": np.ascontiguousarray(np.asarray(inp["w_in"], f)[0]),
        "w_out": np.ascontiguousarray(np.asarray(inp["w_out"], f)[0]),
        "norm_g": np.ascontiguousarray(np.asarray(inp["norm_g"], f)),
        "b_forget": np.ascontiguousarray(np.asarray(inp["b_forget"], f)),
        "gq": gq, "gk": gk, "convw": convw,
        "convb": pc(inp["conv_b"]), "lng": pc(inp["conv_ln_g"]), "lnb": pc(inp["conv_ln_b"]),
    }


def kernel(**inputs):
    x = np.asarray(inputs["x"], np.float32)
    B, S, _ = x.shape
    n = N_CORES
    nb = B // n
    shared = _prep_shared(inputs)
    nc = build_nc(NB=nb, S=S)
    in_maps = []
    for c in range(n):
        m = dict(shared)
        m["x"] = np.ascontiguousarray(x[c * nb:(c + 1) * nb])
        in_maps.append(m)
    res = run_bass_kernel_spmd(nc, in_maps, core_ids=list(range(n)))
    return np.concatenate([np.asarray(r["out"]) for r in res.results], axis=0).astype(np.float32)
```

```python
import numpy as np
from contextlib import ExitStack
import concourse.bass as bass
import concourse.mybir as mybir
from concourse.bass_utils import run_bass_kernel_spmd

F32 = mybir.dt.float32
BF16 = mybir.dt.bfloat16
AF = mybir.ActivationFunctionType
ALU = mybir.AluOpType

D = 1024
NH = 16
HD = 64
KC = 31
IN_COLS = 7184
O_Q, O_K, O_V, O_F, O_GF, O_GLU, O_GC = 0, 1024, 2048, 3072, 3088, 4112, 6160
EPS = 1e-6
MASKV = -30000.0
N_CORES = 8
INTERLEAVE = False
PJB = (1, 7, 3, 0, 5)
NSET = 3
NPE = 23


class Sched:
    def __init__(self, nc, es):
        self.nc = nc
        self.eng = {"pe": nc.tensor, "act": nc.scalar, "dve": nc.vector, "pool": nc.gpsimd, "sp": nc.sync}
        self.sem = {k: es.enter_context(nc.semaphore("s_" + k)) for k in self.eng}
        self.cnt = {k: 0 for k in self.eng}
        self.seen = {k: {} for k in self.eng}
        self.dsem = {}
        self.dcnt = {}
        self.es = es
        self.res = {}
        self.pending_noinc = {k: False for k in self.eng}

    def _semh(self, key):
        return self.sem[key] if key in self.sem else self.dsem[key]

    def _wait(self, engine, tok):
        key, val = tok
        if self.seen[engine].get(key, 0) >= val:
            return
        self.seen[engine][key] = val
        self.eng[engine].wait_ge(self._semh(key), val)

    def _deps(self, engine, reads, writes):
        for r in reads:
            st = self.res.get(r)
            if st is not None and st["w"] is not None:
                if not (engine == "pe" and st["w"][0] == "pe"):
                    self._wait(engine, st["w"])
        for w in writes:
            st = self.res.get(w)
            if st is None:
                continue
            if st["w"] is not None and not (engine == "pe" and st["w"][0] == "pe"):
                self._wait(engine, st["w"])
            for k, v in st["r"].items():
                if k == engine and engine == "pe":
                    continue
                self._wait(engine, (k, v))

    def _record(self, tok, reads, writes):
        for r in reads:
            st = self.res.setdefault(r, {"w": None, "r": {}})
            if st["r"].get(tok[0], 0) < tok[1]:
                st["r"][tok[0]] = tok[1]
        for w in writes:
            self.res[w] = {"w": tok, "r": {}}

    def op(self, engine, fn, reads=(), writes=(), inc=True):
        self._deps(engine, reads, writes)
        ins = fn()
        if inc:
            self.cnt[engine] += 1
            ins.then_inc(self.sem[engine], 1)
            tok = (engine, self.cnt[engine])
            self.pending_noinc[engine] = False
        else:
            tok = (engine, self.cnt[engine] + 1)
            self.pending_noinc[engine] = True
        self._record(tok, reads, writes)
        return tok

    def dma(self, queue, semname, out, in_, reads=(), writes=()):
        if semname not in self.dsem:
            self.dsem[semname] = self.es.enter_context(self.nc.semaphore("d_" + semname))
            self.dcnt[semname] = 0
        self._deps(queue, reads, writes)
        ins = self.eng[queue].dma_start(out=out, in_=in_)
        self.dcnt[semname] += 16
        ins.then_inc(self.dsem[semname], 16)
        tok = (semname, self.dcnt[semname])
        self._record(tok, reads, writes)
        return tok

    def retoken(self, names, tok):
        for n in names:
            self.res[n] = {"w": tok, "r": {}}

    def barrier(self):
        assert not self.pending_noinc["pe"]
        for e in self.eng:
            if e != "pe":
                self.wait_all(e)

    def wait_all(self, engine):
        for k in self.eng:
            if k != engine and self.cnt[k] > 0:
                self._wait(engine, (k, self.cnt[k]))
        for k, v in self.dcnt.items():
            if v > 0:
                self._wait(engine, (k, v))


class _Stop(Exception):
    pass


def build_nc(NB=2, S=2048, STOP=9):
    NT = S // 128
    NTT = S // 512
    nc = bass.Bass("TRN2", target_bir_lowering=False)
    x = nc.dram_tensor("x", [NB, S, D], F32, kind="ExternalInput").ap()
    w_in = nc.dram_tensor("w_in", [D, IN_COLS], F32, kind="ExternalInput").ap()
    w_out = nc.dram_tensor("w_out", [2 * D, D], F32, kind="ExternalInput").ap()
    norm_g = nc.dram_tensor("norm_g", [1, D], F32, kind="ExternalInput").ap()
    b_forget = nc.dram_tensor("b_forget", [1, NH], F32, kind="ExternalInput").ap()
    gq_d = nc.dram_tensor("gq", [128, 8], F32, kind="ExternalInput").ap()
    gk_d = nc.dram_tensor("gk", [128, 8], F32, kind="ExternalInput").ap()
    convw_d = nc.dram_tensor("convw", [128, 8, KC], F32, kind="ExternalInput").ap()
    convb_d = nc.dram_tensor("convb", [128, 8], F32, kind="ExternalInput").ap()
    lng_d = nc.dram_tensor("lng", [128, 8], F32, kind="ExternalInput").ap()
    lnb_d = nc.dram_tensor("lnb", [128, 8], F32, kind="ExternalInput").ap()
    out = nc.dram_tensor("out", [NB, S, D], F32, kind="ExternalOutput").ap()

    w_in_v = w_in.rearrange("(c p) n -> p c n", p=128)
    w_out_v = w_out.rearrange("(c p) n -> p c n", p=128)

    with ExitStack() as es:
        sc = Sched(nc, es)

        def sb(name, shape, dt):
            return es.enter_context(nc.sbuf_tensor(name, shape, dt))

        ident = sb("ident", [128, 128], BF16)
        maskT = sb("maskT", [128, 128], BF16)
        blockones = sb("blockones", [128, 128], BF16)
        onesmean = sb("onesmean", [128, 128], BF16)
        Uneg = sb("Uneg", [128, 128], F32)
        OnesNeg = sb("OnesNeg", [128, 128], F32)
        Sel = sb("Sel", [128, 128], F32)
        bf_tile = sb("bf_tile", [128, NH], F32)
        gq = sb("gq_sb", [128, 8], F32)
        gk = sb("gk_sb", [128, 8], F32)
        gq8 = sb("gq8", [128, 8], F32)
        convw = sb("convw_sb", [128, 8, KC], F32)
        convwh = sb("convwh", [128, 8, KC], F32)
        convb = sb("convb_sb", [128, 8], F32)
        lng = sb("lng_sb", [128, 8], F32)
        lnb = sb("lnb_sb", [128, 8], F32)
        lng_h = sb("lng_h", [128, 8], F32)
        lnb_h = sb("lnb_h", [128, 8], F32)
        lng_q = sb("lng_q", [128, 8], F32)
        lnb_q = sb("lnb_q", [128, 8], F32)
        wf = sb("wf", [128, 8, NH], BF16)
        hT = sb("hT", [128, 8, S], BF16)
        yT = sb("yT", [128, 16, S], BF16)
        wbuf = [sb("wbuf%d" % i, [128, 8, 512], BF16) for i in range(2)]

        PS = [es.enter_context(nc.psum_tensor("ps%d" % i, [128, 1024], F32)) for i in range(4)]

        def bank(b):
            return PS[b // 2][:, (b % 2) * 512:(b % 2) * 512 + 512]

        def bank_bf(b):
            t = PS[b // 2].bitcast(BF16)
            return t[:, (b % 2) * 1024:(b % 2) * 1024 + 1024]

        BN = ["B%d" % i for i in range(8)]

        g = nc.gpsimd
        sc.op("pool", lambda: g.memset(ident[:], 1.0), writes=["ident"])
        sc.op("pool", lambda: g.affine_select(out=ident[:], in_=ident[:], compare_op=ALU.is_equal, fill=0.0,
                                              base=0, pattern=[[1, 128]], channel_multiplier=-1),
              reads=["ident"], writes=["ident"])
        sc.op("pool", lambda: g.memset(maskT[:], 0.0), writes=["maskT"])
        sc.op("pool", lambda: g.affine_select(out=maskT[:], in_=maskT[:], compare_op=ALU.is_ge, fill=MASKV,
                                              base=0, pattern=[[1, 128]], channel_multiplier=-1),
              reads=["maskT"], writes=["maskT"])
        sc.op("pool", lambda: g.memset(blockones[:], 0.0), writes=["blockones"])
        sc.op("pool", lambda: g.memset(blockones[0:64, 0:64], 1.0 / 64), reads=["blockones"], writes=["blockones"])
        sc.op("pool", lambda: g.memset(blockones[64:128, 64:128], 1.0 / 64), reads=["blockones"], writes=["blockones"])
        sc.op("pool", lambda: g.memset(onesmean[:], 1.0 / 1024), writes=["onesmean"])
        sc.op("pool", lambda: g.memset(Uneg[:], -1.0), writes=["Uneg"])
        sc.op("pool", lambda: g.affine_select(out=Uneg[:], in_=Uneg[:], compare_op=ALU.is_ge, fill=0.0,
                                              base=0, pattern=[[1, 128]], channel_multiplier=-1),
              reads=["Uneg"], writes=["Uneg"])
        sc.op("pool", lambda: g.memset(OnesNeg[:], -1.0), writes=["OnesNeg"])
        sc.op("pool", lambda: g.memset(Sel[:], 1.0), writes=["Sel"])
        sc.op("pool", lambda: g.affine_select(out=Sel[:], in_=Sel[:], compare_op=ALU.is_equal, fill=0.0,
                                              base=-127, pattern=[[0, 128]], channel_multiplier=1),
              reads=["Sel"], writes=["Sel"])

        sc.dma("sp", "par", bf_tile[:], b_forget.partition_broadcast(128).rearrange("p o n -> p (o n)"), writes=["bf_tile"])
        sc.dma("sp", "par", gq[:], gq_d, writes=["gq"])
        sc.dma("sp", "par", gk[:], gk_d, writes=["gk"])
        sc.dma("sp", "par", convw[:], convw_d, writes=["convw"])
        sc.dma("sp", "par", convb[:], convb_d, writes=["convb"])
        sc.dma("sp", "par", lng[:], lng_d, writes=["lng"])
        sc.dma("sp", "par", lnb[:], lnb_d, writes=["lnb"])
        sc.retoken(["bf_tile", "gq", "gk", "convw", "convb", "lng", "lnb"], ("par", sc.dcnt["par"]))
        sc.dma("pool", "wfl", wf[:], w_in_v[:, :, O_F:O_F + NH], writes=["wf"])
        v = nc.vector
        sc.op("dve", lambda: v.tensor_scalar(out=gq8[:], in0=gq[:], scalar1=0.125, scalar2=None, op0=ALU.mult),
              reads=["gq"], writes=["gq8"])
        sc.op("dve", lambda: v.tensor_scalar(out=convwh[:], in0=convw[:], scalar1=0.5, scalar2=None, op0=ALU.mult),
              reads=["convw"], writes=["convwh"])
        sc.op("dve", lambda: v.tensor_scalar(out=lng_h[:], in0=lng[:], scalar1=0.5, scalar2=None, op0=ALU.mult),
              reads=["lng"], writes=["lng_h"])
        sc.op("dve", lambda: v.tensor_scalar(out=lnb_h[:], in0=lnb[:], scalar1=0.5, scalar2=None, op0=ALU.mult),
              reads=["lnb"], writes=["lnb_h"])
        sc.op("dve", lambda: v.tensor_scalar(out=lng_q[:], in0=lng[:], scalar1=0.25, scalar2=None, op0=ALU.mult),
              reads=["lng"], writes=["lng_q"])
        sc.op("dve", lambda: v.tensor_scalar(out=lnb_q[:], in0=lnb[:], scalar1=0.25, scalar2=None, op0=ALU.mult),
              reads=["lnb"], writes=["lnb_q"])

        wstate = {"n": 0}

        def load_wgroup(cols, width=128):
            slot = wstate["n"] % 2
            wstate["n"] += 1
            names = []
            tok = None
            for j, c0 in enumerate(cols):
                nm = ["wbuf%d_%d" % (slot, jj) for jj in range(j * width // 128, (j + 1) * width // 128)]
                tok = sc.dma("pool", "w%d" % slot, wbuf[slot][:, :, j * width:(j + 1) * width],
                             w_in_v[:, :, c0:c0 + width], writes=nm)
                names += nm
            sc.retoken(names, tok)
            return slot

        def proj_fm(slot, blk, tt, b_out):
            for c in range(8):
                sc.op("pe", lambda c=c: nc.tensor.matmul(bank(b_out), lhsT=wbuf[slot][:, c, blk * 128:(blk + 1) * 128],
                                                         rhs=hT[:, c, tt * 512:(tt + 1) * 512],
                                                         start=(c == 0), stop=(c == 7)),
                      reads=["wbuf%d_%d" % (slot, blk)] + ["hT%d" % i for i in range(4 * tt, 4 * tt + 4)],
                      writes=[BN[b_out]], inc=(c == 7))

        stopped = False
        for b in range(NB):
          try:
                with ExitStack() as ph:
                    if STOP < 1:
                        raise _Stop()
                    sc.barrier()
                    NXS = 4
                    g_tile = ph.enter_context(nc.sbuf_tensor("g_tile%d" % b, [128, D], F32))
                    sc.dma("sp", "gt", g_tile[:], norm_g.partition_broadcast(128).rearrange("p o n -> p (o n)"), writes=["g_tile"])
                    xt = [ph.enter_context(nc.sbuf_tensor("xt%d_%d" % (b, i), [128, D], F32)) for i in range(NXS)]
                    hb = [ph.enter_context(nc.sbuf_tensor("hb%d_%d" % (b, i), [128, D], BF16)) for i in range(3)]
                    junk = ph.enter_context(nc.sbuf_tensor("junk%d" % b, [128, D], BF16))
                    ssq = ph.enter_context(nc.sbuf_tensor("ssq%d" % b, [128, NT], F32))
                    msx = ph.enter_context(nc.sbuf_tensor("msx%d" % b, [128, NT], F32))
                    rsx = ph.enter_context(nc.sbuf_tensor("rsx%d" % b, [128, NT], F32))

                    def p1_stages(i):
                        s_ = i % NXS
                        hs = i % 3
                        pb = i % 2
                        pbf = bank_bf(pb)

                        def s0():
                            sc.dma("sp", "xl%d" % s_, xt[s_][:], x[b, i * 128:(i + 1) * 128, :], writes=["xt%d" % s_])

                        def s1():
                            sc.op("act", lambda: nc.scalar.activation(out=junk[:], in_=xt[s_][:], func=AF.Square,
                                                                      accum_out=ssq[:, i:i + 1]),
                                  reads=["xt%d" % s_], writes=["junk", "ssq%d" % i])
                            sc.op("act", lambda: nc.scalar.activation(out=msx[:, i:i + 1], in_=ssq[:, i:i + 1],
                                                                      func=AF.Ln, bias=EPS, scale=1.0 / D),
                                  reads=["ssq%d" % i], writes=["msx%d" % i])
                            sc.op("act", lambda: nc.scalar.activation(out=rsx[:, i:i + 1], in_=msx[:, i:i + 1],
                                                                      func=AF.Exp, scale=-0.5),
                                  reads=["msx%d" % i], writes=["rsx%d" % i])

                        def s2():
                            sc.op("dve", lambda: nc.vector.scalar_tensor_tensor(out=hb[hs][:], in0=xt[s_][:],
                                                                                scalar=rsx[:, i:i + 1], in1=g_tile[:],
                                                                                op0=ALU.mult, op1=ALU.mult),
                                  reads=["xt%d" % s_, "rsx%d" % i, "g_tile"], writes=["hb%d" % hs])

                        def s3():
                            for c in range(8):
                                sc.op("pe", lambda: nc.tensor.transpose(pbf[:, c * 128:(c + 1) * 128],
                                                                        hb[hs][:, c * 128:(c + 1) * 128], ident[:]),
                                      reads=["hb%d" % hs, "ident"], writes=[BN[pb]], inc=(c == 7))

                        def s4():
                            sc.op("dve", lambda: nc.vector.tensor_copy(out=hT[:, :, i * 128:(i + 1) * 128],
                                                                       in_=pbf.rearrange("p (c t) -> p c t", c=8)),
                                  reads=[BN[pb]], writes=["hT%d" % i])
                        return [s0, s1, s2, s3, s4]

                    p1u = [p1_stages(i) for i in range(NT)]
                    for slot_i in range(NT + 4):
                        for k in range(4, -1, -1):
                            ui_ = slot_i - k
                            if 0 <= ui_ < NT:
                                p1u[ui_][k]()

                pre = {}
                pre["pairs"] = [load_wgroup([O_Q + 128 * p_, O_K + 128 * p_, O_V + 128 * p_, O_GF + 128 * p_]) for p_ in range(2)]
                ph23 = ExitStack()
                bias_all = ph23.enter_context(nc.sbuf_tensor("bias_all%d" % b, [128, NH, NTT, NT], F32))
                dT = ph23.enter_context(nc.sbuf_tensor("dT%d" % b, [NH, S], BF16))
                with ExitStack() as ph:
                    if STOP < 2:
                        raise _Stop()
                    sc.barrier()
                    xf = ph.enter_context(nc.sbuf_tensor("xf%d" % b, [128, NT, NH], F32))
                    ef = ph.enter_context(nc.sbuf_tensor("ef%d" % b, [128, NT, NH], F32))
                    lfn = ph.enter_context(nc.sbuf_tensor("lfn%d" % b, [128, NT, NH], F32))
                    padA = ph.enter_context(nc.sbuf_tensor("padA%d" % b, [128, 8 + NT, NH], F32))
                    padB = ph.enter_context(nc.sbuf_tensor("padB%d" % b, [128, 8 + NT, NH], F32))
                    c_tok = ph.enter_context(nc.sbuf_tensor("c_tok%d" % b, [128, NT, NH], F32))
                    rB = ph.enter_context(nc.sbuf_tensor("rB%d" % b, [128, NT, NH], F32))
                    d_tok = ph.enter_context(nc.sbuf_tensor("d_tok%d" % b, [128, NT, NH], BF16))
                    NF = NT * NH
                    zf = bank(0)[:, 0:NF]
                    for i in range(NT):
                        for c in range(8):
                            sc.op("pe", lambda c=c: nc.tensor.matmul(bank(0)[:, i * NH:(i + 1) * NH],
                                                                     lhsT=hT[:, c, i * 128:(i + 1) * 128], rhs=wf[:, c, :],
                                                                     start=(c == 0), stop=(c == 7)),
                                  reads=["hT%d" % i, "wf"], writes=[BN[0]], inc=(c == 7 and i == NT - 1))
                    sc.op("dve", lambda: nc.vector.tensor_tensor(
                        out=xf[:], in0=zf.rearrange("p (i h) -> p i h", h=NH),
                        in1=bf_tile[:].rearrange("p (o h) -> p o h", o=1).broadcast_to([128, NT, NH]), op=ALU.add),
                        reads=[BN[0], "bf_tile"], writes=["xf"])
                    sc.op("act", lambda: nc.scalar.activation(out=ef[:], in_=xf[:], func=AF.Exp, scale=-1.0),
                          reads=["xf"], writes=["ef"])
                    sc.op("act", lambda: nc.scalar.activation(out=lfn[:], in_=ef[:], func=AF.Ln, bias=1.0, scale=1.0),
                          reads=["ef"], writes=["lfn"])
                    lfn2 = lfn[:].rearrange("p i h -> p (i h)")
                    sc.op("pe", lambda: nc.tensor.matmul(bank(1)[:, 0:NF], lhsT=Uneg[:], rhs=lfn2, start=True, stop=True),
                          reads=["lfn", "Uneg"], writes=[BN[1]], inc=False)
                    sc.op("pe", lambda: nc.tensor.matmul(bank(1)[:, 256:256 + NF], lhsT=OnesNeg[:], rhs=lfn2, start=True,
                                                         stop=True),
                          reads=["lfn", "OnesNeg"], writes=[BN[1]])
                    sc.op("dve", lambda: nc.vector.memset(padA[:], 0.0), writes=["padA"])
                    sc.op("dve", lambda: nc.vector.memset(padB[:], 0.0), writes=["padB"])
                    if NT > 1:
                        sc.op("dve", lambda: nc.vector.tensor_copy(
                            out=padA[:, 9:8 + NT, :],
                            in_=bank(1)[:, 256:256 + NF].rearrange("p (i h) -> p i h", h=NH)[:, 0:NT - 1, :]),
                            reads=[BN[1], "padA"], writes=["padA"])
                    cur, oth, curn, othn = padA, padB, "padA", "padB"
                    dstep = 1
                    while dstep < NT:
                        sc.op("dve", lambda cur=cur, oth=oth, dstep=dstep: nc.vector.tensor_tensor(
                            out=oth[:, 8:8 + NT, :], in0=cur[:, 8:8 + NT, :], in1=cur[:, 8 - dstep:8 + NT - dstep, :],
                            op=ALU.add), reads=[curn], writes=[othn])
                        cur, oth, curn, othn = oth, cur, othn, curn
                        dstep *= 2
                    sc.op("dve", lambda: nc.vector.tensor_tensor(
                        out=c_tok[:], in0=bank(1)[:, 0:NF].rearrange("p (i h) -> p i h", h=NH), in1=cur[:, 8:8 + NT, :],
                        op=ALU.add), reads=[BN[1], curn], writes=["c_tok"])
                    sc.op("pe", lambda: nc.tensor.matmul(bank(0)[:, 0:NF], lhsT=Sel[:],
                                                         rhs=c_tok[:].rearrange("p i h -> p (i h)"), start=True, stop=True),
                          reads=["c_tok", "Sel"], writes=[BN[0]])
                    sc.op("act", lambda: nc.scalar.copy(out=rB[:].rearrange("p i h -> p (i h)"), in_=bank(0)[:, 0:NF]),
                          reads=[BN[0]], writes=["rB"])
                    c4 = c_tok[:].rearrange("p (a j) h -> p a j h", j=4)
                    r4 = rB[:].rearrange("p (a j) h -> p a j h", j=4)[:, :, 3:4, :].broadcast_to([128, NTT, 4, NH])
                    sc.op("dve", lambda: nc.vector.tensor_tensor(
                        out=d_tok[:].rearrange("p (a j) h -> p a j h", j=4), in0=c4, in1=r4, op=ALU.subtract),
                        reads=["c_tok", "rB"], writes=["d_tok"])
                    for T in range(NTT):
                        sc.op("dve", lambda T=T: nc.vector.tensor_tensor(
                            out=bias_all[:, :, T, :],
                            in0=rB[:, 4 * T + 3:4 * T + 4, :].rearrange("p o h -> p h o").broadcast_to([128, NH, NT]),
                            in1=c_tok[:].rearrange("p i h -> p h i"), op=ALU.subtract),
                            reads=["c_tok", "rB"], writes=["bias_all"])
                    dps = PS[1].bitcast(BF16)
                    for i in range(NT):
                        sc.op("pe", lambda i=i: nc.tensor.transpose(dps[0:NH, i * 128:(i + 1) * 128], d_tok[:, i, :],
                                                                    ident[:]),
                              reads=["d_tok", "ident"], writes=[BN[2], BN[3]], inc=(i == NT - 1))
                    sc.op("act", lambda: nc.scalar.copy(out=dT[:], in_=dps[0:NH, 0:S]),
                          reads=[BN[2], BN[3]], writes=["dT"])

                with ExitStack() as ph:
                    if STOP < 3:
                        raise _Stop()
                    sc.barrier()
                    QA = [[ph.enter_context(nc.sbuf_tensor("QA%d_%d_%d" % (b, pp, i), [128, S], BF16)) for i in range(2)] for pp in range(2)]
                    KA = [[ph.enter_context(nc.sbuf_tensor("KA%d_%d_%d" % (b, pp, i), [128, S], BF16)) for i in range(2)] for pp in range(2)]
                    Vaug = [ph.enter_context(nc.sbuf_tensor("Vaug%d_%d" % (b, pp), [128, NT, 2, 128], BF16)) for pp in range(2)]
                    gate = [ph.enter_context(nc.sbuf_tensor("gate%d_%d" % (b, pp), [128, S], BF16)) for pp in range(2)]
                    sq_sb = [ph.enter_context(nc.sbuf_tensor("sq%d_%d" % (b, i), [128, 512], BF16)) for i in range(2)]
                    msb = [ph.enter_context(nc.sbuf_tensor("msb%d_%d" % (b, i), [128, 512], F32)) for i in range(2)]
                    rstd = [ph.enter_context(nc.sbuf_tensor("rstd%d_%d" % (b, i), [128, 512], F32)) for i in range(2)]
                    th_sb = [ph.enter_context(nc.sbuf_tensor("th%d_%d" % (b, i), [128, 512], F32)) for i in range(2)]
                    pbuf = [ph.enter_context(nc.sbuf_tensor("pbuf%d_%d" % (b, i), [128, 512], BF16)) for i in range(4)]
                    rec = [ph.enter_context(nc.sbuf_tensor("rec%d_%d" % (b, i), [128, 512], F32)) for i in range(3)]
                    for pp in range(2):
                        for hh in range(2):
                            t_ = "%d%d" % (pp, hh)
                            sc.op("dve", lambda: nc.vector.memset(KA[pp][hh][64:128, :], 0.0), writes=["KArow" + t_])
                            sc.op("dve", lambda: nc.vector.memset(KA[pp][hh][64:65, :], 1.0), reads=["KArow" + t_],
                                  writes=["KArow" + t_])
                            sc.op("dve", lambda: nc.vector.memset(QA[pp][hh][64:128, :], 0.0), writes=["QArow" + t_])
                        sc.op("dve", lambda: nc.vector.memset(Vaug[pp][:, :, 0, 64:128], 1.0), writes=["Vones%d0" % pp])
                        sc.op("dve", lambda: nc.vector.memset(Vaug[pp][:, :, 1, 0:64], 1.0), writes=["Vones%d1" % pp])

                    nq = {"n": 0}
                    wslots = {}

                    def load_pair(p):
                        wslots[p] = load_wgroup([O_Q + 128 * p, O_K + 128 * p, O_V + 128 * p, O_GF + 128 * p])

                    def proj_units(p):
                        pp = p % 2
                        units = []

                        def v_unit(i0):
                            st = {}

                            def A():
                                cur_slot = wslots[p]
                                vb = PJB[nq["n"] % 5]
                                nq["n"] += 1
                                st["vb"] = vb
                                for ii in range(4):
                                    i = i0 + ii
                                    for c in range(8):
                                        sc.op("pe", lambda: nc.tensor.matmul(
                                            bank(vb)[:, ii * 128:(ii + 1) * 128], lhsT=hT[:, c, i * 128:(i + 1) * 128],
                                            rhs=wbuf[cur_slot][:, c, 256:384], start=(c == 0), stop=(c == 7)),
                                            reads=["hT%d" % i, "wbuf%d_2" % cur_slot], writes=[BN[vb]],
                                            inc=(c == 7 and ii == 3))

                            def B():
                                vb = st["vb"]
                                bv = bank(vb).rearrange("p (i c) -> p i c", i=4)
                                sc.op("act", lambda: nc.scalar.copy(out=Vaug[pp][:, i0:i0 + 4, 0, 0:64], in_=bv[:, :, 0:64]),
                                      reads=[BN[vb], "Vones%d0" % pp], writes=["V%d0_%d" % (pp, i0 // 4)])
                                sc.op("act", lambda: nc.scalar.copy(out=Vaug[pp][:, i0:i0 + 4, 1, 64:128], in_=bv[:, :, 64:128]),
                                      reads=[BN[vb], "Vones%d1" % pp], writes=["V%d1_%d" % (pp, i0 // 4)])
                            return [A, B]

                        def qk_unit(tt, qk):
                            st = {}

                            def A():
                                cur_slot = wslots[p]
                                n = nq["n"]
                                nq["n"] += 1
                                st["u"] = u = n % 2
                                st["bk"] = bk = PJB[n % 5]
                                proj_fm(cur_slot, qk, tt, bk)

                            def A2():
                                u = st["u"]
                                bk = st["bk"]
                                sc.op("act", lambda: nc.scalar.activation(out=sq_sb[u][:], in_=bank(bk), func=AF.Square),
                                      reads=[BN[bk]], writes=["sq%d" % u])

                            def B1():
                                u = st["u"]
                                ab = 6
                                sc.op("pe", lambda: nc.tensor.matmul(bank(ab), lhsT=blockones[:], rhs=sq_sb[u][:], start=True, stop=True),
                                      reads=["sq%d" % u, "blockones"], writes=[BN[ab]])
                                sc.op("act", lambda: nc.scalar.activation(out=msb[u][:], in_=bank(ab), func=AF.Ln, bias=EPS, scale=1.0),
                                      reads=[BN[ab]], writes=["msb%d" % u])

                            def B2():
                                u = st["u"]
                                sc.op("act", lambda: nc.scalar.activation(out=rstd[u][:], in_=msb[u][:], func=AF.Exp, scale=-0.5),
                                      reads=["msb%d" % u], writes=["rstd%d" % u])

                            def C():
                                u = st["u"]
                                bk = st["bk"]
                                dst = QA[pp] if qk == 0 else KA[pp]
                                gsc = gq8 if qk == 0 else gk
                                nm = "QA" if qk == 0 else "KA"
                                for hh in range(2):
                                    r0 = hh * 64
                                    sc.op("dve", lambda: nc.vector.scalar_tensor_tensor(
                                        out=dst[hh][0:64, tt * 512:(tt + 1) * 512], in0=bank(bk)[r0:r0 + 64, :],
                                        scalar=gsc[r0:r0 + 64, p:p + 1], in1=rstd[u][r0:r0 + 64, :], op0=ALU.mult, op1=ALU.mult),
                                        reads=[BN[bk], "rstd%d" % u, "gq8", "gk"], writes=["%s%d%d_%d" % (nm, pp, hh, tt)])
                            return [A, A2, B1, B2, C]

                        def gate_unit(tt):
                            st = {}

                            def A():
                                cur_slot = wslots[p]
                                n = nq["n"]
                                nq["n"] += 1
                                st["u"] = u = n % 2
                                st["bk"] = bk = PJB[n % 5]
                                proj_fm(cur_slot, 3, tt, bk)

                            def A2():
                                u = st["u"]
                                bk = st["bk"]
                                sc.op("act", lambda: nc.scalar.activation(out=th_sb[u][:], in_=bank(bk), func=AF.Tanh, scale=0.5),
                                      reads=[BN[bk]], writes=["th%d" % u])

                            def B():
                                u = st["u"]
                                bk = st["bk"]
                                sc.op("dve", lambda: nc.vector.scalar_tensor_tensor(
                                    out=gate[pp][:, tt * 512:(tt + 1) * 512], in0=th_sb[u][:], scalar=1.0, in1=bank(bk),
                                    op0=ALU.add, op1=ALU.mult),
                                    reads=[BN[bk], "th%d" % u], writes=["gate%d_%d" % (pp, tt)])
                            return [A, A2, B]

                        def drow_unit():
                            def A():
                                for hh in range(2):
                                    h = 2 * p + hh
                                    sc.dma("sp", "drow%d%d" % (pp, hh), QA[pp][hh][64:65, :], dT[h:h + 1, :], reads=["dT"],
                                           writes=["QArow%d%d" % (pp, hh)])
                            return [A]

                        ulist = [drow_unit()]
                        for i0 in range(0, NT, 4):
                            ulist.append(v_unit(i0))
                        for tt in range(NTT):
                            for qk in range(2):
                                ulist.append(qk_unit(tt, qk))
                        for tt in range(NTT):
                            ulist.append(gate_unit(tt))
                        flat = [lambda: nq.__setitem__("n", 0)]
                        nu = len(ulist)
                        mx = max(len(u_) for u_ in ulist)
                        for slot_i in range(nu + mx - 1):
                            for k in range(mx - 1, -1, -1):
                                ui_ = slot_i - k
                                if 0 <= ui_ < nu and k < len(ulist[ui_]):
                                    flat.append(ulist[ui_][k])
                        return flat

                    def attention(p, next_units):
                        pp = p % 2
                        steps = []
                        for hh in range(2):
                            for T in range(NTT):
                                for J in range(4 * T + 4):
                                    steps.append((hh, T, J))

                        def emit_S(si):
                            hh, T, J = steps[si]
                            j = J - 4 * T
                            col0 = 128 * j if j >= 0 else 0
                            sbk = (2, 6, 3)[si % 3]
                            sc.op("pe", lambda: nc.tensor.matmul(
                                bank(sbk)[:, col0:512], lhsT=KA[pp][hh][:, J * 128:(J + 1) * 128],
                                rhs=QA[pp][hh][:, T * 512 + col0:(T + 1) * 512], start=True, stop=(j < 0)),
                                reads=["KA%d%d_%d" % (pp, hh, J // 4), "KArow%d%d" % (pp, hh), "QA%d%d_%d" % (pp, hh, T),
                                       "QArow%d%d" % (pp, hh)],
                                writes=[BN[sbk]], inc=(j < 0))
                            if j >= 0:
                                sc.op("pe", lambda: nc.tensor.matmul(
                                    bank(sbk)[:, col0:col0 + 128], lhsT=ident[:], rhs=maskT[:], start=False, stop=True),
                                    reads=["ident", "maskT"], writes=[BN[sbk]])

                        def emit_exp(si):
                            hh, T, J = steps[si]
                            h = 2 * p + hh
                            j = J - 4 * T
                            col0 = 128 * j if j >= 0 else 0
                            sbk = (2, 6, 3)[si % 3]
                            pi = si % 4
                            sc.op("act", lambda: nc.scalar.activation(
                                out=pbuf[pi][:, col0:512], in_=bank(sbk)[:, col0:512], func=AF.Exp,
                                bias=bias_all[:, h, T, J:J + 1], scale=1.0),
                                reads=[BN[sbk], "bias_all"], writes=["pbuf%d" % pi])

                        def emit_PV(si):
                            hh, T, J = steps[si]
                            j = J - 4 * T
                            col0 = 128 * j if j >= 0 else 0
                            pi = si % 4
                            acc_n = (hh * NTT + T) % 3
                            ab = (4, 5, 0)[acc_n]
                            last = (J == 4 * T + 3)
                            sc.op("pe", lambda: nc.tensor.matmul(
                                bank(ab)[:, col0:512], lhsT=Vaug[pp][:, J, hh, :], rhs=pbuf[pi][:, col0:512],
                                start=(J == 0), stop=last),
                                reads=["pbuf%d" % pi, "V%d%d_%d" % (pp, hh, J // 4), "Vones%d%d" % (pp, hh)], writes=[BN[ab]],
                                inc=last)
                            if last:
                                num0, den0 = (0, 64) if hh == 0 else (64, 0)
                                ri = acc_n
                                sc.op("dve", lambda: nc.vector.reciprocal(out=rec[ri][num0:num0 + 64, :],
                                                                          in_=bank(ab)[den0:den0 + 64, :]),
                                      reads=[BN[ab]], writes=["rec%d" % ri])
                                sc.op("pool", lambda: nc.gpsimd.tensor_tensor(
                                    out=rec[ri][num0:num0 + 64, :], in0=rec[ri][num0:num0 + 64, :],
                                    in1=gate[pp][num0:num0 + 64, T * 512:(T + 1) * 512], op=ALU.mult),
                                    reads=["rec%d" % ri, "gate%d_%d" % (pp, T)], writes=["rec%d" % ri])
                                sc.op("dve", lambda: nc.vector.scalar_tensor_tensor(
                                    out=yT[num0:num0 + 64, p, T * 512:(T + 1) * 512], in0=bank(ab)[num0:num0 + 64, :],
                                    scalar=0.5, in1=rec[ri][num0:num0 + 64, :], op0=ALU.mult, op1=ALU.mult),
                                    reads=[BN[ab], "rec%d" % ri], writes=["yT%d_%d_%d" % (p, hh, T)])

                        nst = len(steps)
                        nun = len(next_units)
                        stride = max(1, nst // (nun + 1)) if nun else nst
                        ui = 0
                        emit_S(0)
                        if nst > 1:
                            emit_S(1)
                        for si in range(nst):
                            emit_exp(si)
                            if si + 2 < nst:
                                emit_S(si + 2)
                            emit_PV(si)
                            if INTERLEAVE and ui < nun and (si + 1) % stride == 0:
                                next_units[ui]()
                                ui += 1
                        while ui < nun:
                            next_units[ui]()
                            ui += 1

                    wslots[0], wslots[1] = pre["pairs"]
                    for un in proj_units(0):
                        un()
                    for p in range(8):
                        if p + 2 < 8:
                            load_pair(p + 2)
                        if p == 7:
                            pre["glu"] = [load_wgroup([O_GLU + 128 * c_, O_GLU + 1024 + 128 * c_]) for c_ in range(2)]
                        attention(p, proj_units(p + 1) if p + 1 < 8 else [])

                ph23.close()
                with ExitStack() as ph:
                    if STOP < 4:
                        raise _Stop()
                    sc.barrier()
                    wout = ph.enter_context(nc.sbuf_tensor("wout%d" % b, [128, 16, D], BF16))
                    for half in range(2):
                        sc.dma("pool", "wo%d" % half, wout[:, half * 8:(half + 1) * 8, :], w_out_v[:, half * 8:(half + 1) * 8, :],
                               writes=["wout%d" % half])
                    with ExitStack() as ph4:
                        with ExitStack() as ph4a:
                            uT = [ph4a.enter_context(nc.sbuf_tensor("uT%d_%d" % (b, i), [128, S + 32], BF16)) for i in range(2)]
                            Wd = [ph4a.enter_context(nc.sbuf_tensor("Wd%d_%d" % (b, i), [128, KC, 128], BF16)) for i in range(2)]
                            tb = [ph4a.enter_context(nc.sbuf_tensor("tb%d_%d" % (b, i), [128, 512], F32)) for i in range(2)]
                            cacc = [ph4a.enter_context(nc.sbuf_tensor("cacc%d_%d" % (b, i), [128, 512], F32)) for i in range(2)]
                            for i in range(2):
                                sc.op("pool", lambda i=i: nc.gpsimd.memset(uT[i][:, 0:32], 0.0), writes=["uTpad%d" % i])
                            def build_wd(ch):
                                us = ch % 2
                                sc.op("pool", lambda: nc.gpsimd.tensor_tensor(
                                    out=Wd[us][:],
                                    in0=ident[:].rearrange("p (o n) -> p o n", o=1).broadcast_to([128, KC, 128]),
                                    in1=convwh[:, ch, :].rearrange("p (k o) -> p k o", o=1).broadcast_to([128, KC, 128]),
                                    op=ALU.mult), reads=["ident", "convwh"], writes=["Wd%d" % us])

                            def glu_pe(ch, tts, slot_):
                                for tt in tts:
                                    ba = tt % 2
                                    bb = 6 + (tt % 2)
                                    u = tt % 2
                                    proj_fm(slot_, 0, tt, ba)
                                    proj_fm(slot_, 1, tt, bb)
                                    sc.op("act", lambda: nc.scalar.activation(out=tb[u][:], in_=bank(bb), func=AF.Tanh, scale=0.5),
                                          reads=[BN[bb]], writes=["tb%d" % u])

                            def glu_dve(ch, tts):
                                us = ch % 2
                                for tt in tts:
                                    ba = tt % 2
                                    u = tt % 2
                                    sc.op("dve", lambda: nc.vector.scalar_tensor_tensor(
                                        out=uT[us][:, 30 + tt * 512:30 + (tt + 1) * 512], in0=tb[u][:], scalar=1.0,
                                        in1=bank(ba), op0=ALU.add, op1=ALU.mult),
                                        reads=[BN[ba], "tb%d" % u, "uTpad%d" % us], writes=["uT%d_%d" % (us, tt)])

                            def glu_proj(ch, tts, slot_):
                                for tt in tts:
                                    glu_pe(ch, [tt], slot_)
                                    glu_dve(ch, [tt])

                            def taps(ch, tts, rds, k0, k1):
                                us = ch % 2
                                for k in range(k0, k1):
                                    for tt in tts:
                                        ca = cacc[tt % 2]
                                        can = "cacc%d" % (tt % 2)
                                        if k == NPE:
                                            sc.op("dve", lambda: nc.vector.tensor_scalar(
                                                out=ca[:], in0=uT[us][:, tt * 512 + k:tt * 512 + k + 512],
                                                scalar1=convwh[:, ch, k:k + 1], scalar2=None, op0=ALU.mult),
                                                reads=rds[tt] + ["convwh"], writes=[can])
                                        else:
                                            sc.op("dve", lambda: nc.vector.scalar_tensor_tensor(
                                                out=ca[:], in0=uT[us][:, tt * 512 + k:tt * 512 + k + 512],
                                                scalar=convwh[:, ch, k:k + 1], in1=ca[:], op0=ALU.mult, op1=ALU.add),
                                                reads=rds[tt] + ["convwh", can], writes=[can])

                            gslot = {0: pre["glu"][0], 1: pre["glu"][1]}
                            build_wd(0)
                            glu_proj(0, list(range(NTT)), gslot[0])
                            KMID = (NPE + KC) // 2
                            for ch in range(8):
                                us = ch % 2
                                if ch == 7:
                                    pre["gc"] = [load_wgroup([O_GC + 512 * gi], width=512) for gi in range(2)]
                                if ch + 2 < 8:
                                    gslot[ch + 2] = load_wgroup([O_GLU + 128 * (ch + 2), O_GLU + 1024 + 128 * (ch + 2)])
                                if ch + 1 < 8:
                                    build_wd(ch + 1)
                                for tp in range(0, NTT, 2):
                                    tts = [t for t in (tp, tp + 1) if t < NTT]
                                    rds = {}
                                    if ch + 1 < 8:
                                        glu_pe(ch + 1, tts, gslot[ch + 1])
                                    for tt in tts:
                                        cb = 2 + (tt % 2)
                                        rd = ["Wd%d" % us, "uTpad%d" % us, "uT%d_%d" % (us, tt)]
                                        if tt > 0:
                                            rd.append("uT%d_%d" % (us, tt - 1))
                                        rds[tt] = rd
                                        for k in range(NPE):
                                            sc.op("pe", lambda k=k: nc.tensor.matmul(
                                                bank(cb), lhsT=Wd[us][:, k, :], rhs=uT[us][:, tt * 512 + k:tt * 512 + k + 512],
                                                start=(k == 0), stop=(k == NPE - 1)),
                                                reads=rd, writes=[BN[cb]], inc=(k == NPE - 1))
                                    taps(ch, tts, rds, NPE, KMID)
                                    if ch + 1 < 8:
                                        glu_dve(ch + 1, tts)
                                    taps(ch, tts, rds, KMID, KC)
                                    for tt in tts:
                                        cb = 2 + (tt % 2)
                                        ca = cacc[tt % 2]
                                        can = "cacc%d" % (tt % 2)
                                        sc.op("dve", lambda: nc.vector.scalar_tensor_tensor(
                                            out=yT[:, 8 + ch, tt * 512:(tt + 1) * 512], in0=bank(cb),
                                            scalar=convb[:, ch:ch + 1], in1=ca[:], op0=ALU.add, op1=ALU.add),
                                            reads=[BN[cb], "convb", can], writes=["yc%d_%d" % (ch, tt)])
                        sc.barrier()
                        sq2 = [ph4.enter_context(nc.sbuf_tensor("sqc%d_%d" % (b, i), [128, 512], BF16)) for i in range(2)]
                        mean_sb = ph4.enter_context(nc.sbuf_tensor("mean%d" % b, [128, S], F32))
                        rstd_sb = ph4.enter_context(nc.sbuf_tensor("rstdc%d" % b, [128, S], F32))
                        t1 = [ph4.enter_context(nc.sbuf_tensor("t1_%d_%d" % (b, i), [128, 512], F32)) for i in range(NSET)]
                        th2 = [ph4.enter_context(nc.sbuf_tensor("th2_%d_%d" % (b, i), [128, 512], F32)) for i in range(2)]
                        vq = [ph4.enter_context(nc.sbuf_tensor("vq_%d_%d" % (b, i), [128, 512], F32)) for i in range(NSET)]
                        thc = [ph4.enter_context(nc.sbuf_tensor("thc_%d_%d" % (b, i), [128, 512], F32)) for i in range(NSET)]
                        gcv = [ph4.enter_context(nc.sbuf_tensor("gcv_%d_%d" % (b, i), [128, 512], F32)) for i in range(NSET)]
                        m2 = t1[0]
                        vare = t1[1]
                        for tt in range(NTT):
                            for ch in range(8):
                                sc.op("pe", lambda ch=ch: nc.tensor.matmul(
                                    bank(4), lhsT=onesmean[:], rhs=yT[:, 8 + ch, tt * 512:(tt + 1) * 512],
                                    start=(ch == 0), stop=(ch == 7)),
                                    reads=["onesmean", "yc%d_%d" % (ch, tt)], writes=[BN[4]], inc=(ch == 7))
                            for ch in range(8):
                                qi = ch % 2
                                sc.op("act", lambda ch=ch, qi=qi: nc.scalar.activation(
                                    out=sq2[qi][:], in_=yT[:, 8 + ch, tt * 512:(tt + 1) * 512], func=AF.Square),
                                    reads=["yc%d_%d" % (ch, tt)], writes=["sqc%d" % qi])
                                sc.op("pe", lambda ch=ch, qi=qi: nc.tensor.matmul(
                                    bank(5), lhsT=onesmean[:], rhs=sq2[qi][:], start=(ch == 0), stop=(ch == 7)),
                                    reads=["onesmean", "sqc%d" % qi], writes=[BN[5]], inc=True)
                            sc.op("act", lambda: nc.scalar.copy(out=mean_sb[:, tt * 512:(tt + 1) * 512], in_=bank(4)),
                                  reads=[BN[4]], writes=["mean%d" % tt])
                            sc.op("dve", lambda: nc.vector.tensor_tensor(
                                out=m2[:], in0=mean_sb[:, tt * 512:(tt + 1) * 512],
                                in1=mean_sb[:, tt * 512:(tt + 1) * 512], op=ALU.mult),
                                reads=["mean%d" % tt], writes=["t1_0"])
                            sc.op("dve", lambda: nc.vector.scalar_tensor_tensor(
                                out=vare[:], in0=bank(5), scalar=EPS, in1=m2[:], op0=ALU.add, op1=ALU.subtract),
                                reads=[BN[5], "t1_0"], writes=["t1_1"])
                            sc.op("act", lambda: nc.scalar.activation(out=vare[:], in_=vare[:], func=AF.Ln),
                                  reads=["t1_1"], writes=["t1_1"])
                            sc.op("act", lambda: nc.scalar.activation(
                                out=rstd_sb[:, tt * 512:(tt + 1) * 512], in_=vare[:], func=AF.Exp, scale=-0.5),
                                reads=["t1_1"], writes=["rstdc%d" % tt])
                        gslots = pre["gc"]
                        blocks = [(gi, c4, tt) for tt in range(NTT) for gi in range(2) for c4 in range(4)]
                        xo = [ph4.enter_context(nc.sbuf_tensor("xo%d_%d" % (b, i), [128, D], F32)) for i in range(2)]
                        ot = [ph4.enter_context(nc.sbuf_tensor("ot%d_%d" % (b, i), [128, D], F32)) for i in range(1)]

                        def og(i, half):
                            s_ = i % 2
                            T = i // 4
                            ob = 4 + half
                            if half == 0:
                                sc.dma("sp", "xm%d" % s_, xo[s_][:], x[b, i * 128:(i + 1) * 128, :], writes=["xo%d" % s_])
                            for e in range(16):
                                if e < 8:
                                    rd = ["yT%d_0_%d" % (e, T), "yT%d_1_%d" % (e, T)]
                                else:
                                    rd = ["yc%d_%d" % (e - 8, T)]
                                sc.op("pe", lambda: nc.tensor.matmul(
                                    bank(ob), lhsT=yT[:, e, i * 128:(i + 1) * 128],
                                    rhs=wout[:, e, half * 512:(half + 1) * 512], start=(e == 0), stop=(e == 15)),
                                    reads=rd + ["wout%d" % (e // 8)], writes=[BN[ob]], inc=(e == 15))
                            sc.op("dve", lambda: nc.vector.tensor_tensor(
                                out=ot[0][:, half * 512:(half + 1) * 512], in0=bank(ob),
                                in1=xo[s_][:, half * 512:(half + 1) * 512], op=ALU.add),
                                reads=[BN[ob], "xo%d" % s_], writes=["ot_%d" % half])
                            if half == 1:
                                sc.dma("sp", "ost", out[b, i * 128:(i + 1) * 128, :], ot[0][:], reads=["ot_0", "ot_1"])

                        ogs = [(i, half) for i in range(NT) for half in range(2)]

                        LB = (0, 1, 2, 3, 6, 7)

                        def ln_stages(n):
                            gi, c4, tt = blocks[n]
                            ch = gi * 4 + c4
                            u = n % NSET
                            pbk = LB[n % 6]
                            ysl = yT[:, 8 + ch, tt * 512:(tt + 1) * 512]

                            def s0():
                                proj_fm(gslots[gi], c4, tt, pbk)

                            def s1():
                                sc.op("act", lambda: nc.scalar.activation(out=thc[u][:], in_=bank(pbk), func=AF.Tanh, scale=0.5),
                                      reads=[BN[pbk]], writes=["thc%d" % u])
                                sc.op("dve", lambda: nc.vector.tensor_tensor(
                                    out=t1[u][:], in0=ysl, in1=mean_sb[:, tt * 512:(tt + 1) * 512], op=ALU.subtract),
                                    reads=["yc%d_%d" % (ch, tt), "mean%d" % tt], writes=["t1_%d" % u])

                            def s2():
                                sc.op("pool", lambda: nc.gpsimd.tensor_tensor(
                                    out=t1[u][:], in0=t1[u][:], in1=rstd_sb[:, tt * 512:(tt + 1) * 512], op=ALU.mult),
                                    reads=["t1_%d" % u, "rstdc%d" % tt], writes=["t1_%d" % u])

                            def s3():
                                sc.op("act", lambda: nc.scalar.activation(
                                    out=th2[n % 2][:], in_=t1[u][:], func=AF.Tanh, bias=lnb_h[:, ch:ch + 1],
                                    scale=lng_h[:, ch:ch + 1]),
                                    reads=["t1_%d" % u, "lng_h", "lnb_h"], writes=["th2_%d" % (n % 2)])
                                sc.op("dve", lambda: nc.vector.scalar_tensor_tensor(
                                    out=gcv[u][:], in0=thc[u][:], scalar=1.0, in1=bank(pbk), op0=ALU.add,
                                    op1=ALU.mult), reads=[BN[pbk], "thc%d" % u], writes=["gcv%d" % u])
                                sc.op("act", lambda: nc.scalar.activation(
                                    out=vq[u][:], in_=t1[u][:], func=AF.Identity, bias=lnb_q[:, ch:ch + 1],
                                    scale=lng_q[:, ch:ch + 1]),
                                    reads=["t1_%d" % u, "lng_q", "lnb_q"], writes=["vq%d" % u])

                            def s4():
                                sc.op("dve", lambda: nc.vector.scalar_tensor_tensor(
                                    out=vq[u][:], in0=th2[n % 2][:], scalar=1.0, in1=vq[u][:], op0=ALU.add, op1=ALU.mult),
                                    reads=["th2_%d" % (n % 2), "vq%d" % u], writes=["vq%d" % u])

                            def s5():
                                if n % 2 == 0:
                                    sc.op("pool", lambda: nc.gpsimd.tensor_tensor(
                                        out=ysl, in0=vq[u][:], in1=gcv[u][:], op=ALU.mult),
                                        reads=["vq%d" % u, "gcv%d" % u], writes=["yc%d_%d" % (ch, tt)])
                                else:
                                    sc.op("dve", lambda: nc.vector.tensor_tensor(
                                        out=ysl, in0=vq[u][:], in1=gcv[u][:], op=ALU.mult),
                                        reads=["vq%d" % u, "gcv%d" % u], writes=["yc%d_%d" % (ch, tt)])
                            return [s0, s1, s2, s3, s4, s5]

                        lnu = [ln_stages(n) for n in range(len(blocks))]
                        oq = 0
                        for slot_i in range(len(lnu) + 5):
                            for k in range(5, -1, -1):
                                ui_ = slot_i - k
                                if 0 <= ui_ < len(lnu):
                                    lnu[ui_][k]()
                            if slot_i >= 12 and oq < len(ogs) and oq <= slot_i - 12:
                                og(*ogs[oq])
                                oq += 1
                        while oq < len(ogs):
                            og(*ogs[oq])
                            oq += 1
          except _Stop:
            stopped = True
            break
        sc.wait_all("sp")
        if stopped:
            es.pop_all()
    return nc


def _prep_shared(inp):
    f = np.float32
    qg = np.asarray(inp["q_norm_g"], f)[0]
    kg = np.asarray(inp["k_norm_g"], f)[0]
    gq = np.ascontiguousarray(qg.reshape(8, 128).T)
    gk = np.ascontiguousarray(kg.reshape(8, 128).T)
    cw = np.asarray(inp["conv_w"], f)[0]
    convw = np.ascontiguousarray(cw.T.reshape(8, 128, KC).transpose(1, 0, 2))

    def pc(a):
        return np.ascontiguousarray(np.asarray(a, f)[0].reshape(8, 128).T)

    return {
        "w_in": np.ascontiguousarray(np.asarray(inp["w_in"], f)[0]),
        "w_out": np.ascontiguousarray(np.asarray(inp["w_out"], f)[0]),
        "norm_g": np.ascontiguousarray(np.asarray(inp["norm_g"], f)),
        "b_forget": np.ascontiguousarray(np.asarray(inp["b_forget"], f)),
        "gq": gq, "gk": gk, "convw": convw,
        "convb": pc(inp["conv_b"]), "lng": pc(inp["conv_ln_g"]), "lnb": pc(inp["conv_ln_b"]),
    }


def kernel(**inputs):
    x = np.asarray(inputs["x"], np.float32)
    B, S, _ = x.shape
    n = N_CORES
    nb = B // n
    shared = _prep_shared(inputs)
    nc = build_nc(NB=nb, S=S)
    in_maps = []
    for c in range(n):
        m = dict(shared)
        m["x"] = np.ascontiguousarray(x[c * nb:(c + 1) * nb])
        in_maps.append(m)
    res = run_bass_kernel_spmd(nc, in_maps, core_ids=list(range(n)))
    return np.concatenate([np.asarray(r["out"]) for r in res.results], axis=0).astype(np.float32)
```

```python
import numpy as np
from contextlib import ExitStack
import concourse.bass as bass
import concourse.mybir as mybir
from concourse.bass_utils import run_bass_kernel_spmd

F32 = mybir.dt.float32
BF16 = mybir.dt.bfloat16
AF = mybir.ActivationFunctionType
ALU = mybir.AluOpType

D = 1024
NH = 16
HD = 64
KC = 31
IN_COLS = 7184
O_Q, O_K, O_V, O_F, O_GF, O_GLU, O_GC = 0, 1024, 2048, 3072, 3088, 4112, 6160
EPS = 1e-6
MASKV = -30000.0
N_CORES = 8
INTERLEAVE = False
PJB = (1, 7, 3, 0, 5)
NSET = 3
NPE = 23


class Sched:
    def __init__(self, nc, es):
        self.nc = nc
        self.eng = {"pe": nc.tensor, "act": nc.scalar, "dve": nc.vector, "pool": nc.gpsimd, "sp": nc.sync}
        self.sem = {k: es.enter_context(nc.semaphore("s_" + k)) for k in self.eng}
        self.cnt = {k: 0 for k in self.eng}
        self.seen = {k: {} for k in self.eng}
        self.dsem = {}
        self.dcnt = {}
        self.es = es
        self.res = {}
        self.pending_noinc = {k: False for k in self.eng}

    def _semh(self, key):
        return self.sem[key] if key in self.sem else self.dsem[key]

    def _wait(self, engine, tok):
        key, val = tok
        if self.seen[engine].get(key, 0) >= val:
            return
        self.seen[engine][key] = val
        self.eng[engine].wait_ge(self._semh(key), val)

    def _deps(self, engine, reads, writes):
        for r in reads:
            st = self.res.get(r)
            if st is not None and st["w"] is not None:
                if not (engine == "pe" and st["w"][0] == "pe"):
                    self._wait(engine, st["w"])
        for w in writes:
            st = self.res.get(w)
            if st is None:
                continue
            if st["w"] is not None and not (engine == "pe" and st["w"][0] == "pe"):
                self._wait(engine, st["w"])
            for k, v in st["r"].items():
                if k == engine and engine == "pe":
                    continue
                self._wait(engine, (k, v))

    def _record(self, tok, reads, writes):
        for r in reads:
            st = self.res.setdefault(r, {"w": None, "r": {}})
            if st["r"].get(tok[0], 0) < tok[1]:
                st["r"][tok[0]] = tok[1]
        for w in writes:
            self.res[w] = {"w": tok, "r": {}}

    def op(self, engine, fn, reads=(), writes=(), inc=True):
        self._deps(engine, reads, writes)
        ins = fn()
        if inc:
            self.cnt[engine] += 1
            ins.then_inc(self.sem[engine], 1)
            tok = (engine, self.cnt[engine])
            self.pending_noinc[engine] = False
        else:
            tok = (engine, self.cnt[engine] + 1)
            self.pending_noinc[engine] = True
        self._record(tok, reads, writes)
        return tok

    def dma(self, queue, semname, out, in_, reads=(), writes=()):
        if semname not in self.dsem:
            self.dsem[semname] = self.es.enter_context(self.nc.semaphore("d_" + semname))
            self.dcnt[semname] = 0
        self._deps(queue, reads, writes)
        ins = self.eng[queue].dma_start(out=out, in_=in_)
        self.dcnt[semname] += 16
        ins.then_inc(self.dsem[semname], 16)
        tok = (semname, self.dcnt[semname])
        self._record(tok, reads, writes)
        return tok

    def retoken(self, names, tok):
        for n in names:
            self.res[n] = {"w": tok, "r": {}}

    def barrier(self):
        assert not self.pending_noinc["pe"]
        for e in self.eng:
            if e != "pe":
                self.wait_all(e)

    def wait_all(self, engine):
        for k in self.eng:
            if k != engine and self.cnt[k] > 0:
                self._wait(engine, (k, self.cnt[k]))
        for k, v in self.dcnt.items():
            if v > 0:
                self._wait(engine, (k, v))


class _Stop(Exception):
    pass


def build_nc(NB=2, S=2048, STOP=9):
    NT = S // 128
    NTT = S // 512
    nc = bass.Bass("TRN2", target_bir_lowering=False)
    x = nc.dram_tensor("x", [NB, S, D], F32, kind="ExternalInput").ap()
    w_in = nc.dram_tensor("w_in", [D, IN_COLS], F32, kind="ExternalInput").ap()
    w_out = nc.dram_tensor("w_out", [2 * D, D], F32, kind="ExternalInput").ap()
    norm_g = nc.dram_tensor("norm_g", [1, D], F32, kind="ExternalInput").ap()
    b_forget = nc.dram_tensor("b_forget", [1, NH], F32, kind="ExternalInput").ap()
    gq_d = nc.dram_tensor("gq", [128, 8], F32, kind="ExternalInput").ap()
    gk_d = nc.dram_tensor("gk", [128, 8], F32, kind="ExternalInput").ap()
    convw_d = nc.dram_tensor("convw", [128, 8, KC], F32, kind="ExternalInput").ap()
    convb_d = nc.dram_tensor("convb", [128, 8], F32, kind="ExternalInput").ap()
    lng_d = nc.dram_tensor("lng", [128, 8], F32, kind="ExternalInput").ap()
    lnb_d = nc.dram_tensor("lnb", [128, 8], F32, kind="ExternalInput").ap()
    out = nc.dram_tensor("out", [NB, S, D], F32, kind="ExternalOutput").ap()

    w_in_v = w_in.rearrange("(c p) n -> p c n", p=128)
    w_out_v = w_out.rearrange("(c p) n -> p c n", p=128)

    with ExitStack() as es:
        sc = Sched(nc, es)

        def sb(name, shape, dt):
            return es.enter_context(nc.sbuf_tensor(name, shape, dt))

        ident = sb("ident", [128, 128], BF16)
        maskT = sb("maskT", [128, 128], BF16)
        blockones = sb("blockones", [128, 128], BF16)
        onesmean = sb("onesmean", [128, 128], BF16)
        Uneg = sb("Uneg", [128, 128], F32)
        OnesNeg = sb("OnesNeg", [128, 128], F32)
        Sel = sb("Sel", [128, 128], F32)
        bf_tile = sb("bf_tile", [128, NH], F32)
        gq = sb("gq_sb", [128, 8], F32)
        gk = sb("gk_sb", [128, 8], F32)
        gq8 = sb("gq8", [128, 8], F32)
        convw = sb("convw_sb", [128, 8, KC], F32)
        convwh = sb("convwh", [128, 8, KC], F32)
        convb = sb("convb_sb", [128, 8], F32)
        lng = sb("lng_sb", [128, 8], F32)
        lnb = sb("lnb_sb", [128, 8], F32)
        lng_h = sb("lng_h", [128, 8], F32)
        lnb_h = sb("lnb_h", [128, 8], F32)
        lng_q = sb("lng_q", [128, 8], F32)
        lnb_q = sb("lnb_q", [128, 8], F32)
        wf = sb("wf", [128, 8, NH], BF16)
        hT = sb("hT", [128, 8, S], BF16)
        yT = sb("yT", [128, 16, S], BF16)
        wbuf = [sb("wbuf%d" % i, [128, 8, 512], BF16) for i in range(2)]

        PS = [es.enter_context(nc.psum_tensor("ps%d" % i, [128, 1024], F32)) for i in range(4)]

        def bank(b):
            return PS[b // 2][:, (b % 2) * 512:(b % 2) * 512 + 512]

        def bank_bf(b):
            t = PS[b // 2].bitcast(BF16)
            return t[:, (b % 2) * 1024:(b % 2) * 1024 + 1024]

        BN = ["B%d" % i for i in range(8)]

        g = nc.gpsimd
        sc.op("pool", lambda: g.memset(ident[:], 1.0), writes=["ident"])
        sc.op("pool", lambda: g.affine_select(out=ident[:], in_=ident[:], compare_op=ALU.is_equal, fill=0.0,
                                              base=0, pattern=[[1, 128]], channel_multiplier=-1),
              reads=["ident"], writes=["ident"])
        sc.op("pool", lambda: g.memset(maskT[:], 0.0), writes=["maskT"])
        sc.op("pool", lambda: g.affine_select(out=maskT[:], in_=maskT[:], compare_op=ALU.is_ge, fill=MASKV,
                                              base=0, pattern=[[1, 128]], channel_multiplier=-1),
              reads=["maskT"], writes=["maskT"])
        sc.op("pool", lambda: g.memset(blockones[:], 0.0), writes=["blockones"])
        sc.op("pool", lambda: g.memset(blockones[0:64, 0:64], 1.0 / 64), reads=["blockones"], writes=["blockones"])
        sc.op("pool", lambda: g.memset(blockones[64:128, 64:128], 1.0 / 64), reads=["blockones"], writes=["blockones"])
        sc.op("pool", lambda: g.memset(onesmean[:], 1.0 / 1024), writes=["onesmean"])
        sc.op("pool", lambda: g.memset(Uneg[:], -1.0), writes=["Uneg"])
        sc.op("pool", lambda: g.affine_select(out=Uneg[:], in_=Uneg[:], compare_op=ALU.is_ge, fill=0.0,
                                              base=0, pattern=[[1, 128]], channel_multiplier=-1),
              reads=["Uneg"], writes=["Uneg"])
        sc.op("pool", lambda: g.memset(OnesNeg[:], -1.0), writes=["OnesNeg"])
        sc.op("pool", lambda: g.memset(Sel[:], 1.0), writes=["Sel"])
        sc.op("pool", lambda: g.affine_select(out=Sel[:], in_=Sel[:], compare_op=ALU.is_equal, fill=0.0,
                                              base=-127, pattern=[[0, 128]], channel_multiplier=1),
              reads=["Sel"], writes=["Sel"])

        sc.dma("sp", "par", bf_tile[:], b_forget.partition_broadcast(128).rearrange("p o n -> p (o n)"), writes=["bf_tile"])
        sc.dma("sp", "par", gq[:], gq_d, writes=["gq"])
        sc.dma("sp", "par", gk[:], gk_d, writes=["gk"])
        sc.dma("sp", "par", convw[:], convw_d, writes=["convw"])
        sc.dma("sp", "par", convb[:], convb_d, writes=["convb"])
        sc.dma("sp", "par", lng[:], lng_d, writes=["lng"])
        sc.dma("sp", "par", lnb[:], lnb_d, writes=["lnb"])
        sc.retoken(["bf_tile", "gq", "gk", "convw", "convb", "lng", "lnb"], ("par", sc.dcnt["par"]))
        sc.dma("pool", "wfl", wf[:], w_in_v[:, :, O_F:O_F + NH], writes=["wf"])
        v = nc.vector
        sc.op("dve", lambda: v.tensor_scalar(out=gq8[:], in0=gq[:], scalar1=0.125, scalar2=None, op0=ALU.mult),
              reads=["gq"], writes=["gq8"])
        sc.op("dve", lambda: v.tensor_scalar(out=convwh[:], in0=convw[:], scalar1=0.5, scalar2=None, op0=ALU.mult),
              reads=["convw"], writes=["convwh"])
        sc.op("dve", lambda: v.tensor_scalar(out=lng_h[:], in0=lng[:], scalar1=0.5, scalar2=None, op0=ALU.mult),
              reads=["lng"], writes=["lng_h"])
        sc.op("dve", lambda: v.tensor_scalar(out=lnb_h[:], in0=lnb[:], scalar1=0.5, scalar2=None, op0=ALU.mult),
              reads=["lnb"], writes=["lnb_h"])
        sc.op("dve", lambda: v.tensor_scalar(out=lng_q[:], in0=lng[:], scalar1=0.25, scalar2=None, op0=ALU.mult),
              reads=["lng"], writes=["lng_q"])
        sc.op("dve", lambda: v.tensor_scalar(out=lnb_q[:], in0=lnb[:], scalar1=0.25, scalar2=None, op0=ALU.mult),
              reads=["lnb"], writes=["lnb_q"])

        wstate = {"n": 0}

        def load_wgroup(cols, width=128):
            slot = wstate["n"] % 2
            wstate["n"] += 1
            names = []
            tok = None
            for j, c0 in enumerate(cols):
                nm = ["wbuf%d_%d" % (slot, jj) for jj in range(j * width // 128, (j + 1) * width // 128)]
                tok = sc.dma("pool", "w%d" % slot, wbuf[slot][:, :, j * width:(j + 1) * width],
                             w_in_v[:, :, c0:c0 + width], writes=nm)
                names += nm
            sc.retoken(names, tok)
            return slot

        def proj_fm(slot, blk, tt, b_out):
            for c in range(8):
                sc.op("pe", lambda c=c: nc.tensor.matmul(bank(b_out), lhsT=wbuf[slot][:, c, blk * 128:(blk + 1) * 128],
                                                         rhs=hT[:, c, tt * 512:(tt + 1) * 512],
                                                         start=(c == 0), stop=(c == 7)),
                      reads=["wbuf%d_%d" % (slot, blk)] + ["hT%d" % i for i in range(4 * tt, 4 * tt + 4)],
                      writes=[BN[b_out]], inc=(c == 7))

        stopped = False
        for b in range(NB):
          try:
                with ExitStack() as ph:
                    if STOP < 1:
                        raise _Stop()
                    sc.barrier()
                    NXS = 4
                    g_tile = ph.enter_context(nc.sbuf_tensor("g_tile%d" % b, [128, D], F32))
                    sc.dma("sp", "gt", g_tile[:], norm_g.partition_broadcast(128).rearrange("p o n -> p (o n)"), writes=["g_tile"])
                    xt = [ph.enter_context(nc.sbuf_tensor("xt%d_%d" % (b, i), [128, D], F32)) for i in range(NXS)]
                    hb = [ph.enter_context(nc.sbuf_tensor("hb%d_%d" % (b, i), [128, D], BF16)) for i in range(3)]
                    junk = ph.enter_context(nc.sbuf_tensor("junk%d" % b, [128, D], BF16))
                    ssq = ph.enter_context(nc.sbuf_tensor("ssq%d" % b, [128, NT], F32))
                    msx = ph.enter_context(nc.sbuf_tensor("msx%d" % b, [128, NT], F32))
                    rsx = ph.enter_context(nc.sbuf_tensor("rsx%d" % b, [128, NT], F32))

                    def p1_stages(i):
                        s_ = i % NXS
                        hs = i % 3
                        pb = i % 2
                        pbf = bank_bf(pb)

                        def s0():
                            sc.dma("sp", "xl%d" % s_, xt[s_][:], x[b, i * 128:(i + 1) * 128, :], writes=["xt%d" % s_])

                        def s1():
                            sc.op("act", lambda: nc.scalar.activation(out=junk[:], in_=xt[s_][:], func=AF.Square,
                                                                      accum_out=ssq[:, i:i + 1]),
                                  reads=["xt%d" % s_], writes=["junk", "ssq%d" % i])
                            sc.op("act", lambda: nc.scalar.activation(out=msx[:, i:i + 1], in_=ssq[:, i:i + 1],
                                                                      func=AF.Ln, bias=EPS, scale=1.0 / D),
                                  reads=["ssq%d" % i], writes=["msx%d" % i])
                            sc.op("act", lambda: nc.scalar.activation(out=rsx[:, i:i + 1], in_=msx[:, i:i + 1],
                                                                      func=AF.Exp, scale=-0.5),
                                  reads=["msx%d" % i], writes=["rsx%d" % i])

                        def s2():
                            sc.op("dve", lambda: nc.vector.scalar_tensor_tensor(out=hb[hs][:], in0=xt[s_][:],
                                                                                scalar=rsx[:, i:i + 1], in1=g_tile[:],
                                                                                op0=ALU.mult, op1=ALU.mult),
                                  reads=["xt%d" % s_, "rsx%d" % i, "g_tile"], writes=["hb%d" % hs])

                        def s3():
                            for c in range(8):
                                sc.op("pe", lambda: nc.tensor.transpose(pbf[:, c * 128:(c + 1) * 128],
                                                                        hb[hs][:, c * 128:(c + 1) * 128], ident[:]),
                                      reads=["hb%d" % hs, "ident"], writes=[BN[pb]], inc=(c == 7))

                        def s4():
                            sc.op("dve", lambda: nc.vector.tensor_copy(out=hT[:, :, i * 128:(i + 1) * 128],
                                                                       in_=pbf.rearrange("p (c t) -> p c t", c=8)),
                                  reads=[BN[pb]], writes=["hT%d" % i])
                        return [s0, s1, s2, s3, s4]

                    p1u = [p1_stages(i) for i in range(NT)]
                    for slot_i in range(NT + 4):
                        for k in range(4, -1, -1):
                            ui_ = slot_i - k
                            if 0 <= ui_ < NT:
                                p1u[ui_][k]()

                pre = {}
                pre["pairs"] = [load_wgroup([O_Q + 128 * p_, O_K + 128 * p_, O_V + 128 * p_, O_GF + 128 * p_]) for p_ in range(2)]
                ph23 = ExitStack()
                bias_all = ph23.enter_context(nc.sbuf_tensor("bias_all%d" % b, [128, NH, NTT, NT], F32))
                dT = ph23.enter_context(nc.sbuf_tensor("dT%d" % b, [NH, S], BF16))
                with ExitStack() as ph:
                    if STOP < 2:
                        raise _Stop()
                    sc.barrier()
                    xf = ph.enter_context(nc.sbuf_tensor("xf%d" % b, [128, NT, NH], F32))
                    ef = ph.enter_context(nc.sbuf_tensor("ef%d" % b, [128, NT, NH], F32))
                    lfn = ph.enter_context(nc.sbuf_tensor("lfn%d" % b, [128, NT, NH], F32))
                    padA = ph.enter_context(nc.sbuf_tensor("padA%d" % b, [128, 8 + NT, NH], F32))
                    padB = ph.enter_context(nc.sbuf_tensor("padB%d" % b, [128, 8 + NT, NH], F32))
                    c_tok = ph.enter_context(nc.sbuf_tensor("c_tok%d" % b, [128, NT, NH], F32))
                    rB = ph.enter_context(nc.sbuf_tensor("rB%d" % b, [128, NT, NH], F32))
                    d_tok = ph.enter_context(nc.sbuf_tensor("d_tok%d" % b, [128, NT, NH], BF16))
                    NF = NT * NH
                    zf = bank(0)[:, 0:NF]
                    for i in range(NT):
                        for c in range(8):
                            sc.op("pe", lambda c=c: nc.tensor.matmul(bank(0)[:, i * NH:(i + 1) * NH],
                                                                     lhsT=hT[:, c, i * 128:(i + 1) * 128], rhs=wf[:, c, :],
                                                                     start=(c == 0), stop=(c == 7)),
                                  reads=["hT%d" % i, "wf"], writes=[BN[0]], inc=(c == 7 and i == NT - 1))
                    sc.op("dve", lambda: nc.vector.tensor_tensor(
                        out=xf[:], in0=zf.rearrange("p (i h) -> p i h", h=NH),
                        in1=bf_tile[:].rearrange("p (o h) -> p o h", o=1).broadcast_to([128, NT, NH]), op=ALU.add),
                        reads=[BN[0], "bf_tile"], writes=["xf"])
                    sc.op("act", lambda: nc.scalar.activation(out=ef[:], in_=xf[:], func=AF.Exp, scale=-1.0),
                          reads=["xf"], writes=["ef"])
                    sc.op("act", lambda: nc.scalar.activation(out=lfn[:], in_=ef[:], func=AF.Ln, bias=1.0, scale=1.0),
                          reads=["ef"], writes=["lfn"])
                    lfn2 = lfn[:].rearrange("p i h -> p (i h)")
                    sc.op("pe", lambda: nc.tensor.matmul(bank(1)[:, 0:NF], lhsT=Uneg[:], rhs=lfn2, start=True, stop=True),
                          reads=["lfn", "Uneg"], writes=[BN[1]], inc=False)
                    sc.op("pe", lambda: nc.tensor.matmul(bank(1)[:, 256:256 + NF], lhsT=OnesNeg[:], rhs=lfn2, start=True,
                                                         stop=True),
                          reads=["lfn", "OnesNeg"], writes=[BN[1]])
                    sc.op("dve", lambda: nc.vector.memset(padA[:], 0.0), writes=["padA"])
                    sc.op("dve", lambda: nc.vector.memset(padB[:], 0.0), writes=["padB"])
                    if NT > 1:
                        sc.op("dve", lambda: nc.vector.tensor_copy(
                            out=padA[:, 9:8 + NT, :],
                            in_=bank(1)[:, 256:256 + NF].rearrange("p (i h) -> p i h", h=NH)[:, 0:NT - 1, :]),
                            reads=[BN[1], "padA"], writes=["padA"])
                    cur, oth, curn, othn = padA, padB, "padA", "padB"
                    dstep = 1
                    while dstep < NT:
                        sc.op("dve", lambda cur=cur, oth=oth, dstep=dstep: nc.vector.tensor_tensor(
                            out=oth[:, 8:8 + NT, :], in0=cur[:, 8:8 + NT, :], in1=cur[:, 8 - dstep:8 + NT - dstep, :],
                            op=ALU.add), reads=[curn], writes=[othn])
                        cur, oth, curn, othn = oth, cur, othn, curn
                        dstep *= 2
                    sc.op("dve", lambda: nc.vector.tensor_tensor(
                        out=c_tok[:], in0=bank(1)[:, 0:NF].rearrange("p (i h) -> p i h", h=NH), in1=cur[:, 8:8 + NT, :],
                        op=ALU.add), reads=[BN[1], curn], writes=["c_tok"])
                    sc.op("pe", lambda: nc.tensor.matmul(bank(0)[:, 0:NF], lhsT=Sel[:],
                                                         rhs=c_tok[:].rearrange("p i h -> p (i h)"), start=True, stop=True),
                          reads=["c_tok", "Sel"], writes=[BN[0]])
                    sc.op("act", lambda: nc.scalar.copy(out=rB[:].rearrange("p i h -> p (i h)"), in_=bank(0)[:, 0:NF]),
                          reads=[BN[0]], writes=["rB"])
                    c4 = c_tok[:].rearrange("p (a j) h -> p a j h", j=4)
                    r4 = rB[:].rearrange("p (a j) h -> p a j h", j=4)[:, :, 3:4, :].broadcast_to([128, NTT, 4, NH])
                    sc.op("dve", lambda: nc.vector.tensor_tensor(
                        out=d_tok[:].rearrange("p (a j) h -> p a j h", j=4), in0=c4, in1=r4, op=ALU.subtract),
                        reads=["c_tok", "rB"], writes=["d_tok"])
                    for T in range(NTT):
                        sc.op("dve", lambda T=T: nc.vector.tensor_tensor(
                            out=bias_all[:, :, T, :],
                            in0=rB[:, 4 * T + 3:4 * T + 4, :].rearrange("p o h -> p h o").broadcast_to([128, NH, NT]),
                            in1=c_tok[:].rearrange("p i h -> p h i"), op=ALU.subtract),
                            reads=["c_tok", "rB"], writes=["bias_all"])
                    dps = PS[1].bitcast(BF16)
                    for i in range(NT):
                        sc.op("pe", lambda i=i: nc.tensor.transpose(dps[0:NH, i * 128:(i + 1) * 128], d_tok[:, i, :],
                                                                    ident[:]),
                              reads=["d_tok", "ident"], writes=[BN[2], BN[3]], inc=(i == NT - 1))
                    sc.op("act", lambda: nc.scalar.copy(out=dT[:], in_=dps[0:NH, 0:S]),
                          reads=[BN[2], BN[3]], writes=["dT"])

                with ExitStack() as ph:
                    if STOP < 3:
                        raise _Stop()
                    sc.barrier()
                    QA = [[ph.enter_context(nc.sbuf_tensor("QA%d_%d_%d" % (b, pp, i), [128, S], BF16)) for i in range(2)] for pp in range(2)]
                    KA = [[ph.enter_context(nc.sbuf_tensor("KA%d_%d_%d" % (b, pp, i), [128, S], BF16)) for i in range(2)] for pp in range(2)]
                    Vaug = [ph.enter_context(nc.sbuf_tensor("Vaug%d_%d" % (b, pp), [128, NT, 2, 128], BF16)) for pp in range(2)]
                    gate = [ph.enter_context(nc.sbuf_tensor("gate%d_%d" % (b, pp), [128, S], BF16)) for pp in range(2)]
                    sq_sb = [ph.enter_context(nc.sbuf_tensor("sq%d_%d" % (b, i), [128, 512], BF16)) for i in range(2)]
                    msb = [ph.enter_context(nc.sbuf_tensor("msb%d_%d" % (b, i), [128, 512], F32)) for i in range(2)]
                    rstd = [ph.enter_context(nc.sbuf_tensor("rstd%d_%d" % (b, i), [128, 512], F32)) for i in range(2)]
                    th_sb = [ph.enter_context(nc.sbuf_tensor("th%d_%d" % (b, i), [128, 512], F32)) for i in range(2)]
                    pbuf = [ph.enter_context(nc.sbuf_tensor("pbuf%d_%d" % (b, i), [128, 512], BF16)) for i in range(4)]
                    rec = [ph.enter_context(nc.sbuf_tensor("rec%d_%d" % (b, i), [128, 512], F32)) for i in range(3)]
                    for pp in range(2):
                        for hh in range(2):
                            t_ = "%d%d" % (pp, hh)
                            sc.op("pool", lambda: nc.gpsimd.memset(QA[pp][hh][64:128, :], 0.0), writes=["QArow" + t_])
                            sc.op("pool", lambda: nc.gpsimd.memset(KA[pp][hh][64:128, :], 0.0), writes=["KArow" + t_])
                            sc.op("pool", lambda: nc.gpsimd.memset(KA[pp][hh][64:65, :], 1.0), reads=["KArow" + t_],
                                  writes=["KArow" + t_])
                        sc.op("dve", lambda: nc.vector.memset(Vaug[pp][:, :, 0, 64:128], 1.0), writes=["Vones%d0" % pp])
                        sc.op("dve", lambda: nc.vector.memset(Vaug[pp][:, :, 1, 0:64], 1.0), writes=["Vones%d1" % pp])

                    nq = {"n": 0}
                    wslots = {}

                    def load_pair(p):
                        wslots[p] = load_wgroup([O_Q + 128 * p, O_K + 128 * p, O_V + 128 * p, O_GF + 128 * p])

                    def proj_units(p):
                        pp = p % 2
                        units = []

                        def v_unit(i0):
                            st = {}

                            def A():
                                cur_slot = wslots[p]
                                vb = PJB[nq["n"] % 5]
                                nq["n"] += 1
                                st["vb"] = vb
                                for ii in range(4):
                                    i = i0 + ii
                                    for c in range(8):
                                        sc.op("pe", lambda: nc.tensor.matmul(
                                            bank(vb)[:, ii * 128:(ii + 1) * 128], lhsT=hT[:, c, i * 128:(i + 1) * 128],
                                            rhs=wbuf[cur_slot][:, c, 256:384], start=(c == 0), stop=(c == 7)),
                                            reads=["hT%d" % i, "wbuf%d_2" % cur_slot], writes=[BN[vb]],
                                            inc=(c == 7 and ii == 3))

                            def B():
                                vb = st["vb"]
                                bv = bank(vb).rearrange("p (i c) -> p i c", i=4)
                                sc.op("act", lambda: nc.scalar.copy(out=Vaug[pp][:, i0:i0 + 4, 0, 0:64], in_=bv[:, :, 0:64]),
                                      reads=[BN[vb], "Vones%d0" % pp], writes=["V%d0_%d" % (pp, i0 // 4)])
                                sc.op("act", lambda: nc.scalar.copy(out=Vaug[pp][:, i0:i0 + 4, 1, 64:128], in_=bv[:, :, 64:128]),
                                      reads=[BN[vb], "Vones%d1" % pp], writes=["V%d1_%d" % (pp, i0 // 4)])
                            return [A, B]

                        def qk_unit(tt, qk):
                            st = {}

                            def A():
                                cur_slot = wslots[p]
                                n = nq["n"]
                                nq["n"] += 1
                                st["u"] = u = n % 2
                                st["bk"] = bk = PJB[n % 5]
                                proj_fm(cur_slot, qk, tt, bk)

                            def A2():
                                u = st["u"]
                                bk = st["bk"]
                                sc.op("act", lambda: nc.scalar.activation(out=sq_sb[u][:], in_=bank(bk), func=AF.Square),
                                      reads=[BN[bk]], writes=["sq%d" % u])

                            def B1():
                                u = st["u"]
                                ab = 6
                                sc.op("pe", lambda: nc.tensor.matmul(bank(ab), lhsT=blockones[:], rhs=sq_sb[u][:], start=True, stop=True),
                                      reads=["sq%d" % u, "blockones"], writes=[BN[ab]])
                                sc.op("act", lambda: nc.scalar.activation(out=msb[u][:], in_=bank(ab), func=AF.Ln, bias=EPS, scale=1.0),
                                      reads=[BN[ab]], writes=["msb%d" % u])

                            def B2():
                                u = st["u"]
                                sc.op("act", lambda: nc.scalar.activation(out=rstd[u][:], in_=msb[u][:], func=AF.Exp, scale=-0.5),
                                      reads=["msb%d" % u], writes=["rstd%d" % u])

                            def C():
                                u = st["u"]
                                bk = st["bk"]
                                dst = QA[pp] if qk == 0 else KA[pp]
                                gsc = gq8 if qk == 0 else gk
                                nm = "QA" if qk == 0 else "KA"
                                for hh in range(2):
                                    r0 = hh * 64
                                    sc.op("dve", lambda: nc.vector.scalar_tensor_tensor(
                                        out=dst[hh][0:64, tt * 512:(tt + 1) * 512], in0=bank(bk)[r0:r0 + 64, :],
                                        scalar=gsc[r0:r0 + 64, p:p + 1], in1=rstd[u][r0:r0 + 64, :], op0=ALU.mult, op1=ALU.mult),
                                        reads=[BN[bk], "rstd%d" % u, "gq8", "gk"], writes=["%s%d%d_%d" % (nm, pp, hh, tt)])
                            return [A, A2, B1, B2, C]

                        def gate_unit(tt):
                            st = {}

                            def A():
                                cur_slot = wslots[p]
                                n = nq["n"]
                                nq["n"] += 1
                                st["u"] = u = n % 2
                                st["bk"] = bk = PJB[n % 5]
                                proj_fm(cur_slot, 3, tt, bk)

                            def A2():
                                u = st["u"]
                                bk = st["bk"]
                                sc.op("act", lambda: nc.scalar.activation(out=th_sb[u][:], in_=bank(bk), func=AF.Tanh, scale=0.5),
                                      reads=[BN[bk]], writes=["th%d" % u])

                            def B():
                                u = st["u"]
                                bk = st["bk"]
                                sc.op("dve", lambda: nc.vector.scalar_tensor_tensor(
                                    out=gate[pp][:, tt * 512:(tt + 1) * 512], in0=th_sb[u][:], scalar=1.0, in1=bank(bk),
                                    op0=ALU.add, op1=ALU.mult),
                                    reads=[BN[bk], "th%d" % u], writes=["gate%d_%d" % (pp, tt)])
                            return [A, A2, B]

                        def drow_unit():
                            def A():
                                for hh in range(2):
                                    h = 2 * p + hh
                                    sc.dma("sp", "drow%d%d" % (pp, hh), QA[pp][hh][64:65, :], dT[h:h + 1, :], reads=["dT"],
                                           writes=["QArow%d%d" % (pp, hh)])
                            return [A]

                        ulist = [drow_unit()]
                        for i0 in range(0, NT, 4):
                            ulist.append(v_unit(i0))
                        for tt in range(NTT):
                            for qk in range(2):
                                ulist.append(qk_unit(tt, qk))
                        for tt in range(NTT):
                            ulist.append(gate_unit(tt))
                        flat = [lambda: nq.__setitem__("n", 0)]
                        nu = len(ulist)
                        mx = max(len(u_) for u_ in ulist)
                        for slot_i in range(nu + mx - 1):
                            for k in range(mx - 1, -1, -1):
                                ui_ = slot_i - k
                                if 0 <= ui_ < nu and k < len(ulist[ui_]):
                                    flat.append(ulist[ui_][k])
                        return flat

                    def attention(p, next_units):
                        pp = p % 2
                        steps = []
                        for hh in range(2):
                            for T in range(NTT):
                                for J in range(4 * T + 4):
                                    steps.append((hh, T, J))

                        def emit_S(si):
                            hh, T, J = steps[si]
                            j = J - 4 * T
                            col0 = 128 * j if j >= 0 else 0
                            sbk = (2, 6, 3)[si % 3]
                            sc.op("pe", lambda: nc.tensor.matmul(
                                bank(sbk)[:, col0:512], lhsT=KA[pp][hh][:, J * 128:(J + 1) * 128],
                                rhs=QA[pp][hh][:, T * 512 + col0:(T + 1) * 512], start=True, stop=(j < 0)),
                                reads=["KA%d%d_%d" % (pp, hh, J // 4), "KArow%d%d" % (pp, hh), "QA%d%d_%d" % (pp, hh, T),
                                       "QArow%d%d" % (pp, hh)],
                                writes=[BN[sbk]], inc=(j < 0))
                            if j >= 0:
                                sc.op("pe", lambda: nc.tensor.matmul(
                                    bank(sbk)[:, col0:col0 + 128], lhsT=ident[:], rhs=maskT[:], start=False, stop=True),
                                    reads=["ident", "maskT"], writes=[BN[sbk]])

                        def emit_exp(si):
                            hh, T, J = steps[si]
                            h = 2 * p + hh
                            j = J - 4 * T
                            col0 = 128 * j if j >= 0 else 0
                            sbk = (2, 6, 3)[si % 3]
                            pi = si % 4
                            sc.op("act", lambda: nc.scalar.activation(
                                out=pbuf[pi][:, col0:512], in_=bank(sbk)[:, col0:512], func=AF.Exp,
                                bias=bias_all[:, h, T, J:J + 1], scale=1.0),
                                reads=[BN[sbk], "bias_all"], writes=["pbuf%d" % pi])

                        def emit_PV(si):
                            hh, T, J = steps[si]
                            j = J - 4 * T
                            col0 = 128 * j if j >= 0 else 0
                            pi = si % 4
                            acc_n = (hh * NTT + T) % 3
                            ab = (4, 5, 0)[acc_n]
                            last = (J == 4 * T + 3)
                            sc.op("pe", lambda: nc.tensor.matmul(
                                bank(ab)[:, col0:512], lhsT=Vaug[pp][:, J, hh, :], rhs=pbuf[pi][:, col0:512],
                                start=(J == 0), stop=last),
                                reads=["pbuf%d" % pi, "V%d%d_%d" % (pp, hh, J // 4), "Vones%d%d" % (pp, hh)], writes=[BN[ab]],
                                inc=last)
                            if last:
                                num0, den0 = (0, 64) if hh == 0 else (64, 0)
                                ri = acc_n
                                sc.op("dve", lambda: nc.vector.reciprocal(out=rec[ri][num0:num0 + 64, :],
                                                                          in_=bank(ab)[den0:den0 + 64, :]),
                                      reads=[BN[ab]], writes=["rec%d" % ri])
                                sc.op("pool", lambda: nc.gpsimd.tensor_tensor(
                                    out=rec[ri][num0:num0 + 64, :], in0=rec[ri][num0:num0 + 64, :],
                                    in1=gate[pp][num0:num0 + 64, T * 512:(T + 1) * 512], op=ALU.mult),
                                    reads=["rec%d" % ri, "gate%d_%d" % (pp, T)], writes=["rec%d" % ri])
                                sc.op("dve", lambda: nc.vector.scalar_tensor_tensor(
                                    out=yT[num0:num0 + 64, p, T * 512:(T + 1) * 512], in0=bank(ab)[num0:num0 + 64, :],
                                    scalar=0.5, in1=rec[ri][num0:num0 + 64, :], op0=ALU.mult, op1=ALU.mult),
                                    reads=[BN[ab], "rec%d" % ri], writes=["yT%d_%d_%d" % (p, hh, T)])

                        nst = len(steps)
                        nun = len(next_units)
                        stride = max(1, nst // (nun + 1)) if nun else nst
                        ui = 0
                        emit_S(0)
                        if nst > 1:
                            emit_S(1)
                        for si in range(nst):
                            emit_exp(si)
                            if si + 2 < nst:
                                emit_S(si + 2)
                            emit_PV(si)
                            if INTERLEAVE and ui < nun and (si + 1) % stride == 0:
                                next_units[ui]()
                                ui += 1
                        while ui < nun:
                            next_units[ui]()
                            ui += 1

                    wslots[0], wslots[1] = pre["pairs"]
                    for un in proj_units(0):
                        un()
                    for p in range(8):
                        if p + 2 < 8:
                            load_pair(p + 2)
                        if p == 7:
                            pre["glu"] = [load_wgroup([O_GLU + 128 * c_, O_GLU + 1024 + 128 * c_]) for c_ in range(2)]
                        attention(p, proj_units(p + 1) if p + 1 < 8 else [])

                ph23.close()
                with ExitStack() as ph:
                    if STOP < 4:
                        raise _Stop()
                    sc.barrier()
                    wout = ph.enter_context(nc.sbuf_tensor("wout%d" % b, [128, 16, D], BF16))
                    for half in range(2):
                        sc.dma("pool", "wo%d" % half, wout[:, half * 8:(half + 1) * 8, :], w_out_v[:, half * 8:(half + 1) * 8, :],
                               writes=["wout%d" % half])
                    with ExitStack() as ph4:
                        with ExitStack() as ph4a:
                            uT = [ph4a.enter_context(nc.sbuf_tensor("uT%d_%d" % (b, i), [128, S + 32], BF16)) for i in range(2)]
                            Wd = [ph4a.enter_context(nc.sbuf_tensor("Wd%d_%d" % (b, i), [128, KC, 128], BF16)) for i in range(2)]
                            tb = [ph4a.enter_context(nc.sbuf_tensor("tb%d_%d" % (b, i), [128, 512], F32)) for i in range(2)]
                            cacc = [ph4a.enter_context(nc.sbuf_tensor("cacc%d_%d" % (b, i), [128, 512], F32)) for i in range(2)]
                            for i in range(2):
                                sc.op("pool", lambda i=i: nc.gpsimd.memset(uT[i][:, 0:32], 0.0), writes=["uTpad%d" % i])
                            def build_wd(ch):
                                us = ch % 2
                                sc.op("pool", lambda: nc.gpsimd.tensor_tensor(
                                    out=Wd[us][:],
                                    in0=ident[:].rearrange("p (o n) -> p o n", o=1).broadcast_to([128, KC, 128]),
                                    in1=convwh[:, ch, :].rearrange("p (k o) -> p k o", o=1).broadcast_to([128, KC, 128]),
                                    op=ALU.mult), reads=["ident", "convwh"], writes=["Wd%d" % us])

                            def glu_pe(ch, tts, slot_):
                                for tt in tts:
                                    ba = tt % 2
                                    bb = 6 + (tt % 2)
                                    u = tt % 2
                                    proj_fm(slot_, 0, tt, ba)
                                    proj_fm(slot_, 1, tt, bb)
                                    sc.op("act", lambda: nc.scalar.activation(out=tb[u][:], in_=bank(bb), func=AF.Tanh, scale=0.5),
                                          reads=[BN[bb]], writes=["tb%d" % u])

                            def glu_dve(ch, tts):
                                us = ch % 2
                                for tt in tts:
                                    ba = tt % 2
                                    u = tt % 2
                                    sc.op("dve", lambda: nc.vector.scalar_tensor_tensor(
                                        out=uT[us][:, 30 + tt * 512:30 + (tt + 1) * 512], in0=tb[u][:], scalar=1.0,
                                        in1=bank(ba), op0=ALU.add, op1=ALU.mult),
                                        reads=[BN[ba], "tb%d" % u, "uTpad%d" % us], writes=["uT%d_%d" % (us, tt)])

                            def glu_proj(ch, tts, slot_):
                                for tt in tts:
                                    glu_pe(ch, [tt], slot_)
                                    glu_dve(ch, [tt])

                            def taps(ch, tts, rds, k0, k1):
                                us = ch % 2
                                for k in range(k0, k1):
                                    for tt in tts:
                                        ca = cacc[tt % 2]
                                        can = "cacc%d" % (tt % 2)
                                        if k == NPE:
                                            sc.op("dve", lambda: nc.vector.tensor_scalar(
                                                out=ca[:], in0=uT[us][:, tt * 512 + k:tt * 512 + k + 512],
                                                scalar1=convwh[:, ch, k:k + 1], scalar2=None, op0=ALU.mult),
                                                reads=rds[tt] + ["convwh"], writes=[can])
                                        else:
                                            sc.op("dve", lambda: nc.vector.scalar_tensor_tensor(
                                                out=ca[:], in0=uT[us][:, tt * 512 + k:tt * 512 + k + 512],
                                                scalar=convwh[:, ch, k:k + 1], in1=ca[:], op0=ALU.mult, op1=ALU.add),
                                                reads=rds[tt] + ["convwh", can], writes=[can])

                            gslot = {0: pre["glu"][0], 1: pre["glu"][1]}
                            build_wd(0)
                            glu_proj(0, list(range(NTT)), gslot[0])
                            KMID = (NPE + KC) // 2
                            for ch in range(8):
                                us = ch % 2
                                if ch == 7:
                                    pre["gc"] = [load_wgroup([O_GC + 512 * gi], width=512) for gi in range(2)]
                                if ch + 2 < 8:
                                    gslot[ch + 2] = load_wgroup([O_GLU + 128 * (ch + 2), O_GLU + 1024 + 128 * (ch + 2)])
                                if ch + 1 < 8:
                                    build_wd(ch + 1)
                                for tp in range(0, NTT, 2):
                                    tts = [t for t in (tp, tp + 1) if t < NTT]
                                    rds = {}
                                    if ch + 1 < 8:
                                        glu_pe(ch + 1, tts, gslot[ch + 1])
                                    for tt in tts:
                                        cb = 2 + (tt % 2)
                                        rd = ["Wd%d" % us, "uTpad%d" % us, "uT%d_%d" % (us, tt)]
                                        if tt > 0:
                                            rd.append("uT%d_%d" % (us, tt - 1))
                                        rds[tt] = rd
                                        for k in range(NPE):
                                            sc.op("pe", lambda k=k: nc.tensor.matmul(
                                                bank(cb), lhsT=Wd[us][:, k, :], rhs=uT[us][:, tt * 512 + k:tt * 512 + k + 512],
                                                start=(k == 0), stop=(k == NPE - 1)),
                                                reads=rd, writes=[BN[cb]], inc=(k == NPE - 1))
                                    taps(ch, tts, rds, NPE, KMID)
                                    if ch + 1 < 8:
                                        glu_dve(ch + 1, tts)
                                    taps(ch, tts, rds, KMID, KC)
                                    for tt in tts:
                                        cb = 2 + (tt % 2)
                                        ca = cacc[tt % 2]
                                        can = "cacc%d" % (tt % 2)
                                        sc.op("dve", lambda: nc.vector.scalar_tensor_tensor(
                                            out=yT[:, 8 + ch, tt * 512:(tt + 1) * 512], in0=bank(cb),
                                            scalar=convb[:, ch:ch + 1], in1=ca[:], op0=ALU.add, op1=ALU.add),
                                            reads=[BN[cb], "convb", can], writes=["yc%d_%d" % (ch, tt)])
                        sc.barrier()
                        sq2 = [ph4.enter_context(nc.sbuf_tensor("sqc%d_%d" % (b, i), [128, 512], BF16)) for i in range(2)]
                        mean_sb = ph4.enter_context(nc.sbuf_tensor("mean%d" % b, [128, S], F32))
                        rstd_sb = ph4.enter_context(nc.sbuf_tensor("rstdc%d" % b, [128, S], F32))
                        t1 = [ph4.enter_context(nc.sbuf_tensor("t1_%d_%d" % (b, i), [128, 512], F32)) for i in range(NSET)]
                        th2 = [ph4.enter_context(nc.sbuf_tensor("th2_%d_%d" % (b, i), [128, 512], F32)) for i in range(2)]
                        vq = [ph4.enter_context(nc.sbuf_tensor("vq_%d_%d" % (b, i), [128, 512], F32)) for i in range(NSET)]
                        thc = [ph4.enter_context(nc.sbuf_tensor("thc_%d_%d" % (b, i), [128, 512], F32)) for i in range(NSET)]
                        gcv = [ph4.enter_context(nc.sbuf_tensor("gcv_%d_%d" % (b, i), [128, 512], F32)) for i in range(NSET)]
                        m2 = t1[0]
                        vare = t1[1]
                        for tt in range(NTT):
                            for ch in range(8):
                                sc.op("pe", lambda ch=ch: nc.tensor.matmul(
                                    bank(4), lhsT=onesmean[:], rhs=yT[:, 8 + ch, tt * 512:(tt + 1) * 512],
                                    start=(ch == 0), stop=(ch == 7)),
                                    reads=["onesmean", "yc%d_%d" % (ch, tt)], writes=[BN[4]], inc=(ch == 7))
                            for ch in range(8):
                                qi = ch % 2
                                sc.op("act", lambda ch=ch, qi=qi: nc.scalar.activation(
                                    out=sq2[qi][:], in_=yT[:, 8 + ch, tt * 512:(tt + 1) * 512], func=AF.Square),
                                    reads=["yc%d_%d" % (ch, tt)], writes=["sqc%d" % qi])
                                sc.op("pe", lambda ch=ch, qi=qi: nc.tensor.matmul(
                                    bank(5), lhsT=onesmean[:], rhs=sq2[qi][:], start=(ch == 0), stop=(ch == 7)),
                                    reads=["onesmean", "sqc%d" % qi], writes=[BN[5]], inc=True)
                            sc.op("act", lambda: nc.scalar.copy(out=mean_sb[:, tt * 512:(tt + 1) * 512], in_=bank(4)),
                                  reads=[BN[4]], writes=["mean%d" % tt])
                            sc.op("dve", lambda: nc.vector.tensor_tensor(
                                out=m2[:], in0=mean_sb[:, tt * 512:(tt + 1) * 512],
                                in1=mean_sb[:, tt * 512:(tt + 1) * 512], op=ALU.mult),
                                reads=["mean%d" % tt], writes=["t1_0"])
                            sc.op("dve", lambda: nc.vector.scalar_tensor_tensor(
                                out=vare[:], in0=bank(5), scalar=EPS, in1=m2[:], op0=ALU.add, op1=ALU.subtract),
                                reads=[BN[5], "t1_0"], writes=["t1_1"])
                            sc.op("act", lambda: nc.scalar.activation(out=vare[:], in_=vare[:], func=AF.Ln),
                                  reads=["t1_1"], writes=["t1_1"])
                            sc.op("act", lambda: nc.scalar.activation(
                                out=rstd_sb[:, tt * 512:(tt + 1) * 512], in_=vare[:], func=AF.Exp, scale=-0.5),
                                reads=["t1_1"], writes=["rstdc%d" % tt])
                        gslots = pre["gc"]
                        blocks = [(gi, c4, tt) for tt in range(NTT) for gi in range(2) for c4 in range(4)]
                        xo = [ph4.enter_context(nc.sbuf_tensor("xo%d_%d" % (b, i), [128, D], F32)) for i in range(2)]
                        ot = [ph4.enter_context(nc.sbuf_tensor("ot%d_%d" % (b, i), [128, D], F32)) for i in range(1)]

                        def og(i, half):
                            s_ = i % 2
                            T = i // 4
                            ob = 4 + half
                            if half == 0:
                                sc.dma("sp", "xm%d" % s_, xo[s_][:], x[b, i * 128:(i + 1) * 128, :], writes=["xo%d" % s_])
                            for e in range(16):
                                if e < 8:
                                    rd = ["yT%d_0_%d" % (e, T), "yT%d_1_%d" % (e, T)]
                                else:
                                    rd = ["yc%d_%d" % (e - 8, T)]
                                sc.op("pe", lambda: nc.tensor.matmul(
                                    bank(ob), lhsT=yT[:, e, i * 128:(i + 1) * 128],
                                    rhs=wout[:, e, half * 512:(half + 1) * 512], start=(e == 0), stop=(e == 15)),
                                    reads=rd + ["wout%d" % (e // 8)], writes=[BN[ob]], inc=(e == 15))
                            sc.op("dve", lambda: nc.vector.tensor_tensor(
                                out=ot[0][:, half * 512:(half + 1) * 512], in0=bank(ob),
                                in1=xo[s_][:, half * 512:(half + 1) * 512], op=ALU.add),
                                reads=[BN[ob], "xo%d" % s_], writes=["ot_%d" % half])
                            if half == 1:
                                sc.dma("sp", "ost", out[b, i * 128:(i + 1) * 128, :], ot[0][:], reads=["ot_0", "ot_1"])

                        ogs = [(i, half) for i in range(NT) for half in range(2)]

                        LB = (0, 1, 2, 3, 6, 7)

                        def ln_stages(n):
                            gi, c4, tt = blocks[n]
                            ch = gi * 4 + c4
                            u = n % NSET
                            pbk = LB[n % 6]
                            ysl = yT[:, 8 + ch, tt * 512:(tt + 1) * 512]

                            def s0():
                                proj_fm(gslots[gi], c4, tt, pbk)

                            def s1():
                                sc.op("act", lambda: nc.scalar.activation(out=thc[u][:], in_=bank(pbk), func=AF.Tanh, scale=0.5),
                                      reads=[BN[pbk]], writes=["thc%d" % u])
                                sc.op("dve", lambda: nc.vector.tensor_tensor(
                                    out=t1[u][:], in0=ysl, in1=mean_sb[:, tt * 512:(tt + 1) * 512], op=ALU.subtract),
                                    reads=["yc%d_%d" % (ch, tt), "mean%d" % tt], writes=["t1_%d" % u])

                            def s2():
                                sc.op("pool", lambda: nc.gpsimd.tensor_tensor(
                                    out=t1[u][:], in0=t1[u][:], in1=rstd_sb[:, tt * 512:(tt + 1) * 512], op=ALU.mult),
                                    reads=["t1_%d" % u, "rstdc%d" % tt], writes=["t1_%d" % u])

                            def s3():
                                sc.op("act", lambda: nc.scalar.activation(
                                    out=th2[n % 2][:], in_=t1[u][:], func=AF.Tanh, bias=lnb_h[:, ch:ch + 1],
                                    scale=lng_h[:, ch:ch + 1]),
                                    reads=["t1_%d" % u, "lng_h", "lnb_h"], writes=["th2_%d" % (n % 2)])
                                sc.op("dve", lambda: nc.vector.scalar_tensor_tensor(
                                    out=gcv[u][:], in0=thc[u][:], scalar=1.0, in1=bank(pbk), op0=ALU.add,
                                    op1=ALU.mult), reads=[BN[pbk], "thc%d" % u], writes=["gcv%d" % u])
                                sc.op("act", lambda: nc.scalar.activation(
                                    out=vq[u][:], in_=t1[u][:], func=AF.Identity, bias=lnb_q[:, ch:ch + 1],
                                    scale=lng_q[:, ch:ch + 1]),
                                    reads=["t1_%d" % u, "lng_q", "lnb_q"], writes=["vq%d" % u])

                            def s4():
                                sc.op("dve", lambda: nc.vector.scalar_tensor_tensor(
                                    out=vq[u][:], in0=th2[n % 2][:], scalar=1.0, in1=vq[u][:], op0=ALU.add, op1=ALU.mult),
                                    reads=["th2_%d" % (n % 2), "vq%d" % u], writes=["vq%d" % u])

                            def s5():
                                if n % 2 == 0:
                                    sc.op("pool", lambda: nc.gpsimd.tensor_tensor(
                                        out=ysl, in0=vq[u][:], in1=gcv[u][:], op=ALU.mult),
                                        reads=["vq%d" % u, "gcv%d" % u], writes=["yc%d_%d" % (ch, tt)])
                                else:
                                    sc.op("dve", lambda: nc.vector.tensor_tensor(
                                        out=ysl, in0=vq[u][:], in1=gcv[u][:], op=ALU.mult),
                                        reads=["vq%d" % u, "gcv%d" % u], writes=["yc%d_%d" % (ch, tt)])
                            return [s0, s1, s2, s3, s4, s5]

                        lnu = [ln_stages(n) for n in range(len(blocks))]
                        oq = 0
                        for slot_i in range(len(lnu) + 5):
                            for k in range(5, -1, -1):
                                ui_ = slot_i - k
                                if 0 <= ui_ < len(lnu):
                                    lnu[ui_][k]()
                            if slot_i >= 12 and oq < len(ogs) and oq <= slot_i - 12:
                                og(*ogs[oq])
                                oq += 1
                        while oq < len(ogs):
                            og(*ogs[oq])
                            oq += 1
          except _Stop:
            stopped = True
            break
        sc.wait_all("sp")
        if stopped:
            es.pop_all()
    return nc


def _prep_shared(inp):
    f = np.float32
    qg = np.asarray(inp["q_norm_g"], f)[0]
    kg = np.asarray(inp["k_norm_g"], f)[0]
    gq = np.ascontiguousarray(qg.reshape(8, 128).T)
    gk = np.ascontiguousarray(kg.reshape(8, 128).T)
    cw = np.asarray(inp["conv_w"], f)[0]
    convw = np.ascontiguousarray(cw.T.reshape(8, 128, KC).transpose(1, 0, 2))

    def pc(a):
        return np.ascontiguousarray(np.asarray(a, f)[0].reshape(8, 128).T)

    return {
        "w_in": np.ascontiguousarray(np.asarray(inp["w_in"], f)[0]),
        "w_out": np.ascontiguousarray(np.asarray(inp["w_out"], f)[0]),
        "norm_g": np.ascontiguousarray(np.asarray(inp["norm_g"], f)),
        "b_forget": np.ascontiguousarray(np.asarray(inp["b_forget"], f)),
        "gq": gq, "gk": gk, "convw": convw,
        "convb": pc(inp["conv_b"]), "lng": pc(inp["conv_ln_g"]), "lnb": pc(inp["conv_ln_b"]),
    }


def kernel(**inputs):
    x = np.asarray(inputs["x"], np.float32)
    B, S, _ = x.shape
    n = N_CORES
    nb = B // n
    shared = _prep_shared(inputs)
    nc = build_nc(NB=nb, S=S)
    in_maps = []
    for c in range(n):
        m = dict(shared)
        m["x"] = np.ascontiguousarray(x[c * nb:(c + 1) * nb])
        in_maps.append(m)
    res = run_bass_kernel_spmd(nc, in_maps, core_ids=list(range(n)))
    return np.concatenate([np.asarray(r["out"]) for r in res.results], axis=0).astype(np.float32)
```

```python
import numpy as np
from contextlib import ExitStack
import concourse.bass as bass
import concourse.mybir as mybir
from concourse.bass_utils import run_bass_kernel_spmd

F32 = mybir.dt.float32
BF16 = mybir.dt.bfloat16
AF = mybir.ActivationFunctionType
ALU = mybir.AluOpType

D = 1024
NH = 16
HD = 64
KC = 31
IN_COLS = 7184
O_Q, O_K, O_V, O_F, O_GF, O_GLU, O_GC = 0, 1024, 2048, 3072, 3088, 4112, 6160
EPS = 1e-6
MASKV = -30000.0
N_CORES = 8
INTERLEAVE = False
PJB = (1, 7, 3, 0, 5)
NSET = 3
NPE = 22


class Sched:
    def __init__(self, nc, es):
        self.nc = nc
        self.eng = {"pe": nc.tensor, "act": nc.scalar, "dve": nc.vector, "pool": nc.gpsimd, "sp": nc.sync}
        self.sem = {k: es.enter_context(nc.semaphore("s_" + k)) for k in self.eng}
        self.cnt = {k: 0 for k in self.eng}
        self.seen = {k: {} for k in self.eng}
        self.dsem = {}
        self.dcnt = {}
        self.es = es
        self.res = {}
        self.pending_noinc = {k: False for k in self.eng}

    def _semh(self, key):
        return self.sem[key] if key in self.sem else self.dsem[key]

    def _wait(self, engine, tok):
        key, val = tok
        if self.seen[engine].get(key, 0) >= val:
            return
        self.seen[engine][key] = val
        self.eng[engine].wait_ge(self._semh(key), val)

    def _deps(self, engine, reads, writes):
        for r in reads:
            st = self.res.get(r)
            if st is not None and st["w"] is not None:
                if not (engine == "pe" and st["w"][0] == "pe"):
                    self._wait(engine, st["w"])
        for w in writes:
            st = self.res.get(w)
            if st is None:
                continue
            if st["w"] is not None and not (engine == "pe" and st["w"][0] == "pe"):
                self._wait(engine, st["w"])
            for k, v in st["r"].items():
                if k == engine and engine == "pe":
                    continue
                self._wait(engine, (k, v))

    def _record(self, tok, reads, writes):
        for r in reads:
            st = self.res.setdefault(r, {"w": None, "r": {}})
            if st["r"].get(tok[0], 0) < tok[1]:
                st["r"][tok[0]] = tok[1]
        for w in writes:
            self.res[w] = {"w": tok, "r": {}}

    def op(self, engine, fn, reads=(), writes=(), inc=True):
        self._deps(engine, reads, writes)
        ins = fn()
        if inc:
            self.cnt[engine] += 1
            ins.then_inc(self.sem[engine], 1)
            tok = (engine, self.cnt[engine])
            self.pending_noinc[engine] = False
        else:
            tok = (engine, self.cnt[engine] + 1)
            self.pending_noinc[engine] = True
        self._record(tok, reads, writes)
        return tok

    def dma(self, queue, semname, out, in_, reads=(), writes=()):
        if semname not in self.dsem:
            self.dsem[semname] = self.es.enter_context(self.nc.semaphore("d_" + semname))
            self.dcnt[semname] = 0
        self._deps(queue, reads, writes)
        ins = self.eng[queue].dma_start(out=out, in_=in_)
        self.dcnt[semname] += 16
        ins.then_inc(self.dsem[semname], 16)
        tok = (semname, self.dcnt[semname])
        self._record(tok, reads, writes)
        return tok

    def retoken(self, names, tok):
        for n in names:
            self.res[n] = {"w": tok, "r": {}}

    def barrier(self):
        assert not self.pending_noinc["pe"]
        for e in self.eng:
            if e != "pe":
                self.wait_all(e)

    def wait_all(self, engine):
        for k in self.eng:
            if k != engine and self.cnt[k] > 0:
                self._wait(engine, (k, self.cnt[k]))
        for k, v in self.dcnt.items():
            if v > 0:
                self._wait(engine, (k, v))


class _Stop(Exception):
    pass


def build_nc(NB=2, S=2048, STOP=9):
    NT = S // 128
    NTT = S // 512
    nc = bass.Bass("TRN2", target_bir_lowering=False)
    x = nc.dram_tensor("x", [NB, S, D], F32, kind="ExternalInput").ap()
    w_in = nc.dram_tensor("w_in", [D, IN_COLS], F32, kind="ExternalInput").ap()
    w_out = nc.dram_tensor("w_out", [2 * D, D], F32, kind="ExternalInput").ap()
    norm_g = nc.dram_tensor("norm_g", [1, D], F32, kind="ExternalInput").ap()
    b_forget = nc.dram_tensor("b_forget", [1, NH], F32, kind="ExternalInput").ap()
    gq_d = nc.dram_tensor("gq", [128, 8], F32, kind="ExternalInput").ap()
    gk_d = nc.dram_tensor("gk", [128, 8], F32, kind="ExternalInput").ap()
    convw_d = nc.dram_tensor("convw", [128, 8, KC], F32, kind="ExternalInput").ap()
    convb_d = nc.dram_tensor("convb", [128, 8], F32, kind="ExternalInput").ap()
    lng_d = nc.dram_tensor("lng", [128, 8], F32, kind="ExternalInput").ap()
    lnb_d = nc.dram_tensor("lnb", [128, 8], F32, kind="ExternalInput").ap()
    out = nc.dram_tensor("out", [NB, S, D], F32, kind="ExternalOutput").ap()

    w_in_v = w_in.rearrange("(c p) n -> p c n", p=128)
    w_out_v = w_out.rearrange("(c p) n -> p c n", p=128)

    with ExitStack() as es:
        sc = Sched(nc, es)

        def sb(name, shape, dt):
            return es.enter_context(nc.sbuf_tensor(name, shape, dt))

        ident = sb("ident", [128, 128], BF16)
        maskT = sb("maskT", [128, 128], BF16)
        blockones = sb("blockones", [128, 128], BF16)
        onesmean = sb("onesmean", [128, 128], BF16)
        Uneg = sb("Uneg", [128, 128], F32)
        OnesNeg = sb("OnesNeg", [128, 128], F32)
        Sel = sb("Sel", [128, 128], F32)
        bf_tile = sb("bf_tile", [128, NH], F32)
        gq = sb("gq_sb", [128, 8], F32)
        gk = sb("gk_sb", [128, 8], F32)
        gq8 = sb("gq8", [128, 8], F32)
        convw = sb("convw_sb", [128, 8, KC], F32)
        convwh = sb("convwh", [128, 8, KC], F32)
        convb = sb("convb_sb", [128, 8], F32)
        lng = sb("lng_sb", [128, 8], F32)
        lnb = sb("lnb_sb", [128, 8], F32)
        lng_h = sb("lng_h", [128, 8], F32)
        lnb_h = sb("lnb_h", [128, 8], F32)
        lng_q = sb("lng_q", [128, 8], F32)
        lnb_q = sb("lnb_q", [128, 8], F32)
        wf = sb("wf", [128, 8, NH], BF16)
        hT = sb("hT", [128, 8, S], BF16)
        yT = sb("yT", [128, 16, S], BF16)
        wbuf = [sb("wbuf%d" % i, [128, 8, 512], BF16) for i in range(2)]

        PS = [es.enter_context(nc.psum_tensor("ps%d" % i, [128, 1024], F32)) for i in range(4)]

        def bank(b):
            return PS[b // 2][:, (b % 2) * 512:(b % 2) * 512 + 512]

        def bank_bf(b):
            t = PS[b // 2].bitcast(BF16)
            return t[:, (b % 2) * 1024:(b % 2) * 1024 + 1024]

        BN = ["B%d" % i for i in range(8)]

        g = nc.gpsimd
        sc.op("pool", lambda: g.memset(ident[:], 1.0), writes=["ident"])
        sc.op("pool", lambda: g.affine_select(out=ident[:], in_=ident[:], compare_op=ALU.is_equal, fill=0.0,
                                              base=0, pattern=[[1, 128]], channel_multiplier=-1),
              reads=["ident"], writes=["ident"])
        sc.op("pool", lambda: g.memset(maskT[:], 0.0), writes=["maskT"])
        sc.op("pool", lambda: g.affine_select(out=maskT[:], in_=maskT[:], compare_op=ALU.is_ge, fill=MASKV,
                                              base=0, pattern=[[1, 128]], channel_multiplier=-1),
              reads=["maskT"], writes=["maskT"])
        sc.op("pool", lambda: g.memset(blockones[:], 0.0), writes=["blockones"])
        sc.op("pool", lambda: g.memset(blockones[0:64, 0:64], 1.0 / 64), reads=["blockones"], writes=["blockones"])
        sc.op("pool", lambda: g.memset(blockones[64:128, 64:128], 1.0 / 64), reads=["blockones"], writes=["blockones"])
        sc.op("pool", lambda: g.memset(onesmean[:], 1.0 / 1024), writes=["onesmean"])
        sc.op("pool", lambda: g.memset(Uneg[:], -1.0), writes=["Uneg"])
        sc.op("pool", lambda: g.affine_select(out=Uneg[:], in_=Uneg[:], compare_op=ALU.is_ge, fill=0.0,
                                              base=0, pattern=[[1, 128]], channel_multiplier=-1),
              reads=["Uneg"], writes=["Uneg"])
        sc.op("pool", lambda: g.memset(OnesNeg[:], -1.0), writes=["OnesNeg"])
        sc.op("pool", lambda: g.memset(Sel[:], 1.0), writes=["Sel"])
        sc.op("pool", lambda: g.affine_select(out=Sel[:], in_=Sel[:], compare_op=ALU.is_equal, fill=0.0,
                                              base=-127, pattern=[[0, 128]], channel_multiplier=1),
              reads=["Sel"], writes=["Sel"])

        sc.dma("sp", "par", bf_tile[:], b_forget.partition_broadcast(128).rearrange("p o n -> p (o n)"), writes=["bf_tile"])
        sc.dma("sp", "par", gq[:], gq_d, writes=["gq"])
        sc.dma("sp", "par", gk[:], gk_d, writes=["gk"])
        sc.dma("sp", "par", convw[:], convw_d, writes=["convw"])
        sc.dma("sp", "par", convb[:], convb_d, writes=["convb"])
        sc.dma("sp", "par", lng[:], lng_d, writes=["lng"])
        sc.dma("sp", "par", lnb[:], lnb_d, writes=["lnb"])
        sc.retoken(["bf_tile", "gq", "gk", "convw", "convb", "lng", "lnb"], ("par", sc.dcnt["par"]))
        sc.dma("pool", "wfl", wf[:], w_in_v[:, :, O_F:O_F + NH], writes=["wf"])
        v = nc.vector
        sc.op("dve", lambda: v.tensor_scalar(out=gq8[:], in0=gq[:], scalar1=0.125, scalar2=None, op0=ALU.mult),
              reads=["gq"], writes=["gq8"])
        sc.op("dve", lambda: v.tensor_scalar(out=convwh[:], in0=convw[:], scalar1=0.5, scalar2=None, op0=ALU.mult),
              reads=["convw"], writes=["convwh"])
        sc.op("dve", lambda: v.tensor_scalar(out=lng_h[:], in0=lng[:], scalar1=0.5, scalar2=None, op0=ALU.mult),
              reads=["lng"], writes=["lng_h"])
        sc.op("dve", lambda: v.tensor_scalar(out=lnb_h[:], in0=lnb[:], scalar1=0.5, scalar2=None, op0=ALU.mult),
              reads=["lnb"], writes=["lnb_h"])
        sc.op("dve", lambda: v.tensor_scalar(out=lng_q[:], in0=lng[:], scalar1=0.25, scalar2=None, op0=ALU.mult),
              reads=["lng"], writes=["lng_q"])
        sc.op("dve", lambda: v.tensor_scalar(out=lnb_q[:], in0=lnb[:], scalar1=0.25, scalar2=None, op0=ALU.mult),
              reads=["lnb"], writes=["lnb_q"])

        wstate = {"n": 0}

        def load_wgroup(cols, width=128):
            slot = wstate["n"] % 2
            wstate["n"] += 1
            names = []
            tok = None
            for j, c0 in enumerate(cols):
                nm = ["wbuf%d_%d" % (slot, jj) for jj in range(j * width // 128, (j + 1) * width // 128)]
                tok = sc.dma("pool", "w%d" % slot, wbuf[slot][:, :, j * width:(j + 1) * width],
                             w_in_v[:, :, c0:c0 + width], writes=nm)
                names += nm
            sc.retoken(names, tok)
            return slot

        def proj_fm(slot, blk, tt, b_out):
            for c in range(8):
                sc.op("pe", lambda c=c: nc.tensor.matmul(bank(b_out), lhsT=wbuf[slot][:, c, blk * 128:(blk + 1) * 128],
                                                         rhs=hT[:, c, tt * 512:(tt + 1) * 512],
                                                         start=(c == 0), stop=(c == 7)),
                      reads=["wbuf%d_%d" % (slot, blk)] + ["hT%d" % i for i in range(4 * tt, 4 * tt + 4)],
                      writes=[BN[b_out]], inc=(c == 7))

        stopped = False
        for b in range(NB):
          try:
                with ExitStack() as ph:
                    if STOP < 1:
                        raise _Stop()
                    sc.barrier()
                    NXS = 4
                    g_tile = ph.enter_context(nc.sbuf_tensor("g_tile%d" % b, [128, D], F32))
                    sc.dma("sp", "gt", g_tile[:], norm_g.partition_broadcast(128).rearrange("p o n -> p (o n)"), writes=["g_tile"])
                    xt = [ph.enter_context(nc.sbuf_tensor("xt%d_%d" % (b, i), [128, D], F32)) for i in range(NXS)]
                    hb = [ph.enter_context(nc.sbuf_tensor("hb%d_%d" % (b, i), [128, D], BF16)) for i in range(3)]
                    junk = ph.enter_context(nc.sbuf_tensor("junk%d" % b, [128, D], BF16))
                    ssq = ph.enter_context(nc.sbuf_tensor("ssq%d" % b, [128, NT], F32))
                    msx = ph.enter_context(nc.sbuf_tensor("msx%d" % b, [128, NT], F32))
                    rsx = ph.enter_context(nc.sbuf_tensor("rsx%d" % b, [128, NT], F32))

                    def p1_stages(i):
                        s_ = i % NXS
                        hs = i % 3
                        pb = i % 2
                        pbf = bank_bf(pb)

                        def s0():
                            sc.dma("sp", "xl%d" % s_, xt[s_][:], x[b, i * 128:(i + 1) * 128, :], writes=["xt%d" % s_])

                        def s1():
                            sc.op("act", lambda: nc.scalar.activation(out=junk[:], in_=xt[s_][:], func=AF.Square,
                                                                      accum_out=ssq[:, i:i + 1]),
                                  reads=["xt%d" % s_], writes=["junk", "ssq%d" % i])
                            sc.op("act", lambda: nc.scalar.activation(out=msx[:, i:i + 1], in_=ssq[:, i:i + 1],
                                                                      func=AF.Ln, bias=EPS, scale=1.0 / D),
                                  reads=["ssq%d" % i], writes=["msx%d" % i])
                            sc.op("act", lambda: nc.scalar.activation(out=rsx[:, i:i + 1], in_=msx[:, i:i + 1],
                                                                      func=AF.Exp, scale=-0.5),
                                  reads=["msx%d" % i], writes=["rsx%d" % i])

                        def s2():
                            sc.op("dve", lambda: nc.vector.scalar_tensor_tensor(out=hb[hs][:], in0=xt[s_][:],
                                                                                scalar=rsx[:, i:i + 1], in1=g_tile[:],
                                                                                op0=ALU.mult, op1=ALU.mult),
                                  reads=["xt%d" % s_, "rsx%d" % i, "g_tile"], writes=["hb%d" % hs])

                        def s3():
                            for c in range(8):
                                sc.op("pe", lambda: nc.tensor.transpose(pbf[:, c * 128:(c + 1) * 128],
                                                                        hb[hs][:, c * 128:(c + 1) * 128], ident[:]),
                                      reads=["hb%d" % hs, "ident"], writes=[BN[pb]], inc=(c == 7))

                        def s4():
                            sc.op("dve", lambda: nc.vector.tensor_copy(out=hT[:, :, i * 128:(i + 1) * 128],
                                                                       in_=pbf.rearrange("p (c t) -> p c t", c=8)),
                                  reads=[BN[pb]], writes=["hT%d" % i])
                        return [s0, s1, s2, s3, s4]

                    p1u = [p1_stages(i) for i in range(NT)]
                    for slot_i in range(NT + 4):
                        for k in range(4, -1, -1):
                            ui_ = slot_i - k
                            if 0 <= ui_ < NT:
                                p1u[ui_][k]()

                pre = {}
                pre["pairs"] = [load_wgroup([O_Q + 128 * p_, O_K + 128 * p_, O_V + 128 * p_, O_GF + 128 * p_]) for p_ in range(2)]
                ph23 = ExitStack()
                bias_all = ph23.enter_context(nc.sbuf_tensor("bias_all%d" % b, [128, NH, NTT, NT], F32))
                dT = ph23.enter_context(nc.sbuf_tensor("dT%d" % b, [NH, S], BF16))
                with ExitStack() as ph:
                    if STOP < 2:
                        raise _Stop()
                    sc.barrier()
                    xf = ph.enter_context(nc.sbuf_tensor("xf%d" % b, [128, NT, NH], F32))
                    ef = ph.enter_context(nc.sbuf_tensor("ef%d" % b, [128, NT, NH], F32))
                    lfn = ph.enter_context(nc.sbuf_tensor("lfn%d" % b, [128, NT, NH], F32))
                    padA = ph.enter_context(nc.sbuf_tensor("padA%d" % b, [128, 8 + NT, NH], F32))
                    padB = ph.enter_context(nc.sbuf_tensor("padB%d" % b, [128, 8 + NT, NH], F32))
                    c_tok = ph.enter_context(nc.sbuf_tensor("c_tok%d" % b, [128, NT, NH], F32))
                    rB = ph.enter_context(nc.sbuf_tensor("rB%d" % b, [128, NT, NH], F32))
                    d_tok = ph.enter_context(nc.sbuf_tensor("d_tok%d" % b, [128, NT, NH], BF16))
                    NF = NT * NH
                    zf = bank(0)[:, 0:NF]
                    for i in range(NT):
                        for c in range(8):
                            sc.op("pe", lambda c=c: nc.tensor.matmul(bank(0)[:, i * NH:(i + 1) * NH],
                                                                     lhsT=hT[:, c, i * 128:(i + 1) * 128], rhs=wf[:, c, :],
                                                                     start=(c == 0), stop=(c == 7)),
                                  reads=["hT%d" % i, "wf"], writes=[BN[0]], inc=(c == 7 and i == NT - 1))
                    sc.op("dve", lambda: nc.vector.tensor_tensor(
                        out=xf[:], in0=zf.rearrange("p (i h) -> p i h", h=NH),
                        in1=bf_tile[:].rearrange("p (o h) -> p o h", o=1).broadcast_to([128, NT, NH]), op=ALU.add),
                        reads=[BN[0], "bf_tile"], writes=["xf"])
                    sc.op("act", lambda: nc.scalar.activation(out=ef[:], in_=xf[:], func=AF.Exp, scale=-1.0),
                          reads=["xf"], writes=["ef"])
                    sc.op("act", lambda: nc.scalar.activation(out=lfn[:], in_=ef[:], func=AF.Ln, bias=1.0, scale=1.0),
                          reads=["ef"], writes=["lfn"])
                    lfn2 = lfn[:].rearrange("p i h -> p (i h)")
                    sc.op("pe", lambda: nc.tensor.matmul(bank(1)[:, 0:NF], lhsT=Uneg[:], rhs=lfn2, start=True, stop=True),
                          reads=["lfn", "Uneg"], writes=[BN[1]], inc=False)
                    sc.op("pe", lambda: nc.tensor.matmul(bank(1)[:, 256:256 + NF], lhsT=OnesNeg[:], rhs=lfn2, start=True,
                                                         stop=True),
                          reads=["lfn", "OnesNeg"], writes=[BN[1]])
                    sc.op("dve", lambda: nc.vector.memset(padA[:], 0.0), writes=["padA"])
                    sc.op("dve", lambda: nc.vector.memset(padB[:], 0.0), writes=["padB"])
                    if NT > 1:
                        sc.op("dve", lambda: nc.vector.tensor_copy(
                            out=padA[:, 9:8 + NT, :],
                            in_=bank(1)[:, 256:256 + NF].rearrange("p (i h) -> p i h", h=NH)[:, 0:NT - 1, :]),
                            reads=[BN[1], "padA"], writes=["padA"])
                    cur, oth, curn, othn = padA, padB, "padA", "padB"
                    dstep = 1
                    while dstep < NT:
                        sc.op("dve", lambda cur=cur, oth=oth, dstep=dstep: nc.vector.tensor_tensor(
                            out=oth[:, 8:8 + NT, :], in0=cur[:, 8:8 + NT, :], in1=cur[:, 8 - dstep:8 + NT - dstep, :],
                            op=ALU.add), reads=[curn], writes=[othn])
                        cur, oth, curn, othn = oth, cur, othn, curn
                        dstep *= 2
                    sc.op("dve", lambda: nc.vector.tensor_tensor(
                        out=c_tok[:], in0=bank(1)[:, 0:NF].rearrange("p (i h) -> p i h", h=NH), in1=cur[:, 8:8 + NT, :],
                        op=ALU.add), reads=[BN[1], curn], writes=["c_tok"])
                    sc.op("pe", lambda: nc.tensor.matmul(bank(0)[:, 0:NF], lhsT=Sel[:],
                                                         rhs=c_tok[:].rearrange("p i h -> p (i h)"), start=True, stop=True),
                          reads=["c_tok", "Sel"], writes=[BN[0]])
                    sc.op("act", lambda: nc.scalar.copy(out=rB[:].rearrange("p i h -> p (i h)"), in_=bank(0)[:, 0:NF]),
                          reads=[BN[0]], writes=["rB"])
                    c4 = c_tok[:].rearrange("p (a j) h -> p a j h", j=4)
                    r4 = rB[:].rearrange("p (a j) h -> p a j h", j=4)[:, :, 3:4, :].broadcast_to([128, NTT, 4, NH])
                    sc.op("dve", lambda: nc.vector.tensor_tensor(
                        out=d_tok[:].rearrange("p (a j) h -> p a j h", j=4), in0=c4, in1=r4, op=ALU.subtract),
                        reads=["c_tok", "rB"], writes=["d_tok"])
                    for T in range(NTT):
                        sc.op("dve", lambda T=T: nc.vector.tensor_tensor(
                            out=bias_all[:, :, T, :],
                            in0=rB[:, 4 * T + 3:4 * T + 4, :].rearrange("p o h -> p h o").broadcast_to([128, NH, NT]),
                            in1=c_tok[:].rearrange("p i h -> p h i"), op=ALU.subtract),
                            reads=["c_tok", "rB"], writes=["bias_all"])
                    dps = PS[1].bitcast(BF16)
                    for i in range(NT):
                        sc.op("pe", lambda i=i: nc.tensor.transpose(dps[0:NH, i * 128:(i + 1) * 128], d_tok[:, i, :],
                                                                    ident[:]),
                              reads=["d_tok", "ident"], writes=[BN[2], BN[3]], inc=(i == NT - 1))
                    sc.op("act", lambda: nc.scalar.copy(out=dT[:], in_=dps[0:NH, 0:S]),
                          reads=[BN[2], BN[3]], writes=["dT"])

                with ExitStack() as ph:
                    if STOP < 3:
                        raise _Stop()
                    sc.barrier()
                    QA = [[ph.enter_context(nc.sbuf_tensor("QA%d_%d_%d" % (b, pp, i), [128, S], BF16)) for i in range(2)] for pp in range(2)]
                    KA = [[ph.enter_context(nc.sbuf_tensor("KA%d_%d_%d" % (b, pp, i), [128, S], BF16)) for i in range(2)] for pp in range(2)]
                    Vaug = [ph.enter_context(nc.sbuf_tensor("Vaug%d_%d" % (b, pp), [128, NT, 2, 128], BF16)) for pp in range(2)]
                    gate = [ph.enter_context(nc.sbuf_tensor("gate%d_%d" % (b, pp), [128, S], BF16)) for pp in range(2)]
                    sq_sb = [ph.enter_context(nc.sbuf_tensor("sq%d_%d" % (b, i), [128, 512], BF16)) for i in range(2)]
                    msb = [ph.enter_context(nc.sbuf_tensor("msb%d_%d" % (b, i), [128, 512], F32)) for i in range(2)]
                    rstd = [ph.enter_context(nc.sbuf_tensor("rstd%d_%d" % (b, i), [128, 512], F32)) for i in range(2)]
                    th_sb = [ph.enter_context(nc.sbuf_tensor("th%d_%d" % (b, i), [128, 512], F32)) for i in range(2)]
                    pbuf = [ph.enter_context(nc.sbuf_tensor("pbuf%d_%d" % (b, i), [128, 512], BF16)) for i in range(4)]
                    rec = [ph.enter_context(nc.sbuf_tensor("rec%d_%d" % (b, i), [128, 512], F32)) for i in range(3)]
                    for pp in range(2):
                        for hh in range(2):
                            t_ = "%d%d" % (pp, hh)
                            sc.op("pool", lambda: nc.gpsimd.memset(QA[pp][hh][64:128, :], 0.0), writes=["QArow" + t_])
                            sc.op("pool", lambda: nc.gpsimd.memset(KA[pp][hh][64:128, :], 0.0), writes=["KArow" + t_])
                            sc.op("pool", lambda: nc.gpsimd.memset(KA[pp][hh][64:65, :], 1.0), reads=["KArow" + t_],
                                  writes=["KArow" + t_])
                        sc.op("dve", lambda: nc.vector.memset(Vaug[pp][:, :, 0, 64:128], 1.0), writes=["Vones%d0" % pp])
                        sc.op("dve", lambda: nc.vector.memset(Vaug[pp][:, :, 1, 0:64], 1.0), writes=["Vones%d1" % pp])

                    nq = {"n": 0}
                    wslots = {}

                    def load_pair(p):
                        wslots[p] = load_wgroup([O_Q + 128 * p, O_K + 128 * p, O_V + 128 * p, O_GF + 128 * p])

                    def proj_units(p):
                        pp = p % 2
                        units = []

                        def v_unit(i0):
                            st = {}

                            def A():
                                cur_slot = wslots[p]
                                vb = PJB[nq["n"] % 5]
                                nq["n"] += 1
                                st["vb"] = vb
                                for ii in range(4):
                                    i = i0 + ii
                                    for c in range(8):
                                        sc.op("pe", lambda: nc.tensor.matmul(
                                            bank(vb)[:, ii * 128:(ii + 1) * 128], lhsT=hT[:, c, i * 128:(i + 1) * 128],
                                            rhs=wbuf[cur_slot][:, c, 256:384], start=(c == 0), stop=(c == 7)),
                                            reads=["hT%d" % i, "wbuf%d_2" % cur_slot], writes=[BN[vb]],
                                            inc=(c == 7 and ii == 3))

                            def B():
                                vb = st["vb"]
                                bv = bank(vb).rearrange("p (i c) -> p i c", i=4)
                                sc.op("act", lambda: nc.scalar.copy(out=Vaug[pp][:, i0:i0 + 4, 0, 0:64], in_=bv[:, :, 0:64]),
                                      reads=[BN[vb], "Vones%d0" % pp], writes=["V%d0_%d" % (pp, i0 // 4)])
                                sc.op("act", lambda: nc.scalar.copy(out=Vaug[pp][:, i0:i0 + 4, 1, 64:128], in_=bv[:, :, 64:128]),
                                      reads=[BN[vb], "Vones%d1" % pp], writes=["V%d1_%d" % (pp, i0 // 4)])
                            return [A, B]

                        def qk_unit(tt, qk):
                            st = {}

                            def A():
                                cur_slot = wslots[p]
                                n = nq["n"]
                                nq["n"] += 1
                                st["u"] = u = n % 2
                                st["bk"] = bk = PJB[n % 5]
                                proj_fm(cur_slot, qk, tt, bk)

                            def A2():
                                u = st["u"]
                                bk = st["bk"]
                                sc.op("act", lambda: nc.scalar.activation(out=sq_sb[u][:], in_=bank(bk), func=AF.Square),
                                      reads=[BN[bk]], writes=["sq%d" % u])

                            def B1():
                                u = st["u"]
                                ab = 6
                                sc.op("pe", lambda: nc.tensor.matmul(bank(ab), lhsT=blockones[:], rhs=sq_sb[u][:], start=True, stop=True),
                                      reads=["sq%d" % u, "blockones"], writes=[BN[ab]])
                                sc.op("act", lambda: nc.scalar.activation(out=msb[u][:], in_=bank(ab), func=AF.Ln, bias=EPS, scale=1.0),
                                      reads=[BN[ab]], writes=["msb%d" % u])

                            def B2():
                                u = st["u"]
                                sc.op("act", lambda: nc.scalar.activation(out=rstd[u][:], in_=msb[u][:], func=AF.Exp, scale=-0.5),
                                      reads=["msb%d" % u], writes=["rstd%d" % u])

                            def C():
                                u = st["u"]
                                bk = st["bk"]
                                dst = QA[pp] if qk == 0 else KA[pp]
                                gsc = gq8 if qk == 0 else gk
                                nm = "QA" if qk == 0 else "KA"
                                for hh in range(2):
                                    r0 = hh * 64
                                    sc.op("dve", lambda: nc.vector.scalar_tensor_tensor(
                                        out=dst[hh][0:64, tt * 512:(tt + 1) * 512], in0=bank(bk)[r0:r0 + 64, :],
                                        scalar=gsc[r0:r0 + 64, p:p + 1], in1=rstd[u][r0:r0 + 64, :], op0=ALU.mult, op1=ALU.mult),
                                        reads=[BN[bk], "rstd%d" % u, "gq8", "gk"], writes=["%s%d%d_%d" % (nm, pp, hh, tt)])
                            return [A, A2, B1, B2, C]

                        def gate_unit(tt):
                            st = {}

                            def A():
                                cur_slot = wslots[p]
                                n = nq["n"]
                                nq["n"] += 1
                                st["u"] = u = n % 2
                                st["bk"] = bk = PJB[n % 5]
                                proj_fm(cur_slot, 3, tt, bk)

                            def A2():
                                u = st["u"]
                                bk = st["bk"]
                                sc.op("act", lambda: nc.scalar.activation(out=th_sb[u][:], in_=bank(bk), func=AF.Tanh, scale=0.5),
                                      reads=[BN[bk]], writes=["th%d" % u])

                            def B():
                                u = st["u"]
                                bk = st["bk"]
                                sc.op("dve", lambda: nc.vector.scalar_tensor_tensor(
                                    out=gate[pp][:, tt * 512:(tt + 1) * 512], in0=th_sb[u][:], scalar=1.0, in1=bank(bk),
                                    op0=ALU.add, op1=ALU.mult),
                                    reads=[BN[bk], "th%d" % u], writes=["gate%d_%d" % (pp, tt)])
                            return [A, A2, B]

                        def drow_unit():
                            def A():
                                for hh in range(2):
                                    h = 2 * p + hh
                                    sc.dma("sp", "drow%d%d" % (pp, hh), QA[pp][hh][64:65, :], dT[h:h + 1, :], reads=["dT"],
                                           writes=["QArow%d%d" % (pp, hh)])
                            return [A]

                        ulist = [drow_unit()]
                        for i0 in range(0, NT, 4):
                            ulist.append(v_unit(i0))
                        for tt in range(NTT):
                            for qk in range(2):
                                ulist.append(qk_unit(tt, qk))
                        for tt in range(NTT):
                            ulist.append(gate_unit(tt))
                        flat = [lambda: nq.__setitem__("n", 0)]
                        nu = len(ulist)
                        mx = max(len(u_) for u_ in ulist)
                        for slot_i in range(nu + mx - 1):
                            for k in range(mx - 1, -1, -1):
                                ui_ = slot_i - k
                                if 0 <= ui_ < nu and k < len(ulist[ui_]):
                                    flat.append(ulist[ui_][k])
                        return flat

                    def attention(p, next_units):
                        pp = p % 2
                        steps = []
                        for hh in range(2):
                            for T in range(NTT):
                                for J in range(4 * T + 4):
                                    steps.append((hh, T, J))

                        def emit_S(si):
                            hh, T, J = steps[si]
                            j = J - 4 * T
                            col0 = 128 * j if j >= 0 else 0
                            sbk = (2, 6, 3)[si % 3]
                            sc.op("pe", lambda: nc.tensor.matmul(
                                bank(sbk)[:, col0:512], lhsT=KA[pp][hh][:, J * 128:(J + 1) * 128],
                                rhs=QA[pp][hh][:, T * 512 + col0:(T + 1) * 512], start=True, stop=(j < 0)),
                                reads=["KA%d%d_%d" % (pp, hh, J // 4), "KArow%d%d" % (pp, hh), "QA%d%d_%d" % (pp, hh, T),
                                       "QArow%d%d" % (pp, hh)],
                                writes=[BN[sbk]], inc=(j < 0))
                            if j >= 0:
                                sc.op("pe", lambda: nc.tensor.matmul(
                                    bank(sbk)[:, col0:col0 + 128], lhsT=ident[:], rhs=maskT[:], start=False, stop=True),
                                    reads=["ident", "maskT"], writes=[BN[sbk]])

                        def emit_exp(si):
                            hh, T, J = steps[si]
                            h = 2 * p + hh
                            j = J - 4 * T
                            col0 = 128 * j if j >= 0 else 0
                            sbk = (2, 6, 3)[si % 3]
                            pi = si % 4
                            sc.op("act", lambda: nc.scalar.activation(
                                out=pbuf[pi][:, col0:512], in_=bank(sbk)[:, col0:512], func=AF.Exp,
                                bias=bias_all[:, h, T, J:J + 1], scale=1.0),
                                reads=[BN[sbk], "bias_all"], writes=["pbuf%d" % pi])

                        def emit_PV(si):
                            hh, T, J = steps[si]
                            j = J - 4 * T
                            col0 = 128 * j if j >= 0 else 0
                            pi = si % 4
                            acc_n = (hh * NTT + T) % 3
                            ab = (4, 5, 0)[acc_n]
                            last = (J == 4 * T + 3)
                            sc.op("pe", lambda: nc.tensor.matmul(
                                bank(ab)[:, col0:512], lhsT=Vaug[pp][:, J, hh, :], rhs=pbuf[pi][:, col0:512],
                                start=(J == 0), stop=last),
                                reads=["pbuf%d" % pi, "V%d%d_%d" % (pp, hh, J // 4), "Vones%d%d" % (pp, hh)], writes=[BN[ab]],
                                inc=last)
                            if last:
                                num0, den0 = (0, 64) if hh == 0 else (64, 0)
                                ri = acc_n
                                sc.op("dve", lambda: nc.vector.reciprocal(out=rec[ri][num0:num0 + 64, :],
                                                                          in_=bank(ab)[den0:den0 + 64, :]),
                                      reads=[BN[ab]], writes=["rec%d" % ri])
                                sc.op("pool", lambda: nc.gpsimd.tensor_tensor(
                                    out=rec[ri][num0:num0 + 64, :], in0=rec[ri][num0:num0 + 64, :],
                                    in1=gate[pp][num0:num0 + 64, T * 512:(T + 1) * 512], op=ALU.mult),
                                    reads=["rec%d" % ri, "gate%d_%d" % (pp, T)], writes=["rec%d" % ri])
                                sc.op("dve", lambda: nc.vector.scalar_tensor_tensor(
                                    out=yT[num0:num0 + 64, p, T * 512:(T + 1) * 512], in0=bank(ab)[num0:num0 + 64, :],
                                    scalar=0.5, in1=rec[ri][num0:num0 + 64, :], op0=ALU.mult, op1=ALU.mult),
                                    reads=[BN[ab], "rec%d" % ri], writes=["yT%d_%d_%d" % (p, hh, T)])

                        nst = len(steps)
                        nun = len(next_units)
                        stride = max(1, nst // (nun + 1)) if nun else nst
                        ui = 0
                        emit_S(0)
                        if nst > 1:
                            emit_S(1)
                        for si in range(nst):
                            emit_exp(si)
                            if si + 2 < nst:
                                emit_S(si + 2)
                            emit_PV(si)
                            if INTERLEAVE and ui < nun and (si + 1) % stride == 0:
                                next_units[ui]()
                                ui += 1
                        while ui < nun:
                            next_units[ui]()
                            ui += 1

                    wslots[0], wslots[1] = pre["pairs"]
                    for un in proj_units(0):
                        un()
                    for p in range(8):
                        if p + 2 < 8:
                            load_pair(p + 2)
                        if p == 7:
                            pre["glu"] = [load_wgroup([O_GLU + 128 * c_, O_GLU + 1024 + 128 * c_]) for c_ in range(2)]
                        attention(p, proj_units(p + 1) if p + 1 < 8 else [])

                ph23.close()
                with ExitStack() as ph:
                    if STOP < 4:
                        raise _Stop()
                    sc.barrier()
                    wout = ph.enter_context(nc.sbuf_tensor("wout%d" % b, [128, 16, D], BF16))
                    for half in range(2):
                        sc.dma("pool", "wo%d" % half, wout[:, half * 8:(half + 1) * 8, :], w_out_v[:, half * 8:(half + 1) * 8, :],
                               writes=["wout%d" % half])
                    with ExitStack() as ph4:
                        with ExitStack() as ph4a:
                            uT = [ph4a.enter_context(nc.sbuf_tensor("uT%d_%d" % (b, i), [128, S + 32], BF16)) for i in range(2)]
                            Wd = [ph4a.enter_context(nc.sbuf_tensor("Wd%d_%d" % (b, i), [128, KC, 128], BF16)) for i in range(2)]
                            tb = [ph4a.enter_context(nc.sbuf_tensor("tb%d_%d" % (b, i), [128, 512], F32)) for i in range(2)]
                            cacc = [ph4a.enter_context(nc.sbuf_tensor("cacc%d_%d" % (b, i), [128, 512], F32)) for i in range(2)]
                            for i in range(2):
                                sc.op("pool", lambda i=i: nc.gpsimd.memset(uT[i][:, 0:32], 0.0), writes=["uTpad%d" % i])
                            def build_wd(ch):
                                us = ch % 2
                                sc.op("pool", lambda: nc.gpsimd.tensor_tensor(
                                    out=Wd[us][:],
                                    in0=ident[:].rearrange("p (o n) -> p o n", o=1).broadcast_to([128, KC, 128]),
                                    in1=convwh[:, ch, :].rearrange("p (k o) -> p k o", o=1).broadcast_to([128, KC, 128]),
                                    op=ALU.mult), reads=["ident", "convwh"], writes=["Wd%d" % us])

                            def glu_pe(ch, tts, slot_):
                                for tt in tts:
                                    ba = tt % 2
                                    bb = 6 + (tt % 2)
                                    u = tt % 2
                                    proj_fm(slot_, 0, tt, ba)
                                    proj_fm(slot_, 1, tt, bb)
                                    sc.op("act", lambda: nc.scalar.activation(out=tb[u][:], in_=bank(bb), func=AF.Tanh, scale=0.5),
                                          reads=[BN[bb]], writes=["tb%d" % u])

                            def glu_dve(ch, tts):
                                us = ch % 2
                                for tt in tts:
                                    ba = tt % 2
                                    u = tt % 2
                                    sc.op("dve", lambda: nc.vector.scalar_tensor_tensor(
                                        out=uT[us][:, 30 + tt * 512:30 + (tt + 1) * 512], in0=tb[u][:], scalar=1.0,
                                        in1=bank(ba), op0=ALU.add, op1=ALU.mult),
                                        reads=[BN[ba], "tb%d" % u, "uTpad%d" % us], writes=["uT%d_%d" % (us, tt)])

                            def glu_proj(ch, tts, slot_):
                                for tt in tts:
                                    glu_pe(ch, [tt], slot_)
                                    glu_dve(ch, [tt])

                            def taps(ch, tts, rds, k0, k1):
                                us = ch % 2
                                for k in range(k0, k1):
                                    for tt in tts:
                                        ca = cacc[tt % 2]
                                        can = "cacc%d" % (tt % 2)
                                        if k == NPE:
                                            sc.op("dve", lambda: nc.vector.tensor_scalar(
                                                out=ca[:], in0=uT[us][:, tt * 512 + k:tt * 512 + k + 512],
                                                scalar1=convwh[:, ch, k:k + 1], scalar2=None, op0=ALU.mult),
                                                reads=rds[tt] + ["convwh"], writes=[can])
                                        else:
                                            sc.op("dve", lambda: nc.vector.scalar_tensor_tensor(
                                                out=ca[:], in0=uT[us][:, tt * 512 + k:tt * 512 + k + 512],
                                                scalar=convwh[:, ch, k:k + 1], in1=ca[:], op0=ALU.mult, op1=ALU.add),
                                                reads=rds[tt] + ["convwh", can], writes=[can])

                            gslot = {0: pre["glu"][0], 1: pre["glu"][1]}
                            build_wd(0)
                            glu_proj(0, list(range(NTT)), gslot[0])
                            KMID = (NPE + KC) // 2
                            for ch in range(8):
                                us = ch % 2
                                if ch == 7:
                                    pre["gc"] = [load_wgroup([O_GC + 512 * gi], width=512) for gi in range(2)]
                                if ch + 2 < 8:
                                    gslot[ch + 2] = load_wgroup([O_GLU + 128 * (ch + 2), O_GLU + 1024 + 128 * (ch + 2)])
                                if ch + 1 < 8:
                                    build_wd(ch + 1)
                                for tp in range(0, NTT, 2):
                                    tts = [t for t in (tp, tp + 1) if t < NTT]
                                    rds = {}
                                    if ch + 1 < 8:
                                        glu_pe(ch + 1, tts, gslot[ch + 1])
                                    for tt in tts:
                                        cb = 2 + (tt % 2)
                                        rd = ["Wd%d" % us, "uTpad%d" % us, "uT%d_%d" % (us, tt)]
                                        if tt > 0:
                                            rd.append("uT%d_%d" % (us, tt - 1))
                                        rds[tt] = rd
                                        for k in range(NPE):
                                            sc.op("pe", lambda k=k: nc.tensor.matmul(
                                                bank(cb), lhsT=Wd[us][:, k, :], rhs=uT[us][:, tt * 512 + k:tt * 512 + k + 512],
                                                start=(k == 0), stop=(k == NPE - 1)),
                                                reads=rd, writes=[BN[cb]], inc=(k == NPE - 1))
                                    taps(ch, tts, rds, NPE, KMID)
                                    if ch + 1 < 8:
                                        glu_dve(ch + 1, tts)
                                    taps(ch, tts, rds, KMID, KC)
                                    for tt in tts:
                                        cb = 2 + (tt % 2)
                                        ca = cacc[tt % 2]
                                        can = "cacc%d" % (tt % 2)
                                        sc.op("dve", lambda: nc.vector.scalar_tensor_tensor(
                                            out=yT[:, 8 + ch, tt * 512:(tt + 1) * 512], in0=bank(cb),
                                            scalar=convb[:, ch:ch + 1], in1=ca[:], op0=ALU.add, op1=ALU.add),
                                            reads=[BN[cb], "convb", can], writes=["yc%d_%d" % (ch, tt)])
                        sc.barrier()
                        sq2 = [ph4.enter_context(nc.sbuf_tensor("sqc%d_%d" % (b, i), [128, 512], BF16)) for i in range(2)]
                        mean_sb = ph4.enter_context(nc.sbuf_tensor("mean%d" % b, [128, S], F32))
                        rstd_sb = ph4.enter_context(nc.sbuf_tensor("rstdc%d" % b, [128, S], F32))
                        t1 = [ph4.enter_context(nc.sbuf_tensor("t1_%d_%d" % (b, i), [128, 512], F32)) for i in range(NSET)]
                        th2 = [ph4.enter_context(nc.sbuf_tensor("th2_%d_%d" % (b, i), [128, 512], F32)) for i in range(2)]
                        vq = [ph4.enter_context(nc.sbuf_tensor("vq_%d_%d" % (b, i), [128, 512], F32)) for i in range(NSET)]
                        thc = [ph4.enter_context(nc.sbuf_tensor("thc_%d_%d" % (b, i), [128, 512], F32)) for i in range(NSET)]
                        gcv = [ph4.enter_context(nc.sbuf_tensor("gcv_%d_%d" % (b, i), [128, 512], F32)) for i in range(NSET)]
                        m2 = t1[0]
                        vare = t1[1]
                        for tt in range(NTT):
                            for ch in range(8):
                                sc.op("pe", lambda ch=ch: nc.tensor.matmul(
                                    bank(4), lhsT=onesmean[:], rhs=yT[:, 8 + ch, tt * 512:(tt + 1) * 512],
                                    start=(ch == 0), stop=(ch == 7)),
                                    reads=["onesmean", "yc%d_%d" % (ch, tt)], writes=[BN[4]], inc=(ch == 7))
                            for ch in range(8):
                                qi = ch % 2
                                sc.op("act", lambda ch=ch, qi=qi: nc.scalar.activation(
                                    out=sq2[qi][:], in_=yT[:, 8 + ch, tt * 512:(tt + 1) * 512], func=AF.Square),
                                    reads=["yc%d_%d" % (ch, tt)], writes=["sqc%d" % qi])
                                sc.op("pe", lambda ch=ch, qi=qi: nc.tensor.matmul(
                                    bank(5), lhsT=onesmean[:], rhs=sq2[qi][:], start=(ch == 0), stop=(ch == 7)),
                                    reads=["onesmean", "sqc%d" % qi], writes=[BN[5]], inc=True)
                            sc.op("act", lambda: nc.scalar.copy(out=mean_sb[:, tt * 512:(tt + 1) * 512], in_=bank(4)),
                                  reads=[BN[4]], writes=["mean%d" % tt])
                            sc.op("dve", lambda: nc.vector.tensor_tensor(
                                out=m2[:], in0=mean_sb[:, tt * 512:(tt + 1) * 512],
                                in1=mean_sb[:, tt * 512:(tt + 1) * 512], op=ALU.mult),
                                reads=["mean%d" % tt], writes=["t1_0"])
                            sc.op("dve", lambda: nc.vector.scalar_tensor_tensor(
                                out=vare[:], in0=bank(5), scalar=EPS, in1=m2[:], op0=ALU.add, op1=ALU.subtract),
                                reads=[BN[5], "t1_0"], writes=["t1_1"])
                            sc.op("act", lambda: nc.scalar.activation(out=vare[:], in_=vare[:], func=AF.Ln),
                                  reads=["t1_1"], writes=["t1_1"])
                            sc.op("act", lambda: nc.scalar.activation(
                                out=rstd_sb[:, tt * 512:(tt + 1) * 512], in_=vare[:], func=AF.Exp, scale=-0.5),
                                reads=["t1_1"], writes=["rstdc%d" % tt])
                        gslots = pre["gc"]
                        blocks = [(gi, c4, tt) for tt in range(NTT) for gi in range(2) for c4 in range(4)]
                        xo = [ph4.enter_context(nc.sbuf_tensor("xo%d_%d" % (b, i), [128, D], F32)) for i in range(2)]
                        ot = [ph4.enter_context(nc.sbuf_tensor("ot%d_%d" % (b, i), [128, D], F32)) for i in range(1)]

                        def og(i, half):
                            s_ = i % 2
                            T = i // 4
                            ob = 4 + half
                            if half == 0:
                                sc.dma("sp", "xm%d" % s_, xo[s_][:], x[b, i * 128:(i + 1) * 128, :], writes=["xo%d" % s_])
                            for e in range(16):
                                if e < 8:
                                    rd = ["yT%d_0_%d" % (e, T), "yT%d_1_%d" % (e, T)]
                                else:
                                    rd = ["yc%d_%d" % (e - 8, T)]
                                sc.op("pe", lambda: nc.tensor.matmul(
                                    bank(ob), lhsT=yT[:, e, i * 128:(i + 1) * 128],
                                    rhs=wout[:, e, half * 512:(half + 1) * 512], start=(e == 0), stop=(e == 15)),
                                    reads=rd + ["wout%d" % (e // 8)], writes=[BN[ob]], inc=(e == 15))
                            sc.op("dve", lambda: nc.vector.tensor_tensor(
                                out=ot[0][:, half * 512:(half + 1) * 512], in0=bank(ob),
                                in1=xo[s_][:, half * 512:(half + 1) * 512], op=ALU.add),
                                reads=[BN[ob], "xo%d" % s_], writes=["ot_%d" % half])
                            if half == 1:
                                sc.dma("sp", "ost", out[b, i * 128:(i + 1) * 128, :], ot[0][:], reads=["ot_0", "ot_1"])

                        ogs = [(i, half) for i in range(NT) for half in range(2)]

                        LB = (0, 1, 2, 3, 6, 7)

                        def ln_stages(n):
                            gi, c4, tt = blocks[n]
                            ch = gi * 4 + c4
                            u = n % NSET
                            pbk = LB[n % 6]
                            ysl = yT[:, 8 + ch, tt * 512:(tt + 1) * 512]

                            def s0():
                                proj_fm(gslots[gi], c4, tt, pbk)

                            def s1():
                                sc.op("act", lambda: nc.scalar.activation(out=thc[u][:], in_=bank(pbk), func=AF.Tanh, scale=0.5),
                                      reads=[BN[pbk]], writes=["thc%d" % u])
                                sc.op("dve", lambda: nc.vector.tensor_tensor(
                                    out=t1[u][:], in0=ysl, in1=mean_sb[:, tt * 512:(tt + 1) * 512], op=ALU.subtract),
                                    reads=["yc%d_%d" % (ch, tt), "mean%d" % tt], writes=["t1_%d" % u])

                            def s2():
                                sc.op("pool", lambda: nc.gpsimd.tensor_tensor(
                                    out=t1[u][:], in0=t1[u][:], in1=rstd_sb[:, tt * 512:(tt + 1) * 512], op=ALU.mult),
                                    reads=["t1_%d" % u, "rstdc%d" % tt], writes=["t1_%d" % u])

                            def s3():
                                sc.op("act", lambda: nc.scalar.activation(
                                    out=th2[n % 2][:], in_=t1[u][:], func=AF.Tanh, bias=lnb_h[:, ch:ch + 1],
                                    scale=lng_h[:, ch:ch + 1]),
                                    reads=["t1_%d" % u, "lng_h", "lnb_h"], writes=["th2_%d" % (n % 2)])
                                sc.op("dve", lambda: nc.vector.scalar_tensor_tensor(
                                    out=gcv[u][:], in0=thc[u][:], scalar=1.0, in1=bank(pbk), op0=ALU.add,
                                    op1=ALU.mult), reads=[BN[pbk], "thc%d" % u], writes=["gcv%d" % u])
                                sc.op("act", lambda: nc.scalar.activation(
                                    out=vq[u][:], in_=t1[u][:], func=AF.Identity, bias=lnb_q[:, ch:ch + 1],
                                    scale=lng_q[:, ch:ch + 1]),
                                    reads=["t1_%d" % u, "lng_q", "lnb_q"], writes=["vq%d" % u])

                            def s4():
                                sc.op("dve", lambda: nc.vector.scalar_tensor_tensor(
                                    out=vq[u][:], in0=th2[n % 2][:], scalar=1.0, in1=vq[u][:], op0=ALU.add, op1=ALU.mult),
                                    reads=["th2_%d" % (n % 2), "vq%d" % u], writes=["vq%d" % u])

                            def s5():
                                if n % 2 == 0:
                                    sc.op("pool", lambda: nc.gpsimd.tensor_tensor(
                                        out=ysl, in0=vq[u][:], in1=gcv[u][:], op=ALU.mult),
                                        reads=["vq%d" % u, "gcv%d" % u], writes=["yc%d_%d" % (ch, tt)])
                                else:
                                    sc.op("dve", lambda: nc.vector.tensor_tensor(
                                        out=ysl, in0=vq[u][:], in1=gcv[u][:], op=ALU.mult),
                                        reads=["vq%d" % u, "gcv%d" % u], writes=["yc%d_%d" % (ch, tt)])
                            return [s0, s1, s2, s3, s4, s5]

                        lnu = [ln_stages(n) for n in range(len(blocks))]
                        oq = 0
                        for slot_i in range(len(lnu) + 5):
                            for k in range(5, -1, -1):
                                ui_ = slot_i - k
                                if 0 <= ui_ < len(lnu):
                                    lnu[ui_][k]()
                            if slot_i >= 12 and oq < len(ogs) and oq <= slot_i - 12:
                                og(*ogs[oq])
                                oq += 1
                        while oq < len(ogs):
                            og(*ogs[oq])
                            oq += 1
          except _Stop:
            stopped = True
            break
        sc.wait_all("sp")
        if stopped:
            es.pop_all()
    return nc


def _prep_shared(inp):
    f = np.float32
    qg = np.asarray(inp["q_norm_g"], f)[0]
    kg = np.asarray(inp["k_norm_g"], f)[0]
    gq = np.ascontiguousarray(qg.reshape(8, 128).T)
    gk = np.ascontiguousarray(kg.reshape(8, 128).T)
    cw = np.asarray(inp["conv_w"], f)[0]
    convw = np.ascontiguousarray(cw.T.reshape(8, 128, KC).transpose(1, 0, 2))

    def pc(a):
        return np.ascontiguousarray(np.asarray(a, f)[0].reshape(8, 128).T)

    return {
        "w_in": np.ascontiguousarray(np.asarray(inp["w_in"], f)[0]),
        "w_out": np.ascontiguousarray(np.asarray(inp["w_out"], f)[0]),
        "norm_g": np.ascontiguousarray(np.asarray(inp["norm_g"], f)),
        "b_forget": np.ascontiguousarray(np.asarray(inp["b_forget"], f)),
        "gq": gq, "gk": gk, "convw": convw,
        "convb": pc(inp["conv_b"]), "lng": pc(inp["conv_ln_g"]), "lnb": pc(inp["conv_ln_b"]),
    }


def kernel(**inputs):
    x = np.asarray(inputs["x"], np.float32)
    B, S, _ = x.shape
    n = N_CORES
    nb = B // n
    shared = _prep_shared(inputs)
    nc = build_nc(NB=nb, S=S)
    in_maps = []
    for c in range(n):
        m = dict(shared)
        m["x"] = np.ascontiguousarray(x[c * nb:(c + 1) * nb])
        in_maps.append(m)
    res = run_bass_kernel_spmd(nc, in_maps, core_ids=list(range(n)))
    return np.concatenate([np.asarray(r["out"]) for r in res.results], axis=0).astype(np.float32)
```
